# Optimizing a Trainium2 kernel written in Bass

```python
import jax, jax.numpy as jnp
from jax import lax
import numpy as np

D_MODEL = 2048
BATCH = 4
SEQ = 2048
DEPTH = 4

HEAD_DIM = 128
N_HEADS_A = (D_MODEL // 2) // HEAD_DIM
C_B = D_MODEL // 2
N_HEADS_C = D_MODEL // HEAD_DIM
ROT_DIM = HEAD_DIM // 4
ROPE_THETA = 500000.0
DILATED_BRANCHES = ((128, 1), (512, 4), (2048, 16))
CONV_B_WIDTH = 31
FFN_CONV_WIDTH = 3
D_FF = ((8 * D_MODEL // 3 + 255) // 256) * 256
Q_BLOCK = 128
N_EVEN = (DEPTH + 1) // 2
N_ODD = DEPTH // 2
EPS = 1e-6
EVEN_IN = 3 * N_HEADS_A * HEAD_DIM + 2 * C_B
ODD_IN = 3 * N_HEADS_C * HEAD_DIM + N_HEADS_C

kernel_name = "hybrid_dilated_conformer_fox_convffn"

F32 = jnp.float32


def rmsnorm(x, g):
    xf = x.astype(F32)
    y = xf * lax.rsqrt(jnp.mean(xf * xf, axis=-1, keepdims=True) + EPS)
    return (y * g.astype(F32)).astype(x.dtype)


def layernorm(x, g, b):
    xf = x.astype(F32)
    mu = jnp.mean(xf, axis=-1, keepdims=True)
    var = jnp.mean(jnp.square(xf - mu), axis=-1, keepdims=True)
    y = (xf - mu) * lax.rsqrt(var + EPS)
    return (y * g.astype(F32) + b.astype(F32)).astype(x.dtype)


def causal_dwconv(x, w):
    K = w.shape[0]
    return lax.conv_general_dilated(
        x, w[:, None, :].astype(x.dtype), window_strides=(1,),
        padding=[(K - 1, 0)], dimension_numbers=("NWC", "WIO", "NWC"),
        feature_group_count=x.shape[-1])


def partial_rope(x, positions):
    half = ROT_DIM // 2
    inv = jnp.power(jnp.float32(ROPE_THETA), -jnp.arange(half, dtype=F32) * (2.0 / ROT_DIM))
    ang = positions.astype(F32)[..., None] * inv
    cos = jnp.cos(ang)[:, :, None, :]
    sin = jnp.sin(ang)[:, :, None, :]
    x1 = x[..., :half].astype(F32)
    x2 = x[..., half:ROT_DIM].astype(F32)
    r1 = (x1 * cos - x2 * sin).astype(x.dtype)
    r2 = (x2 * cos + x1 * sin).astype(x.dtype)
    return jnp.concatenate([r1, r2, x[..., ROT_DIM:]], axis=-1)


def dilated_branch(q, k, v, window, dilation):
    B, S, H, E = q.shape
    W = window // dilation
    L = S // dilation
    nb = -(-L // W)
    Lp = nb * W

    def to_sub(t):
        t = t.reshape(B, L, dilation, H, E).transpose(0, 2, 3, 1, 4)
        t = jnp.pad(t, ((0, 0), (0, 0), (0, 0), (0, Lp - L), (0, 0)))
        return t.reshape(B, dilation, H, nb, W, E)

    def with_prev(t):
        prev = jnp.pad(t, ((0, 0), (0, 0), (0, 0), (1, 0), (0, 0), (0, 0)))[:, :, :, :-1]
        return jnp.concatenate([prev, t], axis=4)

    qs = to_sub(q)
    kk = with_prev(to_sub(k))
    vv = with_prev(to_sub(v))
    s = jnp.einsum("brhnqe,brhnke->brhnqk", qs, kk, preferred_element_type=F32)
    i = jnp.arange(W)[:, None]
    j = jnp.arange(2 * W)[None, :]
    dist = W + i - j
    band = (dist >= 0) & (dist <= W)
    not_pad = (jnp.arange(nb)[:, None, None] > 0) | (j >= W)[None]
    mask = band[None] & not_pad
    s = jnp.where(mask, s, -jnp.inf)
    m = jnp.max(s, axis=-1, keepdims=True)
    p = jnp.exp(s - m)
    den = jnp.sum(p, axis=-1, keepdims=True)
    o = jnp.einsum("brhnqk,brhnke->brhnqe", (p / den).astype(v.dtype), vv,
                   preferred_element_type=F32)
    lse = (m + jnp.log(den))[..., 0]

    def from_sub(t):
        tail = t.shape[5:]
        t = t.reshape((B, dilation, H, Lp) + tail)[:, :, :, :L]
        perm = (0, 3, 1, 2) + tuple(range(4, t.ndim))
        return t.transpose(perm).reshape((B, S, H) + tail)

    return from_sub(o), from_sub(lse)


def dilated_attention(q, k, v):
    outs, lses = [], []
    for window, dilation in DILATED_BRANCHES:
        o, l = dilated_branch(q, k, v, window, dilation)
        outs.append(o)
        lses.append(l)
    w = jax.nn.softmax(jnp.stack(lses, axis=0), axis=0)
    return jnp.sum(w[..., None] * jnp.stack(outs, axis=0), axis=0)


def forgetting_attention(q, k, v, logf):
    B, S, H, E = q.shape
    nb = S // Q_BLOCK
    Fc = jnp.cumsum(logf, axis=1)
    Ft = Fc.transpose(0, 2, 1)
    qb = q.reshape(B, nb, Q_BLOCK, H, E).transpose(1, 0, 2, 3, 4)
    Fb = Ft.reshape(B, H, nb, Q_BLOCK).transpose(2, 0, 1, 3)
    pos = jnp.arange(S)
    idxb = pos.reshape(nb, Q_BLOCK)

    def block(args):
        qi, Fi, ti = args
        s = jnp.einsum("bqhe,bkhe->bhqk", qi, k, preferred_element_type=F32)
        s = s + (Fi[..., :, None] - Ft[..., None, :])
        s = jnp.where(ti[:, None] >= pos[None, :], s, -jnp.inf)
        p = jax.nn.softmax(s, axis=-1)
        return jnp.einsum("bhqk,bkhe->bqhe", p.astype(v.dtype), v, preferred_element_type=F32)

    o = lax.map(block, (qb, Fb, idxb))
    return o.transpose(1, 0, 2, 3, 4).reshape(B, S, H, E)


def even_mixer(h, positions, w_in, conv_w, conv_b, ln_g, ln_b, w_out):
    B, S, _ = h.shape
    dA = N_HEADS_A * HEAD_DIM
    u = h @ w_in
    q, k, v, a, gate = jnp.split(u, [dA, 2 * dA, 3 * dA, 3 * dA + C_B], axis=-1)
    q = partial_rope(q.reshape(B, S, N_HEADS_A, HEAD_DIM), positions) * (HEAD_DIM ** -0.5)
    k = partial_rope(k.reshape(B, S, N_HEADS_A, HEAD_DIM), positions)
    v = v.reshape(B, S, N_HEADS_A, HEAD_DIM)
    y_a = dilated_attention(q, k, v).reshape(B, S, dA).astype(h.dtype)
    g = a * jax.nn.sigmoid(gate)
    g = causal_dwconv(g, conv_w) + conv_b
    y_b = jax.nn.silu(layernorm(g, ln_g, ln_b))
    return jnp.concatenate([y_a, y_b], axis=-1) @ w_out


def odd_mixer(h, w_in, b_f, w_out):
    B, S, _ = h.shape
    dC = N_HEADS_C * HEAD_DIM
    u = h @ w_in
    q, k, v, fl = jnp.split(u, [dC, 2 * dC, 3 * dC], axis=-1)
    q = q.reshape(B, S, N_HEADS_C, HEAD_DIM) * (HEAD_DIM ** -0.5)
    k = k.reshape(B, S, N_HEADS_C, HEAD_DIM)
    v = v.reshape(B, S, N_HEADS_C, HEAD_DIM)
    logf = jax.nn.log_sigmoid((fl + b_f).astype(F32))
    y = forgetting_attention(q, k, v, logf).reshape(B, S, dC).astype(h.dtype)
    return y @ w_out


def conv_ffn(h, w_up, conv_w, w_down):
    up = h @ w_up
    g, u = jnp.split(up, [D_FF], axis=-1)
    g = causal_dwconv(g, conv_w)
    return (jax.nn.silu(g) * u) @ w_down


def setup_inputs(seed: int = 0) -> dict:
    key = jax.random.key(seed)
    ks = jax.random.split(key, 20)
    nrm = lambda k, shape, scale: jax.random.normal(k, shape, F32) * scale
    x = jax.random.normal(ks[0], (BATCH, SEQ, D_MODEL), F32)
    offs = jax.random.randint(ks[1], (BATCH, 1), 0, 4096, dtype=jnp.int32)
    positions = jnp.arange(SEQ, dtype=jnp.int32)[None, :] + offs
    dA = N_HEADS_A * HEAD_DIM
    dC = N_HEADS_C * HEAD_DIM
    return {
        "x": x,
        "positions": positions,
        "norm_mix": 1.0 + nrm(ks[2], (DEPTH, D_MODEL), 0.02),
        "norm_ffn": 1.0 + nrm(ks[3], (DEPTH, D_MODEL), 0.02),
        "norm_final": 1.0 + nrm(ks[4], (D_MODEL,), 0.02),
        "ev_w_in": nrm(ks[5], (N_EVEN, D_MODEL, EVEN_IN), D_MODEL ** -0.5),
        "ev_conv_w": nrm(ks[6], (N_EVEN, CONV_B_WIDTH, C_B), CONV_B_WIDTH ** -0.5),
        "ev_conv_b": nrm(ks[7], (N_EVEN, C_B), 0.02),
        "ev_ln_g": 1.0 + nrm(ks[8], (N_EVEN, C_B), 0.02),
        "ev_ln_b": nrm(ks[9], (N_EVEN, C_B), 0.02),
        "ev_w_out": nrm(ks[10], (N_EVEN, dA + C_B, D_MODEL), (dA + C_B) ** -0.5),
        "od_w_in": nrm(ks[11], (N_ODD, D_MODEL, ODD_IN), D_MODEL ** -0.5),
        "od_b_f": nrm(ks[12], (N_ODD, N_HEADS_C), 0.1),
        "od_w_out": nrm(ks[13], (N_ODD, dC, D_MODEL), dC ** -0.5),
        "ffn_w_up": nrm(ks[14], (DEPTH, D_MODEL, 2 * D_FF), D_MODEL ** -0.5),
        "ffn_conv_w": nrm(ks[15], (DEPTH, FFN_CONV_WIDTH, D_FF), FFN_CONV_WIDTH ** -0.5),
        "ffn_w_down": nrm(ks[16], (DEPTH, D_FF, D_MODEL), D_FF ** -0.5),
    }


def reference(x, positions, norm_mix, norm_ffn, norm_final, ev_w_in, ev_conv_w, ev_conv_b,
              ev_ln_g, ev_ln_b, ev_w_out, od_w_in, od_b_f, od_w_out, ffn_w_up, ffn_conv_w,
              ffn_w_down):
    for l in range(DEPTH):
        h = rmsnorm(x, norm_mix[l])
        if l % 2 == 0:
            e = l // 2
            x = x + even_mixer(h, positions, ev_w_in[e], ev_conv_w[e], ev_conv_b[e],
                               ev_ln_g[e], ev_ln_b[e], ev_w_out[e])
        else:
            o = l // 2
            x = x + odd_mixer(h, od_w_in[o], od_b_f[o], od_w_out[o])
        h = rmsnorm(x, norm_ffn[l])
        x = x + conv_ffn(h, ffn_w_up[l], ffn_conv_w[l], ffn_w_down[l])
    return rmsnorm(x, norm_final)
```

```python
import numpy as np
from contextlib import ExitStack
import concourse.bass as bass
import concourse.mybir as mybir
from concourse.bass_utils import run_bass_kernel_spmd

F32 = mybir.dt.float32
BF16 = mybir.dt.bfloat16
I32 = mybir.dt.int32
AF = mybir.ActivationFunctionType
ALU = mybir.AluOpType

D = 2048
T = 1024
KT = D // 128
DFF = 5632
NFT = DFF // 128
EPS = 1e-6
NCORES = 8


class Prog:
    ENGS = ("pe", "act", "dve", "pool", "sp")
    UID = [0]
    POOL = {}

    def __init__(self, nc):
        self.nc = nc
        Prog.UID[0] += 1
        self.uid = "p%d_" % Prog.UID[0]
        self.coll_cnt = {}
        self.ops = {e: [] for e in self.ENGS}
        self.nsig = {e: 0 for e in self.ENGS}
        self.dma_cnt = {}
        self.writer = {}
        self.readers = {}
        self.stack = ExitStack()

    def sb(self, name, shape, dt):
        return self.stack.enter_context(self.nc.sbuf_tensor(self.uid + "sb_" + name, list(shape), dt))

    def ps(self, name, shape, dt=F32):
        return self.stack.enter_context(self.nc.psum_tensor(self.uid + "pp_" + name, list(shape), dt))

    @staticmethod
    def _is_psum(b):
        return ("ps" in b) or b.startswith("ss_")

    def _deps(self, r, w):
        toks = []
        for b in r:
            if b in self.writer:
                toks.append(self.writer[b])
            if self._is_psum(b):
                toks.extend(self.readers.get(b, []))
        for b in w:
            if b in self.writer:
                toks.append(self.writer[b])
            toks.extend(self.readers.get(b, []))
        return toks

    def _commit(self, tok, r, w):
        for b in r:
            self.readers.setdefault(b, []).append(tok)
        for b in w:
            self.writer[b] = tok
            self.readers[b] = []

    def op(self, eng, fn, r=(), w=(), extra=()):
        waits = self._deps(r, w) + list(extra)
        self.nsig[eng] += 1
        tok = ("E:" + eng, self.nsig[eng])
        self.ops[eng].append((fn, waits, tok))
        self._commit(tok, r, w)
        return tok

    def mm_group(self, mms, r=(), w=()):
        waits = self._deps(r, w)
        n = len(mms)
        self.nsig["pe"] += 1
        tok = ("E:pe", self.nsig["pe"])
        for i, fn in enumerate(mms):
            f = (lambda pe, fn=fn, i=i: fn(pe, i == 0, i == n - 1))
            self.ops["pe"].append((f, waits if i == 0 else [], tok if i == n - 1 else None))
        self._commit(tok, r, w)
        return tok

    def dma(self, queue, out, in_, r=(), w=(), key=None, **kw):
        waits = self._deps(r, w)
        if key is None:
            key = w[0] if w else r[0]
        self.dma_cnt[key] = self.dma_cnt.get(key, 0) + 1
        tok = ("D:" + str(key), 16 * self.dma_cnt[key])
        fn = (lambda e: e.dma_start(out=out, in_=in_, **kw))
        self.ops[queue].append((fn, waits, tok))
        self._commit(tok, r, w)
        return tok

    def coll(self, in_ap, out_ap, r=(), w=(), key=None):
        waits = self._deps(r, w)
        if key is None:
            key = w[0]
        self.coll_cnt[key] = self.coll_cnt.get(key, 0) + 1
        tok = ("C:" + str(key), self.coll_cnt[key])
        fn = (lambda e: e.collective_compute("AllGather", ALU.bypass, replica_groups=PAIRS,
                                             ins=[in_ap], outs=[out_ap]))
        self.ops["pool"].append((fn, waits, tok))
        self._commit(tok, r, w)
        return tok

    def check_deadlock(self):
        pos = {e: 0 for e in self.ENGS}
        val = {}
        progress = True
        while progress:
            progress = False
            for e in self.ENGS:
                ops = self.ops[e]
                while pos[e] < len(ops):
                    fn, waits, tok = ops[pos[e]]
                    ok = all((k == "E:pe" and e == "pe") or val.get(k, 0) >= v for (k, v) in waits)
                    if not ok:
                        break
                    if tok is not None:
                        k, v = tok
                        val[k] = val.get(k, 0) + (16 if k.startswith("D:") else 1)
                    pos[e] += 1
                    progress = True
        stuck = {e: (pos[e], len(self.ops[e])) for e in self.ENGS if pos[e] < len(self.ops[e])}
        if stuck:
            msg = []
            for e, (p, n) in stuck.items():
                fn, waits, tok = self.ops[e][p]
                bad = [(k, v, val.get(k, 0)) for (k, v) in waits if val.get(k, 0) < v]
                msg.append("%s stuck at %d/%d waiting %s (tok %s)" % (e, p, n, bad, tok))
            raise RuntimeError("DEADLOCK: " + "; ".join(msg))

    def finish(self, final_tokens=None):
        self.check_deadlock()
        nc = self.nc
        semkeys = (["E:" + e for e in self.ENGS] + ["D:" + str(k) for k in self.dma_cnt]
                   + ["C:" + str(k) for k in self.coll_cnt])
        pool = Prog.POOL
        if pool.get("nc") is not nc:
            pool.clear()
            pool.update(nc=nc, handles=[], base=[], stack=ExitStack())
        while len(pool["handles"]) < len(semkeys):
            pool["handles"].append(pool["stack"].enter_context(nc.semaphore("gs%d" % len(pool["handles"]))))
            pool["base"].append(0)
        slot = {k: i for i, k in enumerate(semkeys)}
        base = {k: pool["base"][slot[k]] for k in semkeys}

        class _Sems(dict):
            pass
        sems = {k: pool["handles"][slot[k]] for k in semkeys}
        totals = {}
        for e_ in self.ENGS:
            totals["E:" + e_] = self.nsig[e_]
        for k, c in self.dma_cnt.items():
            totals["D:" + str(k)] = 16 * c
        for k, c in self.coll_cnt.items():
            totals["C:" + str(k)] = c
        fin = [("D:" + str(k), 16 * c) for k, c in self.dma_cnt.items()]
        fin += [("C:" + str(k), c) for k, c in self.coll_cnt.items()]
        eng_of = {"pe": "tensor", "act": "scalar", "dve": "vector", "pool": "gpsimd", "sp": "sync"}
        with nc.Block() as block:
            for ename in self.ENGS:
                ops = self.ops[ename]
                extra_fin = fin if ename == "sp" else []

                def body(e, ops=ops, ename=ename, extra_fin=extra_fin):
                    seen = {}
                    for fn, waits, tok in ops:
                        for (k, v) in waits:
                            if k == "E:pe" and ename == "pe":
                                continue
                            if seen.get(k, 0) >= v:
                                continue
                            e.wait_ge(sems[k], base[k] + v)
                            seen[k] = v
                        ins = fn(e)
                        if tok is not None:
                            k, v = tok
                            ins.then_inc(sems[k], 16 if k.startswith("D:") else 1)
                    for (k, v) in extra_fin:
                        if seen.get(k, 0) < v:
                            e.wait_ge(sems[k], base[k] + v)

                getattr(block, eng_of[ename])(body)
        for k in semkeys:
            pool["base"][slot[k]] += totals[k]
        self.stack.close()


_FUSED = {"nc": None, "io": {}}


def _new_nc():
    if _FUSED["nc"] is not None:
        return _FUSED["nc"]
    return bass.Bass("TRN2", target_bir_lowering=False)


def dram_in(nc, name, shape, dt):
    if _FUSED["nc"] is not None:
        ap = _FUSED["io"][name]
        assert list(ap.shape) == list(shape), (name, ap.shape, shape)
        return ap
    return nc.dram_tensor(name, list(shape), dt, kind="ExternalInput").ap()


def dram_out(nc, name, shape, dt):
    if _FUSED["nc"] is not None:
        ap = _FUSED["io"][name]
        assert list(ap.shape) == list(shape), (name, ap.shape, shape)
        return ap
    return nc.dram_tensor(name, list(shape), dt, kind="ExternalOutput").ap()


PAIRS = [[0, 1], [2, 3], [4, 5], [6, 7]]


def emit_consts(P):
    ones_f = P.sb("ones_f", [128, 128], F32)
    ones_b = P.sb("ones_b", [128, 128], BF16)
    P.op("dve", lambda e: e.memset(ones_f[:], 1.0), w=["ones_f"])
    P.op("dve", lambda e: e.memset(ones_b[:], 1.0), w=["ones_b"])
    return ones_f, ones_b


def emit_rmsnorm(P, X, xbuf, gain_d, H, hbuf, ones_f, tag, ncols=T, ss_ps=None, ssbuf=None):
    nc = P.nc
    G = P.sb("G_" + tag, [128, KT], F32)
    P.dma("sp", G[:], gain_d.rearrange("(kt p) -> p kt", p=128), w=["G_" + tag],
          allow_slow_non_contiguous=True)
    sq = [P.sb("sq%d_%s" % (i, tag), [128, 512], BF16) for i in range(2)]
    ones_sq = P.sb("ones_sq_" + tag, [128, 128], BF16)
    P.op("dve", lambda e: e.memset(ones_sq[:], 1.0), w=["ones_sq_" + tag])
    if ss_ps is None:
        ss_ps = P.ps("ss_" + tag, [128, ncols])
    if ssbuf is None:
        ssbuf = lambda c: "ss_%s_%d" % (tag, c)
    rstd = P.sb("rstd_" + tag, [128, ncols], F32)
    nch = ncols // 512
    i = 0
    for c in range(nch):
        cs = slice(c * 512, (c + 1) * 512)
        mms = []
        for kt in range(KT):
            s = i % 2
            i += 1
            P.op("act", lambda e, kt=kt, s=s, cs=cs: e.activation(out=sq[s][:], in_=X[:, kt, cs], func=AF.Square),
                 r=[xbuf(kt)], w=["sq%d_%s" % (s, tag)])
            P.mm_group([lambda pe, st, sp_, s=s, cs=cs, kt=kt: pe.matmul(
                ss_ps[:, cs], lhsT=ones_sq[:], rhs=sq[s][:], start=(kt == 0), stop=(kt == KT - 1),
                skip_group_check=True)],
                r=["sq%d_%s" % (s, tag), "ones_sq_" + tag], w=[ssbuf(c)])
        P.op("act", lambda e, cs=cs: e.activation(out=rstd[:, cs], in_=ss_ps[:, cs], func=AF.Sqrt,
                                                  scale=1.0 / D, bias=EPS),
             r=[ssbuf(c)], w=["rstd_%s_%d" % (tag, c)])
        P.op("dve", lambda e, cs=cs: e.reciprocal(out=rstd[:, cs], in_=rstd[:, cs]),
             r=["rstd_%s_%d" % (tag, c)], w=["rstd_%s_%d" % (tag, c)])
    for kt in range(KT):
        P.op("dve", lambda e, kt=kt: e.scalar_tensor_tensor(
            out=H[:, kt, :], in0=X[:, kt, :], scalar=G[:, kt:kt + 1], in1=rstd[:],
            op0=ALU.mult, op1=ALU.mult),
            r=[xbuf(kt), "G_" + tag] + ["rstd_%s_%d" % (tag, c) for c in range(nch)], w=[hbuf(kt)])


def build_p0():
    nc = _new_nc()
    xT_d = dram_in(nc, "xT", [D, T], F32)
    g_d = dram_in(nc, "gain", [D], F32)
    hT_d = dram_out(nc, "hT", [D, T], BF16)
    P = Prog(nc)
    ones_f, ones_b = emit_consts(P)
    X = P.sb("X", [128, KT, T], F32)
    H = P.sb("H", [128, KT, T], BF16)
    xv = xT_d.rearrange("(kt p) t -> p kt t", p=128)
    for kt in range(KT):
        P.dma("sp", X[:, kt, :], xv[:, kt, :], w=["X%d" % kt])
    emit_rmsnorm(P, X, lambda kt: "X%d" % kt, g_d, H, lambda kt: "H%d" % kt, ones_f, "n0")
    hv = hT_d.rearrange("(kt p) t -> p kt t", p=128)
    for kt in range(KT):
        P.dma("sp", hv[:, kt, :], H[:, kt, :], r=["H%d" % kt], w=["hT_out"])
    P.finish()
    return nc


def f_p3(nc, io, final):
    xT_d, h2_d, halo_d = io["xT"], io["h2T"], io["halo"]
    wup_d, cw_d, wdn_d, gn_d = io["w_up"], io["conv_w"], io["w_down"], io["gain_next"]
    xo_d = None if final else io["xT_out"]
    ho_d = io["hT_out"]

    P = Prog(nc)
    ones_f, ones_b = emit_consts(P)
    X = P.sb("X", [128, KT, T], F32)
    H = P.sb("H", [128, KT, T], BF16)
    HH = P.sb("HH", [128, KT, 2], BF16)
    xv = xT_d.rearrange("(kt p) t -> p kt t", p=128)
    hv = h2_d.rearrange("(kt p) t -> p kt t", p=128)
    for kt in range(KT):
        P.dma("sp", H[:, kt, :], hv[:, kt, :], w=["H%d" % kt])
    P.dma("sp", HH[:], halo_d.rearrange("(kt p) t -> p kt t", p=128), w=["HH"])
    HM = P.sb("HM", [128, 1], F32)
    P.dma("sp", HM[:], io["hmul"], w=["HM"])
    P.op("dve", lambda e: e.tensor_scalar(out=HH[:], in0=HH[:], scalar1=HM[:, 0:1], scalar2=None, op0=ALU.mult),
         r=["HH", "HM"], w=["HH"])
    for kt in range(KT):
        P.dma("sp", X[:, kt, :], xv[:, kt, :], w=["X%d" % kt])
    CW = P.sb("CW", [128, 3, NFT], F32)
    for j in range(3):
        P.dma("sp", CW[:, j, :], cw_d[j, :].rearrange("(ft p) -> p ft", p=128), w=["CW"],
              allow_slow_non_contiguous=True)

    GW = 2
    NG = NFT // GW
    NBUF = 2
    Wg = [P.sb("Wg%d" % i, [128, KT, 128 * GW], BF16) for i in range(NBUF)]
    Wu = [P.sb("Wu%d" % i, [128, KT, 128 * GW], BF16) for i in range(NBUF)]
    Wd = [P.sb("Wd%d" % i, [128, GW, D], BF16) for i in range(NBUF)]
    M = [P.sb("M%d" % i, [128, GW, T], BF16) for i in range(2)]
    Gs = [P.sb("Gs%d" % i, [128, T + 2], F32) for i in range(2)]
    Tm = [P.sb("Tm%d" % i, [128, T], F32) for i in range(2)]
    Sl = [P.sb("Sl%d" % i, [128, T], F32) for i in range(2)]
    g_ps = P.ps("g_ps", [128, T])
    u_ps = P.ps("u_ps", [128, T])
    h_ps = P.ps("h_ps", [128, 512])
    y_ps = [P.ps("y_ps%d" % i, [128, 512]) for i in range(3)]

    wupv = wup_d.rearrange("(kt p) n -> p kt n", p=128)
    wdnv = wdn_d.rearrange("(ft p) n -> p ft n", p=128)

    def load_up(g):
        b = g % NBUF
        c0 = g * 128 * GW
        P.dma("pool", Wg[b][:], wupv[:, :, c0:c0 + 128 * GW], w=["Wg%d" % b])
        P.dma("pool", Wu[b][:], wupv[:, :, DFF + c0:DFF + c0 + 128 * GW], w=["Wu%d" % b])

    def load_dn(g):
        b = g % NBUF
        P.dma("pool", Wd[b][:], wdnv[:, g * GW:(g + 1) * GW, :], w=["Wd%d" % b])

    def up_group(g):
        b = g % NBUF
        mb = g % 2
        for i in range(GW):
            ft = g * GW + i
            s = ft % 2
            hbufs = ["H%d" % kt for kt in range(KT)]
            fns = []
            for kt in range(KT):
                for c in range(2):
                    fns.append(lambda pe, st, sp_, kt=kt, c=c, b=b, i=i: pe.matmul(
                        g_ps[:, c * 512:(c + 1) * 512], lhsT=Wg[b][:, kt, i * 128:(i + 1) * 128],
                        rhs=H[:, kt, c * 512:(c + 1) * 512], start=(kt == 0), stop=(kt == KT - 1),
                        skip_group_check=True))
                fns.append(lambda pe, st, sp_, kt=kt, b=b, i=i: pe.matmul(
                    h_ps[:, 0:2], lhsT=Wg[b][:, kt, i * 128:(i + 1) * 128], rhs=HH[:, kt, :],
                    start=(kt == 0), stop=(kt == KT - 1), skip_group_check=True))
            P.mm_group(fns, r=hbufs + ["HH", "Wg%d" % b], w=["g_ps0", "g_ps1", "h_ps"])
            for c in range(2):
                cs = slice(c * 512, (c + 1) * 512)
                P.mm_group([lambda pe, st, sp_, kt=kt, cs=cs, b=b, i=i: pe.matmul(
                    u_ps[:, cs], lhsT=Wu[b][:, kt, i * 128:(i + 1) * 128], rhs=H[:, kt, cs], start=st, stop=sp_)
                    for kt in range(KT)], r=hbufs + ["Wu%d" % b], w=["u_ps%d" % c])
            P.op("act", lambda e, s=s: e.activation(out=Gs[s][:, 0:2], in_=h_ps[:, 0:2], func=AF.Copy),
                 r=["h_ps"], w=["Gs%d_h" % s])
            for c in range(2):
                P.op("act", lambda e, s=s, c=c: e.activation(
                    out=Gs[s][:, 2 + c * 512:2 + (c + 1) * 512], in_=g_ps[:, c * 512:(c + 1) * 512], func=AF.Copy),
                    r=["g_ps%d" % c], w=["Gs%d_%d" % (s, c)])
            gsb = ["Gs%d_h" % s, "Gs%d_0" % s, "Gs%d_1" % s]
            P.op("dve", lambda e, s=s, ft=ft: e.tensor_scalar(
                out=Tm[s][:], in0=Gs[s][:, 2:T + 2], scalar1=CW[:, 2, ft:ft + 1], scalar2=None, op0=ALU.mult),
                r=gsb + ["CW"], w=["Tm%d" % s])
            P.op("dve", lambda e, s=s, ft=ft: e.scalar_tensor_tensor(
                out=Tm[s][:], in0=Gs[s][:, 1:T + 1], scalar=CW[:, 1, ft:ft + 1], in1=Tm[s][:],
                op0=ALU.mult, op1=ALU.add), r=gsb + ["CW", "Tm%d" % s], w=["Tm%d" % s])
            P.op("dve", lambda e, s=s, ft=ft: e.scalar_tensor_tensor(
                out=Tm[s][:], in0=Gs[s][:, 0:T], scalar=CW[:, 0, ft:ft + 1], in1=Tm[s][:],
                op0=ALU.mult, op1=ALU.add), r=gsb + ["CW", "Tm%d" % s], w=["Tm%d" % s])
            P.op("act", lambda e, s=s: e.activation(out=Sl[s][:], in_=Tm[s][:], func=AF.Silu),
                 r=["Tm%d" % s], w=["Sl%d" % s])
            for c in range(2):
                cs = slice(c * 512, (c + 1) * 512)
                P.op("dve", lambda e, s=s, cs=cs, mb=mb, i=i: e.tensor_tensor(
                    out=M[mb][:, i, cs], in0=u_ps[:, cs], in1=Sl[s][:, cs], op=ALU.mult),
                    r=["u_ps%d" % c, "Sl%d" % s], w=["M%d_%d_%d" % (mb, i, c)])

    ycount = [0]

    def down_group(g):
        b = g % NBUF
        mb = g % 2
        for nt in range(KT):
            for c in range(2):
                cs = slice(c * 512, (c + 1) * 512)
                yb = ycount[0] % 3
                ycount[0] += 1
                P.mm_group([lambda pe, st, sp_, i=i, cs=cs, b=b, mb=mb, nt=nt, yb=yb: pe.matmul(
                    y_ps[yb][:], lhsT=Wd[b][:, i, nt * 128:(nt + 1) * 128], rhs=M[mb][:, i, cs], start=st, stop=sp_)
                    for i in range(GW)],
                    r=["M%d_%d_%d" % (mb, i, c) for i in range(GW)] + ["Wd%d" % b], w=["y_ps%d" % yb])
                P.op("dve", lambda e, nt=nt, cs=cs, yb=yb: e.tensor_tensor(
                    out=X[:, nt, cs], in0=y_ps[yb][:], in1=X[:, nt, cs], op=ALU.add),
                    r=["y_ps%d" % yb, "X%d" % nt], w=["X%d" % nt])

    load_up(0)
    load_dn(0)
    for g in range(NG):
        if g + 1 < NG:
            load_up(g + 1)
        up_group(g)
        if g >= 1:
            down_group(g - 1)
        if g + 1 < NG:
            load_dn(g + 1)
    down_group(NG - 1)

    gp = lambda c: "g_ps%d" % c
    xb = lambda kt: "X%d" % kt
    hov = ho_d.rearrange("(kt p) t -> p kt t", p=128)
    if final:
        emit_rmsnorm(P, X, xb, gn_d, X, xb, ones_f, "nn", ss_ps=g_ps, ssbuf=gp)
        for kt in range(KT):
            P.dma("sp", hov[:, kt, :], X[:, kt, :], r=["X%d" % kt], w=["ho"])
    else:
        xov = xo_d.rearrange("(kt p) t -> p kt t", p=128)
        for kt in range(KT):
            P.dma("sp", xov[:, kt, :], X[:, kt, :], r=["X%d" % kt], w=["xo"])
        emit_rmsnorm(P, X, xb, gn_d, H, lambda kt: "H%d" % kt, ones_f, "nn", ss_ps=g_ps, ssbuf=gp)
        for kt in range(KT):
            P.dma("sp", hov[:, kt, :], H[:, kt, :], r=["H%d" % kt], w=["ho"])
    P.finish()
    return nc


HD = 128
SCALE = HD ** -0.5
TWO_PI = float(2.0 * np.pi)
CW1 = 6.28125
CW2 = float(2.0 * np.pi - 6.28125)


def rope_consts():
    half = 16
    inv = np.power(np.float32(500000.0), -np.arange(half, dtype=np.float32) * np.float32(2.0 / 32)).astype(np.float32)
    c = np.zeros((32, 3), np.float32)
    c[:, 0] = np.concatenate([inv, inv])
    c[:16, 1] = -1.0
    c[16:, 1] = 1.0
    pm = np.zeros((32, 32), np.float32)
    for e2 in range(32):
        pm[(e2 + 16) % 32, e2] = 1.0
    return c, pm


def emit_rope_tables(P, pos_d, rc_d, tag="rp"):
    nc = P.nc
    posi = P.sb("posi", [32, T], I32)
    P.dma("sp", posi[:], pos_d.partition_broadcast(32) if hasattr(pos_d, "partition_broadcast") else
          bass.AP(pos_d.tensor, pos_d.offset, [[0, 32], [1, T]]), w=["posi"])
    RC = P.sb("RC", [32, 3], F32)
    P.dma("sp", RC[:], rc_d, w=["RC"])
    posf = P.sb("posf", [32, T], F32)
    P.op("dve", lambda e: e.tensor_copy(out=posf[:], in_=posi[:]), r=["posi"], w=["posf"])
    ang = P.sb("ang", [32, T], F32)
    P.op("dve", lambda e: e.tensor_scalar(out=ang[:], in0=posf[:], scalar1=RC[:, 0:1], scalar2=None, op0=ALU.mult),
         r=["posf", "RC"], w=["ang"])
    tabs = {}
    ni = P.sb("rp_ni", [32, T], I32)
    nf = P.sb("rp_nf", [32, T], F32)
    rr = P.sb("rp_r", [32, T], F32)
    mm = P.sb("rp_m", [32, T], F32)
    for name in ("sin", "cos"):
        if name == "sin":
            P.op("dve", lambda e: e.tensor_scalar(out=nf[:], in0=ang[:], scalar1=1.0 / TWO_PI, scalar2=None,
                                                  op0=ALU.mult), r=["ang"], w=["rp_nf"])
            P.op("dve", lambda e: e.tensor_copy(out=ni[:], in_=nf[:]), r=["rp_nf"], w=["rp_ni"])
            P.op("dve", lambda e: e.tensor_copy(out=nf[:], in_=ni[:]), r=["rp_ni"], w=["rp_nf"])
            P.op("dve", lambda e: e.scalar_tensor_tensor(out=rr[:], in0=nf[:], scalar=-CW1, in1=ang[:],
                                                         op0=ALU.mult, op1=ALU.add), r=["rp_nf", "ang"], w=["rp_r"])
            P.op("dve", lambda e: e.scalar_tensor_tensor(out=rr[:], in0=nf[:], scalar=-CW2, in1=rr[:],
                                                         op0=ALU.mult, op1=ALU.add), r=["rp_nf", "rp_r"], w=["rp_r"])
        else:
            P.op("dve", lambda e: e.tensor_scalar(out=rr[:], in0=rr[:], scalar1=float(np.pi / 2), scalar2=None,
                                                  op0=ALU.add), r=["rp_r"], w=["rp_r"])
        P.op("dve", lambda e: e.tensor_scalar(out=mm[:], in0=rr[:], scalar1=float(np.pi), scalar2=-TWO_PI,
                                              op0=ALU.is_gt, op1=ALU.mult), r=["rp_r"], w=["rp_m"])
        P.op("dve", lambda e: e.tensor_tensor(out=rr[:], in0=rr[:], in1=mm[:], op=ALU.add),
             r=["rp_r", "rp_m"], w=["rp_r"])
        P.op("dve", lambda e: e.tensor_scalar(out=mm[:], in0=rr[:], scalar1=float(-np.pi), scalar2=TWO_PI,
                                              op0=ALU.is_lt, op1=ALU.mult), r=["rp_r"], w=["rp_m"])
        P.op("dve", lambda e: e.tensor_tensor(out=rr[:], in0=rr[:], in1=mm[:], op=ALU.add),
             r=["rp_r", "rp_m"], w=["rp_r"])
        P.op("dve", lambda e: e.tensor_scalar(out=rr[:], in0=rr[:], scalar1=3.1415925, scalar2=-3.1415925,
                                              op0=ALU.min, op1=ALU.max), r=["rp_r"], w=["rp_r"])
        tk = P.sb("tab_" + name + "k", [32, T], F32)
        tq = P.sb("tab_" + name + "q", [32, T], F32)
        P.op("act", lambda e, tk=tk: e.activation(out=tk[:], in_=rr[:], func=AF.Sin), r=["rp_r"], w=["tab_" + name + "k"])
        if name == "sin":
            P.op("dve", lambda e, tk=tk: e.tensor_scalar(out=tk[:], in0=tk[:], scalar1=RC[:, 1:2], scalar2=None,
                                                         op0=ALU.mult), r=["tab_sink", "RC"], w=["tab_sink"])
        P.op("dve", lambda e, tk=tk, tq=tq: e.tensor_scalar(out=tq[:], in0=tk[:], scalar1=SCALE, scalar2=None,
                                                            op0=ALU.mult), r=["tab_" + name + "k"], w=["tab_" + name + "q"])
        tabs[name + "k"] = tk
        tabs[name + "q"] = tq
    return tabs


def build_p1(even):
    import os
    nc = _new_nc()
    NIN = 5120 if even else 6160
    NH = 8 if even else 16
    DA = NH * HD
    hT_d = dram_in(nc, "hT", [D, T], BF16)
    win_d = dram_in(nc, "w_in", [D, NIN], F32)
    qT_d = dram_out(nc, "qT", [DA, T], BF16)
    kT_d = dram_out(nc, "kT", [DA, T], BF16)
    V_d = dram_out(nc, "V", [T, DA], BF16)
    if even:
        pos_d = dram_in(nc, "pos", [1, T], I32)
        rc_d = dram_in(nc, "rope_c", [32, 3], F32)
        pm_d = dram_in(nc, "rope_pm", [32, 32], F32)
        glu_d = dram_out(nc, "gluT", [1024, T], F32)
    else:
        bf_d = dram_in(nc, "b_f", [16, 1], F32)
        lf_d = dram_out(nc, "lf", [16, T], F32)

    P = Prog(nc)
    H = P.sb("H", [128, KT, T], BF16)
    hv = hT_d.rearrange("(kt p) t -> p kt t", p=128)
    for kt in range(KT):
        P.dma("sp", H[:, kt, :], hv[:, kt, :], w=["H%d" % kt])
    hbufs = ["H%d" % kt for kt in range(KT)]
    winv = win_d.rearrange("(kt p) n -> p kt n", p=128)

    NWB = 3
    W = [P.sb("W%d" % i, [128, KT, 512], BF16) for i in range(NWB)]
    wcount = [0]

    def load_cols(c0, ncol):
        b = wcount[0] % NWB
        wcount[0] += 1
        P.dma("pool", W[b][:, :, 0:ncol], winv[:, :, c0:c0 + ncol], w=["W%d" % b])
        return b

    ps = [P.ps("ps%d" % i, [128, T]) for i in range(3)]
    pcount = [0]
    st = [P.sb("st%d" % i, [128, T], BF16) for i in range(3)]
    scount = [0]

    if even:
        tabs = emit_rope_tables(P, pos_d, rc_d)
        Pm = P.sb("Pm", [32, 32], F32)
        P.dma("sp", Pm[:], pm_d, w=["Pm"])
        qs32 = [P.sb("qs32_%d" % i, [32, T], F32) for i in range(2)]
        t1 = [P.sb("rt1_%d" % i, [32, T], F32) for i in range(2)]
        t2 = [P.sb("rt2_%d" % i, [32, T], F32) for i in range(2)]
        sw_ps = P.ps("sw_ps", [32, T])
        rcount = [0]

    def feat_tile(b, i):
        pi = pcount[0] % 3
        pcount[0] += 1
        for c in range(2):
            cs = slice(c * 512, (c + 1) * 512)
            P.mm_group([lambda pe, s_, e_, kt=kt, cs=cs, b=b, i=i, pi=pi: pe.matmul(
                ps[pi][:, cs], lhsT=W[b][:, kt, i * 128:(i + 1) * 128], rhs=H[:, kt, cs], start=s_, stop=e_)
                for kt in range(KT)], r=hbufs + ["W%d" % b], w=["ps%d_%d" % (pi, c)])
        return pi

    def psb(pi):
        return ["ps%d_0" % pi, "ps%d_1" % pi]

    for which, out_d in (("q", qT_d), ("k", kT_d)):
        col0 = 0 if which == "q" else DA
        for g in range(DA // 512):
            b = load_cols(col0 + g * 512, 512)
            for i in range(4):
                h = g * 4 + i
                pi = feat_tile(b, i)
                si = scount[0] % 3
                scount[0] += 1
                sc = SCALE if which == "q" else 1.0
                if even and os.environ.get("K_SKIP2", "") != "rope":
                    ri = rcount[0] % 2
                    rcount[0] += 1
                    P.op("dve", lambda e, pi=pi, ri=ri: e.tensor_copy(out=qs32[ri][:], in_=ps[pi][0:32, :]),
                         r=psb(pi), w=["qs32_%d" % ri])
                    for c in range(2):
                        cs = slice(c * 512, (c + 1) * 512)
                        P.mm_group([lambda pe, s_, e_, cs=cs, ri=ri: pe.matmul(
                            sw_ps[:, cs], lhsT=Pm[:], rhs=qs32[ri][:, cs], start=True, stop=True)],
                            r=["qs32_%d" % ri, "Pm"], w=["sw_ps%d" % c])
                    ct = tabs["cos" + which]
                    sn = tabs["sin" + which]
                    P.op("dve", lambda e, ri=ri, ct=ct: e.tensor_tensor(out=t1[ri][:], in0=qs32[ri][:], in1=ct[:], op=ALU.mult),
                         r=["qs32_%d" % ri, "tab_cos" + which], w=["rt1_%d" % ri])
                    P.op("dve", lambda e, ri=ri, sn=sn: e.tensor_tensor(out=t2[ri][:], in0=sw_ps[:], in1=sn[:], op=ALU.mult),
                         r=["sw_ps0", "sw_ps1", "tab_sin" + which], w=["rt2_%d" % ri])
                    P.op("act", lambda e, pi=pi, si=si, sc=sc: e.activation(out=st[si][:], in_=ps[pi][:],
                                                                            func=AF.Copy, scale=sc),
                         r=psb(pi), w=["st%d_lo" % si, "st%d_hi" % si])
                    P.op("dve", lambda e, ri=ri, si=si: e.tensor_tensor(out=st[si][0:32, :], in0=t1[ri][:], in1=t2[ri][:], op=ALU.add),
                         r=["rt1_%d" % ri, "rt2_%d" % ri], w=["st%d_lo" % si])
                    P.dma("sp", out_d[h * 128:(h + 1) * 128, :], st[si][:], r=["st%d_lo" % si, "st%d_hi" % si],
                          w=[which + "T_out"])
                else:
                    P.op("act", lambda e, pi=pi, si=si, sc=sc: e.activation(out=st[si][:], in_=ps[pi][:],
                                                                            func=AF.Copy, scale=sc),
                         r=psb(pi), w=["st%d_lo" % si, "st%d_hi" % si])
                    P.dma("sp", out_d[h * 128:(h + 1) * 128, :], st[si][:], r=["st%d_lo" % si, "st%d_hi" % si],
                          w=[which + "T_out"])

    vst = [P.sb("vst%d" % i, [128, 512], BF16) for i in range(3)]
    vcount = [0]
    for g in range(DA // 512):
        b = load_cols(2 * DA + g * 512, 512)
        for tt in range(T // 128):
            pi = pcount[0] % 3
            pcount[0] += 1
            P.mm_group([lambda pe, s_, e_, kt=kt, b=b, tt=tt, pi=pi: pe.matmul(
                ps[pi][:, 0:512], lhsT=H[:, kt, tt * 128:(tt + 1) * 128], rhs=W[b][:, kt, :], start=s_, stop=e_)
                for kt in range(KT)], r=hbufs + ["W%d" % b], w=["ps%d_0" % pi])
            vi = vcount[0] % 3
            vcount[0] += 1
            P.op("dve", lambda e, pi=pi, vi=vi: e.tensor_copy(out=vst[vi][:], in_=ps[pi][:, 0:512]),
                 r=["ps%d_0" % pi], w=["vst%d" % vi])
            P.dma("sp", V_d[tt * 128:(tt + 1) * 128, g * 512:(g + 1) * 512], vst[vi][:], r=["vst%d" % vi], w=["V_out"])

    if even and os.environ.get("K_SKIP", "") == "glu":
        pass
    elif even:
        sg = [P.sb("sg%d" % i, [128, T], F32) for i in range(2)]
        gl = [P.sb("gl%d" % i, [128, T], F32) for i in range(2)]
        for g in range(2):
            ba = load_cols(3 * DA + g * 512, 512)
            bg = load_cols(3 * DA + 1024 + g * 512, 512)
            for i in range(4):
                ct_ = g * 4 + i
                pa = feat_tile(ba, i)
                pg = feat_tile(bg, i)
                s = ct_ % 2
                P.op("act", lambda e, pg=pg, s=s: e.activation(out=sg[s][:], in_=ps[pg][:], func=AF.Sigmoid),
                     r=psb(pg), w=["sg%d" % s])
                P.op("dve", lambda e, pa=pa, s=s: e.tensor_tensor(out=gl[s][:], in0=ps[pa][:], in1=sg[s][:], op=ALU.mult),
                     r=psb(pa) + ["sg%d" % s], w=["gl%d" % s])
                P.dma("sp", glu_d[ct_ * 128:(ct_ + 1) * 128, :], gl[s][:], r=["gl%d" % s], w=["glu_out"])
    else:
        b = load_cols(3 * DA, 16)
        BFt = P.sb("BFt", [16, 1], F32)
        P.dma("sp", BFt[:], bf_d, w=["BFt"])
        NB = P.sb("NB", [16, 1], F32)
        P.op("dve", lambda e: e.tensor_scalar(out=NB[:], in0=BFt[:], scalar1=-1.0, scalar2=None, op0=ALU.mult),
             r=["BFt"], w=["NB"])
        pi = pcount[0] % 3
        pcount[0] += 1
        for c in range(2):
            cs = slice(c * 512, (c + 1) * 512)
            P.mm_group([lambda pe, s_, e_, kt=kt, cs=cs, b=b, pi=pi: pe.matmul(
                ps[pi][0:16, cs], lhsT=W[b][:, kt, 0:16], rhs=H[:, kt, cs], start=s_, stop=e_)
                for kt in range(KT)], r=hbufs + ["W%d" % b], w=["ps%d_%d" % (pi, c)])
        e1 = P.sb("e1", [16, T], F32)
        l1 = P.sb("l1", [16, T], F32)
        P.op("act", lambda e, pi=pi: e.activation(out=e1[:], in_=ps[pi][0:16, :], func=AF.Exp, scale=-1.0, bias=NB[:]),
             r=psb(pi) + ["NB"], w=["e1"])
        P.op("act", lambda e: e.activation(out=l1[:], in_=e1[:], func=AF.Ln, bias=1.0), r=["e1"], w=["l1"])
        P.op("dve", lambda e: e.tensor_scalar(out=l1[:], in0=l1[:], scalar1=-1.0, scalar2=None, op0=ALU.mult),
             r=["l1"], w=["l1"])
        P.dma("sp", lf_d, l1[:], r=["l1"], w=["lf_out"])
    P.finish()
    return nc


def emit_outproj_norm(P, OT, xT_d, wout_d, gain_d, xo_d, h2_d, y_ps, ybuf, ones_f, X=None, halo=None):
    if X is None:
        X = P.sb("X", [128, KT, T], F32)
    xv = xT_d.rearrange("(kt p) t -> p kt t", p=128)
    for kt in range(KT):
        P.dma("sp", X[:, kt, :], xv[:, kt, :], w=["X%d" % kt])
    wov = wout_d.rearrange("(kt p) n -> p kt n", p=128)
    Wo = [P.sb("Wo%d" % i, [128, KT, 256], BF16) for i in range(2)]
    otb = ["OT%d" % kt for kt in range(KT)]
    for g in range(D // 256):
        b = g % 2
        P.dma("pool", Wo[b][:], wov[:, :, g * 256:(g + 1) * 256], w=["Wo%d" % b])
        for i in range(2):
            nt = g * 2 + i
            for c in range(2):
                cs = slice(c * 512, (c + 1) * 512)
                P.mm_group([lambda pe, s_, e_, kt=kt, cs=cs, b=b, i=i: pe.matmul(
                    y_ps[:, cs], lhsT=Wo[b][:, kt, i * 128:(i + 1) * 128], rhs=OT[:, kt, cs], start=s_, stop=e_)
                    for kt in range(KT)], r=otb + ["Wo%d" % b], w=[ybuf(c)])
                P.op("dve", lambda e, nt=nt, cs=cs: e.tensor_tensor(
                    out=X[:, nt, cs], in0=y_ps[:, cs], in1=X[:, nt, cs], op=ALU.add),
                    r=[ybuf(c), "X%d" % nt], w=["X%d" % nt])
    xov = xo_d.rearrange("(kt p) t -> p kt t", p=128)
    for kt in range(KT):
        P.dma("sp", xov[:, kt, :], X[:, kt, :], r=["X%d" % kt], w=["xo"])
    emit_rmsnorm(P, X, lambda kt: "X%d" % kt, gain_d, OT, lambda kt: "OT%d" % kt, ones_f, "n2",
                 ss_ps=y_ps, ssbuf=ybuf)
    hov = h2_d.rearrange("(kt p) t -> p kt t", p=128)
    for kt in range(KT):
        P.dma("sp", hov[:, kt, :], OT[:, kt, :], r=["OT%d" % kt], w=["ho"])
    if halo is not None:
        hl_d, hlg_d = halo
        P.dma("sp", hl_d.rearrange("(kt p) t -> p kt t", p=128), OT[:, :, T - 2:T],
              r=["OT%d" % kt for kt in range(KT)], w=["hl"])
        P.coll(hl_d, hlg_d, r=["hl"], w=["hlg"])


def emit_attention(P, nheads, qT_d, ksrc, vsrc, OT, ones_b, kbias_d, tri_d, fox=None, mask=None, per_head=None):
    nc = P.nc
    T2 = 2 * T
    NKT = T2 // 128
    tri = P.sb("tri", [128, 128], BF16)
    P.dma("sp", tri[:], tri_d, w=["tri"])
    if fox is not None:
        ident = P.sb("ident", [128, 128], BF16)
        P.dma("sp", ident[:], fox["ident"], w=["ident"])
    KB = P.sb("KB", [128, NKT], F32)
    P.dma("sp", KB[:], kbias_d, w=["KB"])
    kTh = [P.sb("kTh%d" % i, [128, T2], BF16) for i in range(2)]
    Vh = [P.sb("Vh%d" % i, [128, NKT, 128], BF16) for i in range(2)]
    qh = [P.sb("qh%d" % i, [128, T], BF16) for i in range(2)]
    if fox is not None:
        qaug = [P.sb("qaug%d" % i, [6, T], BF16) for i in range(2)]
        kaug = [P.sb("kaug%d" % i, [6, T2], BF16) for i in range(2)]
        for i in range(2):
            P.op("dve", lambda e, i=i: e.memset(qaug[i][:], 1.0), w=["qaug%d" % i])
            P.op("dve", lambda e, i=i: e.memset(kaug[i][:], 1.0), w=["kaug%d" % i])
    NST, NPT = 3, 4
    PT = [P.sb("PT%d" % i, [128, 512], BF16) for i in range(NPT)]
    rec = [P.sb("rec%d" % i, [128, 512], F32) for i in range(2)]
    st_ps = [P.ps("st_ps%d" % i, [128, 512]) for i in range(NST)]
    o_ps = [P.ps("o_ps%d" % i, [128, 512]) for i in range(2)]
    d_ps = [P.ps("d_ps%d" % i, [128, 512]) for i in range(1)]
    cnt = {"s": 0, "p": 0, "o": 0}
    for h in range(nheads):
        hs = h % 2
        for (csl, src) in ksrc(h):
            P.dma("sp", kTh[hs][:, csl], src, w=["kTh%d" % hs])
        for (ksl, src) in vsrc(h):
            P.dma("sp", Vh[hs][:, ksl, :], src, w=["Vh%d" % hs])
        P.dma("sp", qh[hs][:], qT_d[h * 128:(h + 1) * 128, :], w=["qh%d" % hs])
        rd = ["kTh%d" % hs, "qh%d" % hs]
        if fox is not None:
            FS = fox["FS"]
            P.dma("sp", qaug[hs][0:3, :], FS[0:3, h, T:T2], r=["FS"], w=["qaug%d" % hs])
            P.dma("sp", kaug[hs][3:6, :], FS[3:6, h, :], r=["FS"], w=["kaug%d" % hs])
            rd = rd + ["qaug%d" % hs, "kaug%d" % hs, "ident", "tri"]
        for qc in range(2):
            oi = cnt["o"] % 2
            cnt["o"] += 1
            nk = 8 + 4 * qc + 4
            q0 = qc * 512

            def s_mm(kt, hs=hs, qc=qc, q0=q0, rd=rd):
                si = cnt["s"] % NST
                cnt["s"] += 1
                c_lo = max(0, kt - 8 - 4 * qc) * 128
                mms = [lambda pe, s_, e_, kt=kt, c_lo=c_lo, si=si, hs=hs, q0=q0: pe.matmul(
                    st_ps[si][:, c_lo:512], lhsT=kTh[hs][:, kt * 128:(kt + 1) * 128],
                    rhs=qh[hs][:, q0 + c_lo:q0 + 512], start=s_, stop=e_)]
                if fox is not None:
                    mms.append(lambda pe, s_, e_, kt=kt, c_lo=c_lo, si=si, hs=hs, q0=q0: pe.matmul(
                        st_ps[si][:, c_lo:512], lhsT=kaug[hs][:, kt * 128:(kt + 1) * 128],
                        rhs=qaug[hs][:, q0 + c_lo:q0 + 512], start=s_, stop=e_))
                    if kt - 8 - 4 * qc >= 0:
                        mms.append(lambda pe, s_, e_, c_lo=c_lo, si=si: pe.matmul(
                            st_ps[si][:, c_lo:c_lo + 128], lhsT=ident[:], rhs=tri[:], start=s_, stop=e_))
                P.mm_group(mms, r=rd, w=["st_ps%d" % si])
                return si, c_lo

            def rest(kt, si, c_lo, hs=hs, qc=qc, oi=oi, nk=nk):
                pi = cnt["p"] % NPT
                cnt["p"] += 1
                P.op("act", lambda e, si=si, pi=pi, c_lo=c_lo, kt=kt: e.activation(
                    out=PT[pi][:, c_lo:512], in_=st_ps[si][:, c_lo:512], func=AF.Exp, bias=KB[:, kt:kt + 1]),
                    r=["st_ps%d" % si, "KB"], w=["PT%d" % pi])
                j0 = 8 + 4 * qc - kt
                if mask is not None:
                    jlo = j0 + c_lo // 128
                    ncol = 512 - c_lo
                    P.op("dve", lambda e, pi=pi, c_lo=c_lo, jlo=jlo, ncol=ncol: e.tensor_tensor(
                        out=PT[pi][:, c_lo:512], in0=PT[pi][:, c_lo:512],
                        in1=mask[:, jlo * 128:jlo * 128 + ncol], op=ALU.mult),
                        r=["PT%d" % pi, "mask"], w=["PT%d" % pi])
                first, last = (kt == 0), (kt == nk - 1)
                P.mm_group([lambda pe, s_, e_, kt=kt, pi=pi, c_lo=c_lo, oi=oi, hs=hs, first=first, last=last: pe.matmul(
                    o_ps[oi][:, c_lo:512], lhsT=Vh[hs][:, kt, :], rhs=PT[pi][:, c_lo:512],
                    start=first, stop=last, skip_group_check=True)],
                    r=["Vh%d" % hs, "PT%d" % pi], w=["o_ps%d" % oi])
                P.mm_group([lambda pe, s_, e_, kt=kt, pi=pi, c_lo=c_lo, first=first, last=last: pe.matmul(
                    d_ps[0][:, c_lo:512], lhsT=ones_b[:], rhs=PT[pi][:, c_lo:512],
                    start=first, stop=last, skip_group_check=True)],
                    r=["ones_b", "PT%d" % pi], w=["d_ps0"])

            pend = [s_mm(0), s_mm(1)]
            for kt in range(nk):
                if kt + 2 < nk:
                    pend.append(s_mm(kt + 2))
                rest(kt, *pend.pop(0))
            ri = oi
            P.op("dve", lambda e, ri=ri: e.reciprocal(out=rec[ri][:], in_=d_ps[0][:]),
                 r=["d_ps0"], w=["rec%d" % ri])
            P.op("dve", lambda e, ri=ri, oi=oi, h=h, q0=q0: e.tensor_tensor(
                out=OT[:, h, q0:q0 + 512], in0=o_ps[oi][:], in1=rec[ri][:], op=ALU.mult),
                r=["o_ps%d" % oi, "rec%d" % ri], w=["OT%d" % h])
        if per_head is not None:
            per_head(h)
    return dict(st_ps=st_ps, o_ps=o_ps, d_ps=d_ps)


def kv_sources(kT_d, V_d, Kg, Vg, vrows):
    Vv = V_d.rearrange("(kt p) (hh e) -> p kt hh e", p=128, e=128)
    nvt = vrows // 128

    def ksrc(h):
        j, i = (h * 128) // 1024, (h * 128) % 1024
        return [(slice(0, T), Kg[j][i:i + 128, :]), (slice(T, 2 * T), kT_d[h * 128:(h + 1) * 128, :])]

    def vsrc(h):
        out = []
        for j, g in enumerate(Vg):
            gv = g[0:vrows, :].rearrange("(kt p) (hh e) -> p kt hh e", p=128, e=128)
            out.append((slice(j * nvt, (j + 1) * nvt), gv[:, :, h, :]))
        out.append((slice(8, 16), Vv[:, :, h, :]))
        return out
    return ksrc, vsrc


def build_p2_odd():
    nc = _new_nc()
    T2 = 2 * T
    xT_d = dram_in(nc, "xT", [D, T], F32)
    qT_d = dram_in(nc, "qT", [D, T], BF16)
    kT_d = dram_in(nc, "kT", [D, T2], BF16)
    V_d = dram_in(nc, "V", [T2, D], BF16)
    lf_d = dram_in(nc, "lf", [16, T2], F32)
    kvalid_d = dram_in(nc, "kvalid", [128, 16], F32)
    tri_d = dram_in(nc, "negtri", [128, 128], BF16)
    ident_d = dram_in(nc, "ident", [128, 128], BF16)
    wout_d = dram_in(nc, "w_out", [D, D], F32)
    gain_d = dram_in(nc, "gain", [D], F32)
    xo_d = dram_out(nc, "xT_out", [D, T], F32)
    h2_d = dram_out(nc, "h2T", [D, T], BF16)
    FS = nc.dram_tensor("FS", [6, 16, T2], BF16).ap()

    P = Prog(nc)
    ones_f, ones_b = emit_consts(P)
    A = P.sb("fA", [16, T], F32)
    B = P.sb("fB", [16, T], F32)
    C = P.sb("fC", [16, T], F32)
    carry = P.sb("fcarry", [16, 1], F32)
    P.op("dve", lambda e: e.memset(carry[:], 0.0), w=["fcarry"])
    parts = [P.sb("fp%d" % i, [16, T], BF16) for i in range(6)]
    for half in range(2):
        hsl = slice(half * T, (half + 1) * T)
        P.dma("sp", A[:], lf_d[:, hsl], w=["fA"])
        P.op("dve", lambda e: e.memset(B[:], 1.0), w=["fB"])
        P.op("dve", lambda e: e.tensor_tensor_scan(out=C[:], data0=B[:], data1=A[:], initial=carry[:],
                                                   op0=ALU.mult, op1=ALU.add),
             r=["fA", "fB", "fcarry"], w=["fC"])
        P.op("dve", lambda e: e.tensor_copy(out=carry[:], in_=C[:, T - 1:T]), r=["fC"], w=["fcarry"])
        P.op("dve", lambda e: e.tensor_copy(out=parts[0][:], in_=C[:]), r=["fC"], w=["fp0"])
        P.op("dve", lambda e: e.tensor_copy(out=A[:], in_=parts[0][:]), r=["fp0"], w=["fA"])
        P.op("dve", lambda e: e.tensor_tensor(out=B[:], in0=C[:], in1=A[:], op=ALU.subtract), r=["fC", "fA"], w=["fB"])
        P.op("dve", lambda e: e.tensor_copy(out=parts[1][:], in_=B[:]), r=["fB"], w=["fp1"])
        P.op("dve", lambda e: e.tensor_copy(out=A[:], in_=parts[1][:]), r=["fp1"], w=["fA"])
        P.op("dve", lambda e: e.tensor_tensor(out=C[:], in0=B[:], in1=A[:], op=ALU.subtract), r=["fB", "fA"], w=["fC"])
        P.op("dve", lambda e: e.tensor_copy(out=parts[2][:], in_=C[:]), r=["fC"], w=["fp2"])
        for i in range(3):
            P.op("dve", lambda e, i=i: e.tensor_scalar(out=parts[3 + i][:], in0=parts[i][:], scalar1=-1.0,
                                                       scalar2=None, op0=ALU.mult), r=["fp%d" % i], w=["fp%d" % (3 + i)])
        for i in range(6):
            P.dma("sp", FS[i, :, hsl], parts[i][:], r=["fp%d" % i], w=["FS"])

    OT = P.sb("OT", [128, KT, T], BF16)
    y_ps = P.ps("y_ps", [128, T])
    emit_attention(P, 16, qT_d, kT_d, V_d, OT, ones_b, kvalid_d, tri_d, None, fox=dict(FS=FS, ident=ident_d))
    emit_outproj_norm(P, OT, xT_d, wout_d, gain_d, xo_d, h2_d, y_ps, lambda c: "y_ps%d" % c, ones_f)
    P.finish()
    return nc


def mask_strip():
    k = np.arange(128)[:, None]
    cols = np.arange(16 * 128)[None, :]
    dl = cols - k
    m = ((dl >= 0) & (dl <= 128)).astype(np.float32)
    m += ((dl >= 0) & (dl % 4 == 0) & (dl <= 512))
    m += ((dl >= 0) & (dl % 16 == 0) & (dl <= 2048))
    return m


def build_p2_even():
    nc = _new_nc()
    T2 = 2 * T
    DA = 1024
    xT_d = dram_in(nc, "xT", [D, T], F32)
    qT_d = dram_in(nc, "qT", [DA, T], BF16)
    kT_d = dram_in(nc, "kT", [DA, T2], BF16)
    V_d = dram_in(nc, "V", [T2, DA], BF16)
    glu_d = dram_in(nc, "glu", [1024, T], F32)
    gh_d = dram_in(nc, "glu_halo", [1024, 30], F32)
    cw_d = dram_in(nc, "conv_w", [31, 1024], F32)
    cb_d = dram_in(nc, "conv_b", [1024], F32)
    lg_d = dram_in(nc, "ln_g", [1024], F32)
    lb_d = dram_in(nc, "ln_b", [1024], F32)
    kvalid_d = dram_in(nc, "kvalid", [128, 16], F32)
    mask_d = dram_in(nc, "mask", [128, 2048], BF16)
    tri_d = dram_in(nc, "tri", [128, 128], BF16)
    wout_d = dram_in(nc, "w_out", [D, D], F32)
    gain_d = dram_in(nc, "gain", [D], F32)
    xo_d = dram_out(nc, "xT_out", [D, T], F32)
    h2_d = dram_out(nc, "h2T", [D, T], BF16)

    P = Prog(nc)
    ones_f, ones_b = emit_consts(P)
    X = P.sb("X", [128, KT, T], F32)
    OT = P.sb("OT", [128, KT, T], BF16)
    y_ps = P.ps("y_ps", [128, T])
    mask = P.sb("mask", [128, 2048], BF16)
    P.dma("sp", mask[:], mask_d, w=["mask"])
    CWt = P.sb("CWt", [128, 31, 8], F32)
    for k in range(31):
        P.dma("sp", CWt[:, k, :], cw_d[k, :].rearrange("(ct p) -> p ct", p=128), w=["CWt"],
              allow_slow_non_contiguous=True)
    CP = P.sb("CP", [128, 3, 8], F32)
    for j, dd in enumerate((cb_d, lg_d, lb_d)):
        P.dma("sp", CP[:, j, :], dd.rearrange("(ct p) -> p ct", p=128), w=["CP"], allow_slow_non_contiguous=True)
    Gt = [P.sb("Gt%d" % i, [128, T + 30], F32) for i in range(2)]

    def conv_tile(ct):
        eng = "dve"
        gi = ct % 2
        P.dma("sp", Gt[gi][:, 0:30], gh_d[ct * 128:(ct + 1) * 128, :], w=["Gt%d_h" % gi])
        P.dma("sp", Gt[gi][:, 30:T + 30], glu_d[ct * 128:(ct + 1) * 128, :], w=["Gt%d" % gi])
        gb = ["Gt%d_h" % gi, "Gt%d" % gi]
        ab = "X%d" % (8 + ct)
        P.op(eng, lambda e, gi=gi, ct=ct: e.tensor_scalar(
            out=X[:, 8 + ct, :], in0=Gt[gi][:, 0:T], scalar1=CWt[:, 0, ct:ct + 1], scalar2=CP[:, 0, ct:ct + 1],
            op0=ALU.mult, op1=ALU.add), r=gb + ["CWt", "CP"], w=[ab])
        for k in range(1, 31):
            P.op(eng, lambda e, gi=gi, ct=ct, k=k: e.scalar_tensor_tensor(
                out=X[:, 8 + ct, :], in0=Gt[gi][:, k:k + T], scalar=CWt[:, k, ct:ct + 1], in1=X[:, 8 + ct, :],
                op0=ALU.mult, op1=ALU.add), r=gb + ["CWt", ab], w=[ab])

    ps = emit_attention(P, 8, qT_d, kT_d, V_d, OT, ones_b, kvalid_d, tri_d, None, mask=mask, per_head=conv_tile)
    st_ps = ps["st_ps"]
    sqb = [P.sb("lsq%d" % i, [128, 512], F32) for i in range(2)]
    i = 0
    for c in range(2):
        cs = slice(c * 512, (c + 1) * 512)
        for ct in range(8):
            s_ = i % 2
            i += 1
            P.op("act", lambda e, ct=ct, s_=s_, cs=cs: e.activation(out=sqb[s_][:], in_=X[:, 8 + ct, cs], func=AF.Square),
                 r=["X%d" % (8 + ct)], w=["lsq%d" % s_])
            P.mm_group([lambda pe, a_, b_, ct=ct, cs=cs: pe.matmul(
                y_ps[:, cs], lhsT=ones_f[:], rhs=X[:, 8 + ct, cs], start=(ct == 0), stop=(ct == 7),
                skip_group_check=True)], r=["X%d" % (8 + ct), "ones_f"], w=["y_ps%d" % c])
            P.mm_group([lambda pe, a_, b_, ct=ct, s_=s_, c=c: pe.matmul(
                st_ps[c][:], lhsT=ones_f[:], rhs=sqb[s_][:], start=(ct == 0), stop=(ct == 7),
                skip_group_check=True)], r=["lsq%d" % s_, "ones_f"], w=["st_ps%d" % c])
    mean = P.sb("ln_mean", [128, T], F32)
    var = P.sb("ln_var", [128, T], F32)
    for c in range(2):
        cs = slice(c * 512, (c + 1) * 512)
        P.op("dve", lambda e, cs=cs: e.tensor_scalar(out=mean[:, cs], in0=y_ps[:, cs], scalar1=1.0 / 1024, scalar2=None,
                                                     op0=ALU.mult), r=["y_ps%d" % c], w=["ln_mean%d" % c])
        P.op("dve", lambda e, cs=cs: e.tensor_tensor(out=var[:, cs], in0=mean[:, cs], in1=mean[:, cs], op=ALU.mult),
             r=["ln_mean%d" % c], w=["ln_var%d" % c])
        P.op("dve", lambda e, cs=cs, c=c: e.scalar_tensor_tensor(
            out=var[:, cs], in0=st_ps[c][:], scalar=1.0 / 1024, in1=var[:, cs], op0=ALU.mult, op1=ALU.subtract),
            r=["st_ps%d" % c, "ln_var%d" % c], w=["ln_var%d" % c])
        P.op("act", lambda e, cs=cs: e.activation(out=var[:, cs], in_=var[:, cs], func=AF.Sqrt, bias=EPS),
             r=["ln_var%d" % c], w=["ln_var%d" % c])
        P.op("dve", lambda e, cs=cs: e.reciprocal(out=var[:, cs], in_=var[:, cs]),
             r=["ln_var%d" % c], w=["ln_var%d" % c])
    lt = [P.sb("lt%d" % i, [128, T], F32) for i in range(2)]
    stat = ["ln_mean0", "ln_mean1", "ln_var0", "ln_var1"]
    for ct in range(8):
        li = ct % 2
        P.op("dve", lambda e, ct=ct, li=li: e.tensor_tensor(out=lt[li][:], in0=X[:, 8 + ct, :], in1=mean[:], op=ALU.subtract),
             r=["X%d" % (8 + ct)] + stat, w=["lt%d" % li])
        P.op("dve", lambda e, li=li: e.tensor_tensor(out=lt[li][:], in0=lt[li][:], in1=var[:], op=ALU.mult),
             r=["lt%d" % li] + stat, w=["lt%d" % li])
        P.op("act", lambda e, ct=ct, li=li: e.activation(out=OT[:, 8 + ct, :], in_=lt[li][:], func=AF.Silu,
                                                         scale=CP[:, 1, ct:ct + 1], bias=CP[:, 2, ct:ct + 1]),
             r=["lt%d" % li, "CP"], w=["OT%d" % (8 + ct)])
    emit_outproj_norm(P, OT, xT_d, wout_d, gain_d, xo_d, h2_d, y_ps, lambda c: "y_ps%d" % c, ones_f, X=X)
    P.finish()
    return nc


def build_fused(upto=None):
    nc = bass.Bass("TRN2", target_bir_lowering=False)
    itn = lambda name, shape, dt: nc.dram_tensor(name, list(shape), dt).ap()
    specs = dict(
        xT=([D, T], F32), pos=([1, T], I32), kbias=([128, 16], F32), hmul=([128, 1], F32),
        rope_c=([32, 3], F32), rope_pm=([32, 32], F32), negtri=([128, 128], BF16), tri=([128, 128], BF16),
        ident=([128, 128], BF16), mask=([128, 2048], BF16),
        norm_mix=([4, D], F32), norm_ffn=([4, D], F32), norm_final=([D], F32))
    for e_ in range(2):
        specs.update({"ev_w_in_%d" % e_: ([D, 5120], F32), "ev_conv_w_%d" % e_: ([31, 1024], F32),
                      "ev_conv_b_%d" % e_: ([1024], F32), "ev_ln_g_%d" % e_: ([1024], F32),
                      "ev_ln_b_%d" % e_: ([1024], F32), "ev_w_out_%d" % e_: ([D, D], F32),
                      "od_w_in_%d" % e_: ([D, 6160], F32), "od_b_f_%d" % e_: ([16], F32),
                      "od_w_out_%d" % e_: ([D, D], F32)})
    for l_ in range(4):
        specs.update({"ffn_w_up_%d" % l_: ([D, 2 * DFF], F32), "ffn_conv_w_%d" % l_: ([3, DFF], F32),
                      "ffn_w_down_%d" % l_: ([DFF, D], F32)})

    class _Lazy(dict):
        def __missing__(self, name):
            shape, dt = specs[name]
            ap = nc.dram_tensor(name, list(shape), dt, kind="ExternalInput").ap()
            self[name] = ap
            return ap
    E = _Lazy()
    nc._ext_names = E
    out_d = nc.dram_tensor("out", [D, T], F32, kind="ExternalOutput").ap()
    XM, XN = itn("XM", [D, T], F32), itn("XN", [D, T], F32)
    HT, H2 = itn("HT", [D, T], BF16), itn("H2", [D, T], BF16)
    HL, HLG = itn("HL", [D, 2], BF16), itn("HLG", [2 * D, 2], BF16)
    QTo, KTo, Vo = itn("QTo", [2048, T], BF16), itn("KTo", [2048, T], BF16), itn("Vo", [T, 2048], BF16)
    QTe, KTe, Ve = itn("QTe", [1024, T], BF16), itn("KTe", [1024, T], BF16), itn("Ve", [T, 1024], BF16)
    KG = [itn("KG%d" % j, [2048, T], BF16) for j in range(2)]
    VGo = [itn("VGo%d" % j, [1024, 2048], BF16) for j in range(2)]
    VGe = itn("VGe", [2048, 1024], BF16)
    LF, LFG = itn("LF", [16, T], F32), itn("LFG", [32, T], F32)
    GLU, GHL, GHLG = itn("GLU", [1024, T], F32), itn("GHL", [1024, 32], F32), itn("GHLG", [2048, 32], F32)
    FS = itn("FS", [6, 16, 2 * T], BF16)

    def dump(src):
        P = Prog(nc)
        P.dma("sp", out_d[0:src.shape[0], :], src, w=["dump"])
        P.finish()
        return nc

    f_p0(nc, dict(xT=E["xT"], gain=E["norm_mix"][0], hT=HT))
    xcur = E["xT"]
    for l in range(4):
        e = l // 2
        if upto == "p1_%d" % l:
            f_p1(nc, dict(hT=HT, w_in=E["ev_w_in_%d" % e], qT=QTe, kT=KTe, V=Ve, pos=E["pos"], rope_c=E["rope_c"],
                          rope_pm=E["rope_pm"], gluT=GLU, glu_hl=GHL, glu_hlg=GHLG,
                          k_xchg=[(KTe, KG[0])], v_xchg=[(Ve, VGe)]), True) if l % 2 == 0 else \
                f_p1(nc, dict(hT=HT, w_in=E["od_w_in_%d" % e], qT=QTo, kT=KTo, V=Vo,
                              b_f=E["od_b_f_%d" % e].rearrange("(h o) -> h o", o=1), lf=LF, lfg=LFG,
                              k_xchg=[(KTo[0:1024, :], KG[0]), (KTo[1024:2048, :], KG[1])],
                              v_xchg=[(Vo[0:512, :], VGo[0]), (Vo[512:1024, :], VGo[1])]), False)
            return dump(GLU if l % 2 == 0 else XM)
        if l % 2 == 0:
            f_p1(nc, dict(hT=HT, w_in=E["ev_w_in_%d" % e], qT=QTe, kT=KTe, V=Ve, pos=E["pos"], rope_c=E["rope_c"],
                          rope_pm=E["rope_pm"], gluT=GLU, glu_hl=GHL, glu_hlg=GHLG,
                          k_xchg=[(KTe, KG[0])], v_xchg=[(Ve, VGe)]), True)
            f_p2_even(nc, dict(xT=xcur, qT=QTe, kT=KTe, V=Ve, Kg=[KG[0]], Vg=[VGe], gluT=GLU, glu_hlg=GHLG,
                               conv_w=E["ev_conv_w_%d" % e], conv_b=E["ev_conv_b_%d" % e], ln_g=E["ev_ln_g_%d" % e],
                               ln_b=E["ev_ln_b_%d" % e], kbias=E["kbias"], hmul=E["hmul"], mask=E["mask"], tri=E["tri"],
                               ident=E["ident"],
                               w_out=E["ev_w_out_%d" % e], gain=E["norm_ffn"][l], xT_out=XM, h2T=H2, hl=HL, hlg=HLG))
        else:
            f_p1(nc, dict(hT=HT, w_in=E["od_w_in_%d" % e], qT=QTo, kT=KTo, V=Vo,
                          b_f=E["od_b_f_%d" % e].rearrange("(h o) -> h o", o=1), lf=LF, lfg=LFG,
                          k_xchg=[(KTo[0:1024, :], KG[0]), (KTo[1024:2048, :], KG[1])],
                          v_xchg=[(Vo[0:512, :], VGo[0]), (Vo[512:1024, :], VGo[1])]), False)
            f_p2_odd(nc, dict(xT=xcur, qT=QTo, kT=KTo, V=Vo, Kg=KG, Vg=VGo, lf=LF, lfg=LFG, FS=FS,
                              kbias=E["kbias"], negtri=E["negtri"], ident=E["ident"],
                              w_out=E["od_w_out_%d" % e], gain=E["norm_ffn"][l], xT_out=XM, h2T=H2, hl=HL, hlg=HLG))
        if upto == "p2_%d" % l:
            return dump(XM)
        final = (l == 3)
        f_p3(nc, dict(xT=XM, h2T=H2, halo=HLG[0:D, :], hmul=E["hmul"], w_up=E["ffn_w_up_%d" % l],
                      conv_w=E["ffn_conv_w_%d" % l], w_down=E["ffn_w_down_%d" % l],
                      gain_next=(E["norm_final"] if final else E["norm_mix"][l + 1]),
                      xT_out=XN, hT_out=(out_d if final else HT)), final)
        xcur = XN
        if upto == "p3_%d" % l:
            return dump(XN)
    return nc


_NC_CACHE = {}


def kernel(x, positions, norm_mix, norm_ffn, norm_final, ev_w_in, ev_conv_w, ev_conv_b, ev_ln_g, ev_ln_b,
           ev_w_out, od_w_in, od_b_f, od_w_out, ffn_w_up, ffn_conv_w, ffn_w_down):
    import ml_dtypes
    bf = ml_dtypes.bfloat16
    f32 = np.float32
    x = np.asarray(x, f32)
    positions = np.asarray(positions, np.int32)
    A = lambda a: np.ascontiguousarray(np.asarray(a, f32))
    if "fused" not in _NC_CACHE:
        _NC_CACHE["fused"] = build_fused()
    nc = _NC_CACHE["fused"]
    rc, pm = rope_consts()
    shared = dict(
        rope_c=rc, rope_pm=pm,
        negtri=np.where(np.arange(128)[None, :] >= np.arange(128)[:, None], 0.0, -30000.0).astype(bf),
        tri=(np.arange(128)[None, :] >= np.arange(128)[:, None]).astype(bf),
        ident=np.eye(128).astype(bf), mask=mask_strip().astype(bf),
        norm_mix=A(norm_mix), norm_ffn=A(norm_ffn), norm_final=A(norm_final))
    for e_ in range(2):
        for nm, arr in (("ev_w_in", ev_w_in), ("ev_conv_w", ev_conv_w), ("ev_conv_b", ev_conv_b), ("ev_ln_g", ev_ln_g),
                        ("ev_ln_b", ev_ln_b), ("ev_w_out", ev_w_out), ("od_w_in", od_w_in), ("od_b_f", od_b_f),
                        ("od_w_out", od_w_out)):
            shared["%s_%d" % (nm, e_)] = A(np.asarray(arr)[e_])
    for l_ in range(4):
        for nm, arr in (("ffn_w_up", ffn_w_up), ("ffn_conv_w", ffn_conv_w), ("ffn_w_down", ffn_w_down)):
            shared["%s_%d" % (nm, l_)] = A(np.asarray(arr)[l_])
    kb_a = np.concatenate([np.full((128, 8), -30000.0, f32), np.zeros((128, 8), f32)], 1)
    kb_b = np.zeros((128, 16), f32)
    in_maps = []
    for c in range(NCORES):
        b, h = c // 2, c % 2
        m = dict(shared)
        m["xT"] = np.ascontiguousarray(x[b, h * T:(h + 1) * T, :].T)
        m["pos"] = np.ascontiguousarray(positions[b:b + 1, h * T:(h + 1) * T])
        m["kbias"] = kb_b if h == 1 else kb_a
        m["hmul"] = np.full((128, 1), float(h), f32)
        in_maps.append(m)
    used = set(nc._ext_names.keys())
    in_maps = [{k: v for k, v in m.items() if k in used} for m in in_maps]
    res = run_bass_kernel_spmd(nc, in_maps, core_ids=list(range(NCORES)))
    out = np.empty((4, 2 * T, D), f32)
    for c in range(NCORES):
        b, h = c // 2, c % 2
        out[b, h * T:(h + 1) * T, :] = np.asarray(res.results[c]["out"]).T
    return out


def f_p0(nc, io):
    P = Prog(nc)
    ones_f, ones_b = emit_consts(P)
    X = P.sb("X", [128, KT, T], F32)
    H = P.sb("H", [128, KT, T], BF16)
    xv = io["xT"].rearrange("(kt p) t -> p kt t", p=128)
    for kt in range(KT):
        P.dma("sp", X[:, kt, :], xv[:, kt, :], w=["X%d" % kt])
    emit_rmsnorm(P, X, lambda kt: "X%d" % kt, io["gain"], H, lambda kt: "H%d" % kt, ones_f, "n0")
    hv = io["hT"].rearrange("(kt p) t -> p kt t", p=128)
    for kt in range(KT):
        P.dma("sp", hv[:, kt, :], H[:, kt, :], r=["H%d" % kt], w=["hT_out"])
    P.finish()


def f_p1(nc, io, even):
    NH = 8 if even else 16
    DA = NH * HD
    hT_d, win_d = io["hT"], io["w_in"]
    qT_d, kT_d, V_d = io["qT"], io["kT"], io["V"]
    P = Prog(nc)
    H = P.sb("H", [128, KT, T], BF16)
    hv = hT_d.rearrange("(kt p) t -> p kt t", p=128)
    for kt in range(KT):
        P.dma("sp", H[:, kt, :], hv[:, kt, :], w=["H%d" % kt])
    hbufs = ["H%d" % kt for kt in range(KT)]
    winv = win_d.rearrange("(kt p) n -> p kt n", p=128)
    NWB = 4
    W = [P.sb("W%d" % i, [128, KT, 512], BF16) for i in range(NWB)]
    ps = [P.ps("ps%d" % i, [128, T]) for i in range(3)]
    pcount = [0]
    st = [P.sb("st%d" % i, [128, T], BF16) for i in range(3)]
    scount = [0]
    if even:
        tabs = emit_rope_tables(P, io["pos"], io["rope_c"])
        Pm = P.sb("Pm", [32, 32], F32)
        P.dma("sp", Pm[:], io["rope_pm"], w=["Pm"])
        qs32 = [P.sb("qs32_%d" % i, [32, T], F32) for i in range(2)]
        t1 = [P.sb("rt1_%d" % i, [32, T], F32) for i in range(2)]
        t2 = [P.sb("rt2_%d" % i, [32, T], F32) for i in range(2)]
        sw_ps = P.ps("sw_ps", [32, T])
        rcount = [0]

    groups = []
    for g in range(DA // 512):
        groups.append(("k", DA + g * 512, 512, g))
    for g in range(DA // 512):
        groups.append(("v", 2 * DA + g * 512, 512, g))
    groups.append(("xchg", 0, 0, 0))
    for g in range(DA // 512):
        groups.append(("q", g * 512, 512, g))
    if even:
        for g in range(2):
            groups.append(("glu_a", 3 * DA + g * 512, 512, g))
            groups.append(("glu_g", 3 * DA + 1024 + g * 512, 512, g))
    else:
        groups.append(("f", 3 * DA, 16, 0))
    wgroups = [g for g in groups if g[0] != "xchg"]
    slot_of = {}

    def load(i):
        if i >= len(wgroups):
            return
        kind, c0, ncol, _ = wgroups[i]
        b = i % NWB
        slot_of[i] = b
        P.dma("pool", W[b][:, :, 0:ncol], winv[:, :, c0:c0 + ncol], w=["W%d" % b])

    def feat_tile(b, i):
        pi = pcount[0] % 3
        pcount[0] += 1
        for c in range(2):
            cs = slice(c * 512, (c + 1) * 512)
            P.mm_group([lambda pe, s_, e_, kt=kt, cs=cs, b=b, i=i, pi=pi: pe.matmul(
                ps[pi][:, cs], lhsT=W[b][:, kt, i * 128:(i + 1) * 128], rhs=H[:, kt, cs], start=s_, stop=e_)
                for kt in range(KT)], r=hbufs + ["W%d" % b], w=["ps%d_%d" % (pi, c)])
        return pi

    def psb(pi):
        return ["ps%d_0" % pi, "ps%d_1" % pi]

    def qk_group(which, b, g):
        out_d = qT_d if which == "q" else kT_d
        for i in range(4):
            h = g * 4 + i
            pi = feat_tile(b, i)
            si = scount[0] % 3
            scount[0] += 1
            sc = SCALE if which == "q" else 1.0
            if even:
                ri = rcount[0] % 2
                rcount[0] += 1
                P.op("dve", lambda e, pi=pi, ri=ri: e.tensor_copy(out=qs32[ri][:], in_=ps[pi][0:32, :]),
                     r=psb(pi), w=["qs32_%d" % ri])
                for c in range(2):
                    cs = slice(c * 512, (c + 1) * 512)
                    P.mm_group([lambda pe, s_, e_, cs=cs, ri=ri: pe.matmul(
                        sw_ps[:, cs], lhsT=Pm[:], rhs=qs32[ri][:, cs], start=True, stop=True)],
                        r=["qs32_%d" % ri, "Pm"], w=["sw_ps%d" % c])
                ct = tabs["cos" + which]
                sn = tabs["sin" + which]
                P.op("dve", lambda e, ri=ri, ct=ct: e.tensor_tensor(out=t1[ri][:], in0=qs32[ri][:], in1=ct[:], op=ALU.mult),
                     r=["qs32_%d" % ri, "tab_cos" + which], w=["rt1_%d" % ri])
                P.op("dve", lambda e, ri=ri, sn=sn: e.tensor_tensor(out=t2[ri][:], in0=sw_ps[:], in1=sn[:], op=ALU.mult),
                     r=["sw_ps0", "sw_ps1", "tab_sin" + which], w=["rt2_%d" % ri])
                P.op("act", lambda e, pi=pi, si=si, sc=sc: e.activation(out=st[si][:], in_=ps[pi][:], func=AF.Copy, scale=sc),
                     r=psb(pi), w=["st%d_lo" % si, "st%d_hi" % si])
                P.op("dve", lambda e, ri=ri, si=si: e.tensor_tensor(out=st[si][0:32, :], in0=t1[ri][:], in1=t2[ri][:], op=ALU.add),
                     r=["rt1_%d" % ri, "rt2_%d" % ri], w=["st%d_lo" % si])
            else:
                P.op("act", lambda e, pi=pi, si=si, sc=sc: e.activation(out=st[si][:], in_=ps[pi][:], func=AF.Copy, scale=sc),
                     r=psb(pi), w=["st%d_lo" % si, "st%d_hi" % si])
            P.dma("sp", out_d[h * 128:(h + 1) * 128, :], st[si][:], r=["st%d_lo" % si, "st%d_hi" % si],
                  w=[which + "T_out"])

    vst = [P.sb("vst%d" % i, [128, 512], BF16) for i in range(3)]
    vcount = [0]

    def v_group(b, g):
        for tt in range(T // 128):
            pi = pcount[0] % 3
            pcount[0] += 1
            P.mm_group([lambda pe, s_, e_, kt=kt, b=b, tt=tt, pi=pi: pe.matmul(
                ps[pi][:, 0:512], lhsT=H[:, kt, tt * 128:(tt + 1) * 128], rhs=W[b][:, kt, :], start=s_, stop=e_)
                for kt in range(KT)], r=hbufs + ["W%d" % b], w=["ps%d_0" % pi])
            vi = vcount[0] % 3
            vcount[0] += 1
            P.op("dve", lambda e, pi=pi, vi=vi: e.tensor_copy(out=vst[vi][:], in_=ps[pi][:, 0:512]),
                 r=["ps%d_0" % pi], w=["vst%d" % vi])
            P.dma("sp", V_d[tt * 128:(tt + 1) * 128, g * 512:(g + 1) * 512], vst[vi][:], r=["vst%d" % vi], w=["V_out"])

    if even:
        sg = [P.sb("sg%d" % i, [128, T], F32) for i in range(2)]
        gl = [P.sb("gl%d" % i, [128, T], F32) for i in range(2)]

    def glu_group(ba, bg, g):
        for i in range(4):
            ct_ = g * 4 + i
            pa = feat_tile(ba, i)
            pg = feat_tile(bg, i)
            s_ = ct_ % 2
            P.op("act", lambda e, pg=pg, s_=s_: e.activation(out=sg[s_][:], in_=ps[pg][:], func=AF.Sigmoid),
                 r=psb(pg), w=["sg%d" % s_])
            P.op("dve", lambda e, pa=pa, s_=s_: e.tensor_tensor(out=gl[s_][:], in0=ps[pa][:], in1=sg[s_][:], op=ALU.mult),
                 r=psb(pa) + ["sg%d" % s_], w=["gl%d" % s_])
            P.dma("sp", io["gluT"][ct_ * 128:(ct_ + 1) * 128, :], gl[s_][:], r=["gl%d" % s_], w=["glu_out"])
            P.dma("sp", io["glu_hl"][ct_ * 128:(ct_ + 1) * 128, :], gl[s_][:, T - 32:T], r=["gl%d" % s_], w=["gluh_out"])

    def f_group(b):
        BFt = P.sb("BFt", [16, 1], F32)
        P.dma("sp", BFt[:], io["b_f"], w=["BFt"])
        NB = P.sb("NB", [16, 1], F32)
        P.op("dve", lambda e: e.tensor_scalar(out=NB[:], in0=BFt[:], scalar1=-1.0, scalar2=None, op0=ALU.mult),
             r=["BFt"], w=["NB"])
        pi = pcount[0] % 3
        pcount[0] += 1
        for c in range(2):
            cs = slice(c * 512, (c + 1) * 512)
            P.mm_group([lambda pe, s_, e_, kt=kt, cs=cs, b=b, pi=pi: pe.matmul(
                ps[pi][0:16, cs], lhsT=W[b][:, kt, 0:16], rhs=H[:, kt, cs], start=s_, stop=e_)
                for kt in range(KT)], r=hbufs + ["W%d" % b], w=["ps%d_%d" % (pi, c)])
        e1 = P.sb("e1", [16, T], F32)
        l1 = P.sb("l1", [16, T], F32)
        P.op("act", lambda e, pi=pi: e.activation(out=e1[:], in_=ps[pi][0:16, :], func=AF.Exp, scale=-1.0, bias=NB[:]),
             r=psb(pi) + ["NB"], w=["e1"])
        P.op("act", lambda e: e.activation(out=l1[:], in_=e1[:], func=AF.Ln, bias=1.0), r=["e1"], w=["l1"])
        P.op("dve", lambda e: e.tensor_scalar(out=l1[:], in0=l1[:], scalar1=-1.0, scalar2=None, op0=ALU.mult),
             r=["l1"], w=["l1"])
        P.dma("sp", io["lf"], l1[:], r=["l1"], w=["lf_out"])

    def xchg():
        for j, (src, dst) in enumerate(io["k_xchg"]):
            P.coll(src, dst, r=["kT_out"], w=["Kg%d" % j])
        for j, (src, dst) in enumerate(io["v_xchg"]):
            P.coll(src, dst, r=["V_out"], w=["Vg%d" % j])

    PF = NWB - 1
    for i in range(PF):
        load(i)
    wi = 0
    pend_a = None
    xdone = False
    for grp in groups:
        kind = grp[0]
        if kind == "xchg":
            continue
        load(wi + PF)
        if wi + PF >= len(wgroups) - 1 and not xdone:
            xchg()
            xdone = True
        b = slot_of[wi]
        if kind in ("q", "k"):
            qk_group(kind, b, grp[3])
        elif kind == "v":
            v_group(b, grp[3])
        elif kind == "glu_a":
            pend_a = b
        elif kind == "glu_g":
            glu_group(pend_a, b, grp[3])
        elif kind == "f":
            f_group(b)
        wi += 1
    if even:
        P.coll(io["glu_hl"], io["glu_hlg"], r=["gluh_out"], w=["gluhg"])
    else:
        P.coll(io["lf"], io["lfg"], r=["lf_out"], w=["lfg"])
    P.finish()


def f_p2_odd(nc, io):
    P = Prog(nc)
    ones_f, ones_b = emit_consts(P)
    FS = io["FS"]
    A = P.sb("fA", [16, T], F32)
    B = P.sb("fB", [16, T], F32)
    C = P.sb("fC", [16, T], F32)
    carry = P.sb("fcarry", [16, 1], F32)
    P.op("dve", lambda e: e.memset(carry[:], 0.0), w=["fcarry"])
    parts = [P.sb("fp%d" % i, [16, T], BF16) for i in range(6)]
    for half in range(2):
        hsl = slice(half * T, (half + 1) * T)
        src = io["lfg"][0:16, :] if half == 0 else io["lf"]
        P.dma("sp", A[:], src, w=["fA"])
        P.op("dve", lambda e: e.memset(B[:], 1.0), w=["fB"])
        P.op("dve", lambda e: e.tensor_tensor_scan(out=C[:], data0=B[:], data1=A[:], initial=carry[:],
                                                   op0=ALU.mult, op1=ALU.add),
             r=["fA", "fB", "fcarry"], w=["fC"])
        P.op("dve", lambda e: e.tensor_copy(out=carry[:], in_=C[:, T - 1:T]), r=["fC"], w=["fcarry"])
        P.op("dve", lambda e: e.tensor_copy(out=parts[0][:], in_=C[:]), r=["fC"], w=["fp0"])
        P.op("dve", lambda e: e.tensor_copy(out=A[:], in_=parts[0][:]), r=["fp0"], w=["fA"])
        P.op("dve", lambda e: e.tensor_tensor(out=B[:], in0=C[:], in1=A[:], op=ALU.subtract), r=["fC", "fA"], w=["fB"])
        P.op("dve", lambda e: e.tensor_copy(out=parts[1][:], in_=B[:]), r=["fB"], w=["fp1"])
        P.op("dve", lambda e: e.tensor_copy(out=A[:], in_=parts[1][:]), r=["fp1"], w=["fA"])
        P.op("dve", lambda e: e.tensor_tensor(out=C[:], in0=B[:], in1=A[:], op=ALU.subtract), r=["fB", "fA"], w=["fC"])
        P.op("dve", lambda e: e.tensor_copy(out=parts[2][:], in_=C[:]), r=["fC"], w=["fp2"])
        for i in range(3):
            P.op("dve", lambda e, i=i: e.tensor_scalar(out=parts[3 + i][:], in0=parts[i][:], scalar1=-1.0,
                                                       scalar2=None, op0=ALU.mult), r=["fp%d" % i], w=["fp%d" % (3 + i)])
        for i in range(6):
            P.dma("sp", FS[i, :, hsl], parts[i][:], r=["fp%d" % i], w=["FS"])
    OT = P.sb("OT", [128, KT, T], BF16)
    y_ps = P.ps("y_ps", [128, T])
    ksrc, vsrc = kv_sources(io["kT"], io["V"], io["Kg"], io["Vg"], 512)
    emit_attention(P, 16, io["qT"], ksrc, vsrc, OT, ones_b, io["kbias"], io["negtri"],
                   fox=dict(FS=FS, ident=io["ident"]))
    emit_outproj_norm(P, OT, io["xT"], io["w_out"], io["gain"], io["xT_out"], io["h2T"], y_ps,
                      lambda c: "y_ps%d" % c, ones_f, halo=(io["hl"], io["hlg"]))
    P.finish()


def f_p2_even(nc, io):
    P = Prog(nc)
    ones_f, ones_b = emit_consts(P)
    X = P.sb("X", [128, KT, T], F32)
    OT = P.sb("OT", [128, KT, T], BF16)
    y_ps = P.ps("y_ps", [128, T])
    mask = P.sb("mask", [128, 2048], BF16)
    P.dma("sp", mask[:], io["mask"], w=["mask"])
    HM = P.sb("HM", [128, 1], F32)
    P.dma("sp", HM[:], io["hmul"], w=["HM"])
    CWt = P.sb("CWt", [128, 31, 8], F32)
    for k in range(31):
        P.dma("sp", CWt[:, k, :], io["conv_w"][k, :].rearrange("(ct p) -> p ct", p=128), w=["CWt"],
              allow_slow_non_contiguous=True)
    CP = P.sb("CP", [128, 3, 8], F32)
    for j, dd in enumerate((io["conv_b"], io["ln_g"], io["ln_b"])):
        P.dma("sp", CP[:, j, :], dd.rearrange("(ct p) -> p ct", p=128), w=["CP"], allow_slow_non_contiguous=True)
    Gt = [P.sb("Gt%d" % i, [128, T + 30], BF16) for i in range(2)]
    Dg = [P.sb("Dg%d" % i, [128, 31, 128], BF16) for i in range(2)]
    identc = P.sb("identc", [128, 128], BF16)
    P.dma("sp", identc[:], io["ident"], w=["identc"])
    glu_d, ghg = io["gluT"], io["glu_hlg"]

    def conv_tile(ct):
        gi = ct % 2
        P.dma("pool", Gt[gi][:, 0:30], ghg[ct * 128:(ct + 1) * 128, 2:32], w=["Gt%d_h" % gi])
        P.dma("pool", Gt[gi][:, 30:T + 30], glu_d[ct * 128:(ct + 1) * 128, :], w=["Gt%d" % gi])
        P.op("dve", lambda e, gi=gi: e.tensor_scalar(out=Gt[gi][:, 0:30], in0=Gt[gi][:, 0:30], scalar1=HM[:, 0:1],
                                                     scalar2=None, op0=ALU.mult), r=["Gt%d_h" % gi, "HM"], w=["Gt%d_h" % gi])
        for k in range(31):
            P.op("dve", lambda e, gi=gi, ct=ct, k=k: e.tensor_scalar(
                out=Dg[gi][:, k, :], in0=identc[:], scalar1=CWt[:, k, ct:ct + 1], scalar2=None, op0=ALU.mult),
                r=["identc", "CWt"], w=["Dg%d" % gi])
        gb = ["Gt%d_h" % gi, "Gt%d" % gi, "Dg%d" % gi]
        ab = "X%d" % (8 + ct)
        for c in range(2):
            P.mm_group([lambda pe, s_, e_, gi=gi, k=k, c=c: pe.matmul(
                y_ps[:, c * 512:(c + 1) * 512], lhsT=Dg[gi][:, k, :], rhs=Gt[gi][:, k + c * 512:k + c * 512 + 512],
                start=s_, stop=e_) for k in range(31)], r=gb, w=["y_ps%d" % c])
            P.op("act", lambda e, ct=ct, c=c: e.activation(
                out=X[:, 8 + ct, c * 512:(c + 1) * 512], in_=y_ps[:, c * 512:(c + 1) * 512], func=AF.Identity,
                bias=CP[:, 0, ct:ct + 1]), r=["y_ps%d" % c, "CP"], w=[ab])

    ksrc, vsrc = kv_sources(io["kT"], io["V"], io["Kg"], io["Vg"], 1024)
    ps = emit_attention(P, 8, io["qT"], ksrc, vsrc, OT, ones_b, io["kbias"], io["tri"], mask=mask, per_head=conv_tile)
    st_ps = ps["st_ps"]
    sqb = [P.sb("lsq%d" % i, [128, 512], F32) for i in range(2)]
    i = 0
    for c in range(2):
        cs = slice(c * 512, (c + 1) * 512)
        for ct in range(8):
            s_ = i % 2
            i += 1
            P.op("act", lambda e, ct=ct, s_=s_, cs=cs: e.activation(out=sqb[s_][:], in_=X[:, 8 + ct, cs], func=AF.Square),
                 r=["X%d" % (8 + ct)], w=["lsq%d" % s_])
            P.mm_group([lambda pe, a_, b_, ct=ct, cs=cs: pe.matmul(
                y_ps[:, cs], lhsT=ones_f[:], rhs=X[:, 8 + ct, cs], start=(ct == 0), stop=(ct == 7),
                skip_group_check=True)], r=["X%d" % (8 + ct), "ones_f"], w=["y_ps%d" % c])
            P.mm_group([lambda pe, a_, b_, ct=ct, s_=s_, c=c: pe.matmul(
                st_ps[c][:], lhsT=ones_f[:], rhs=sqb[s_][:], start=(ct == 0), stop=(ct == 7),
                skip_group_check=True)], r=["lsq%d" % s_, "ones_f"], w=["st_ps%d" % c])
    mean = P.sb("ln_mean", [128, T], F32)
    var = P.sb("ln_var", [128, T], F32)
    for c in range(2):
        cs = slice(c * 512, (c + 1) * 512)
        P.op("dve", lambda e, cs=cs: e.tensor_scalar(out=mean[:, cs], in0=y_ps[:, cs], scalar1=1.0 / 1024, scalar2=None,
                                                     op0=ALU.mult), r=["y_ps%d" % c], w=["ln_mean%d" % c])
        P.op("dve", lambda e, cs=cs: e.tensor_tensor(out=var[:, cs], in0=mean[:, cs], in1=mean[:, cs], op=ALU.mult),
             r=["ln_mean%d" % c], w=["ln_var%d" % c])
        P.op("dve", lambda e, cs=cs, c=c: e.scalar_tensor_tensor(
            out=var[:, cs], in0=st_ps[c][:], scalar=1.0 / 1024, in1=var[:, cs], op0=ALU.mult, op1=ALU.subtract),
            r=["st_ps%d" % c, "ln_var%d" % c], w=["ln_var%d" % c])
        P.op("act", lambda e, cs=cs: e.activation(out=var[:, cs], in_=var[:, cs], func=AF.Sqrt, bias=EPS),
             r=["ln_var%d" % c], w=["ln_var%d" % c])
        P.op("dve", lambda e, cs=cs: e.reciprocal(out=var[:, cs], in_=var[:, cs]),
             r=["ln_var%d" % c], w=["ln_var%d" % c])
    lt = [P.sb("lt%d" % i, [128, T], F32) for i in range(2)]
    stat = ["ln_mean0", "ln_mean1", "ln_var0", "ln_var1"]
    for ct in range(8):
        li = ct % 2
        P.op("dve", lambda e, ct=ct, li=li: e.tensor_tensor(out=lt[li][:], in0=X[:, 8 + ct, :], in1=mean[:], op=ALU.subtract),
             r=["X%d" % (8 + ct)] + stat, w=["lt%d" % li])
        P.op("dve", lambda e, li=li: e.tensor_tensor(out=lt[li][:], in0=lt[li][:], in1=var[:], op=ALU.mult),
             r=["lt%d" % li] + stat, w=["lt%d" % li])
        P.op("act", lambda e, ct=ct, li=li: e.activation(out=OT[:, 8 + ct, :], in_=lt[li][:], func=AF.Silu,
                                                         scale=CP[:, 1, ct:ct + 1], bias=CP[:, 2, ct:ct + 1]),
             r=["lt%d" % li, "CP"], w=["OT%d" % (8 + ct)])
    emit_outproj_norm(P, OT, io["xT"], io["w_out"], io["gain"], io["xT_out"], io["h2T"], y_ps,
                      lambda c: "y_ps%d" % c, ones_f, X=X, halo=((io["hl"], io["hlg"]) if "hl" in io else None))
    P.finish()
```

```python
import numpy as np
from contextlib import ExitStack
import concourse.bass as bass
import concourse.mybir as mybir
from concourse.bass_utils import run_bass_kernel_spmd

F32 = mybir.dt.float32
BF16 = mybir.dt.bfloat16
I32 = mybir.dt.int32
AF = mybir.ActivationFunctionType
ALU = mybir.AluOpType

D = 2048
T = 1024
KT = D // 128
DFF = 5632
NFT = DFF // 128
EPS = 1e-6
NCORES = 8


class Prog:
    ENGS = ("pe", "act", "dve", "pool", "sp")
    UID = [0]
    POOL = {}

    def __init__(self, nc):
        self.nc = nc
        Prog.UID[0] += 1
        self.uid = "p%d_" % Prog.UID[0]
        self.coll_cnt = {}
        self.ops = {e: [] for e in self.ENGS}
        self.nsig = {e: 0 for e in self.ENGS}
        self.dma_cnt = {}
        self.writer = {}
        self.readers = {}
        self.stack = ExitStack()

    def sb(self, name, shape, dt):
        return self.stack.enter_context(self.nc.sbuf_tensor(self.uid + "sb_" + name, list(shape), dt))

    def ps(self, name, shape, dt=F32):
        return self.stack.enter_context(self.nc.psum_tensor(self.uid + "pp_" + name, list(shape), dt))

    @staticmethod
    def _is_psum(b):
        return ("ps" in b) or b.startswith("ss_")

    def _deps(self, r, w):
        toks = []
        for b in r:
            if b in self.writer:
                toks.append(self.writer[b])
            if self._is_psum(b):
                toks.extend(self.readers.get(b, []))
        for b in w:
            if b in self.writer:
                toks.append(self.writer[b])
            toks.extend(self.readers.get(b, []))
        return toks

    def _commit(self, tok, r, w):
        for b in r:
            self.readers.setdefault(b, []).append(tok)
        for b in w:
            self.writer[b] = tok
            self.readers[b] = []

    def op(self, eng, fn, r=(), w=(), extra=()):
        waits = self._deps(r, w) + list(extra)
        self.nsig[eng] += 1
        tok = ("E:" + eng, self.nsig[eng])
        self.ops[eng].append((fn, waits, tok))
        self._commit(tok, r, w)
        return tok

    def mm_group(self, mms, r=(), w=()):
        waits = self._deps(r, w)
        n = len(mms)
        self.nsig["pe"] += 1
        tok = ("E:pe", self.nsig["pe"])
        for i, fn in enumerate(mms):
            f = (lambda pe, fn=fn, i=i: fn(pe, i == 0, i == n - 1))
            self.ops["pe"].append((f, waits if i == 0 else [], tok if i == n - 1 else None))
        self._commit(tok, r, w)
        return tok

    def dma(self, queue, out, in_, r=(), w=(), key=None, **kw):
        waits = self._deps(r, w)
        if key is None:
            key = w[0] if w else r[0]
        self.dma_cnt[key] = self.dma_cnt.get(key, 0) + 1
        tok = ("D:" + str(key), 16 * self.dma_cnt[key])
        fn = (lambda e: e.dma_start(out=out, in_=in_, **kw))
        self.ops[queue].append((fn, waits, tok))
        self._commit(tok, r, w)
        return tok

    def coll(self, in_ap, out_ap, r=(), w=(), key=None):
        waits = self._deps(r, w)
        if key is None:
            key = w[0]
        self.coll_cnt[key] = self.coll_cnt.get(key, 0) + 1
        tok = ("C:" + str(key), self.coll_cnt[key])
        fn = (lambda e: e.collective_compute("AllGather", ALU.bypass, replica_groups=PAIRS,
                                             ins=[in_ap], outs=[out_ap]))
        self.ops["pool"].append((fn, waits, tok))
        self._commit(tok, r, w)
        return tok

    def check_deadlock(self):
        pos = {e: 0 for e in self.ENGS}
        val = {}
        progress = True
        while progress:
            progress = False
            for e in self.ENGS:
                ops = self.ops[e]
                while pos[e] < len(ops):
                    fn, waits, tok = ops[pos[e]]
                    ok = all((k == "E:pe" and e == "pe") or val.get(k, 0) >= v for (k, v) in waits)
                    if not ok:
                        break
                    if tok is not None:
                        k, v = tok
                        val[k] = val.get(k, 0) + (16 if k.startswith("D:") else 1)
                    pos[e] += 1
                    progress = True
        stuck = {e: (pos[e], len(self.ops[e])) for e in self.ENGS if pos[e] < len(self.ops[e])}
        if stuck:
            msg = []
            for e, (p, n) in stuck.items():
                fn, waits, tok = self.ops[e][p]
                bad = [(k, v, val.get(k, 0)) for (k, v) in waits if val.get(k, 0) < v]
                msg.append("%s stuck at %d/%d waiting %s (tok %s)" % (e, p, n, bad, tok))
            raise RuntimeError("DEADLOCK: " + "; ".join(msg))

    def finish(self, final_tokens=None):
        self.check_deadlock()
        nc = self.nc
        semkeys = (["E:" + e for e in self.ENGS] + ["D:" + str(k) for k in self.dma_cnt]
                   + ["C:" + str(k) for k in self.coll_cnt])
        pool = Prog.POOL
        if pool.get("nc") is not nc:
            pool.clear()
            pool.update(nc=nc, handles=[], base=[], stack=ExitStack())
        while len(pool["handles"]) < len(semkeys):
            pool["handles"].append(pool["stack"].enter_context(nc.semaphore("gs%d" % len(pool["handles"]))))
            pool["base"].append(0)
        slot = {k: i for i, k in enumerate(semkeys)}
        base = {k: pool["base"][slot[k]] for k in semkeys}

        class _Sems(dict):
            pass
        sems = {k: pool["handles"][slot[k]] for k in semkeys}
        totals = {}
        for e_ in self.ENGS:
            totals["E:" + e_] = self.nsig[e_]
        for k, c in self.dma_cnt.items():
            totals["D:" + str(k)] = 16 * c
        for k, c in self.coll_cnt.items():
            totals["C:" + str(k)] = c
        fin = [("D:" + str(k), 16 * c) for k, c in self.dma_cnt.items()]
        fin += [("C:" + str(k), c) for k, c in self.coll_cnt.items()]
        eng_of = {"pe": "tensor", "act": "scalar", "dve": "vector", "pool": "gpsimd", "sp": "sync"}
        with nc.Block() as block:
            for ename in self.ENGS:
                ops = self.ops[ename]
                extra_fin = fin if ename == "sp" else []

                def body(e, ops=ops, ename=ename, extra_fin=extra_fin):
                    seen = {}
                    for fn, waits, tok in ops:
                        for (k, v) in waits:
                            if k == "E:pe" and ename == "pe":
                                continue
                            if seen.get(k, 0) >= v:
                                continue
                            e.wait_ge(sems[k], base[k] + v)
                            seen[k] = v
                        ins = fn(e)
                        if tok is not None:
                            k, v = tok
                            ins.then_inc(sems[k], 16 if k.startswith("D:") else 1)
                    for (k, v) in extra_fin:
                        if seen.get(k, 0) < v:
                            e.wait_ge(sems[k], base[k] + v)

                getattr(block, eng_of[ename])(body)
        for k in semkeys:
            pool["base"][slot[k]] += totals[k]
        self.stack.close()


_FUSED = {"nc": None, "io": {}}


def _new_nc():
    if _FUSED["nc"] is not None:
        return _FUSED["nc"]
    return bass.Bass("TRN2", target_bir_lowering=False)


def dram_in(nc, name, shape, dt):
    if _FUSED["nc"] is not None:
        ap = _FUSED["io"][name]
        assert list(ap.shape) == list(shape), (name, ap.shape, shape)
        return ap
    return nc.dram_tensor(name, list(shape), dt, kind="ExternalInput").ap()


def dram_out(nc, name, shape, dt):
    if _FUSED["nc"] is not None:
        ap = _FUSED["io"][name]
        assert list(ap.shape) == list(shape), (name, ap.shape, shape)
        return ap
    return nc.dram_tensor(name, list(shape), dt, kind="ExternalOutput").ap()


PAIRS = [[0, 1], [2, 3], [4, 5], [6, 7]]


def emit_consts(P):
    ones_f = P.sb("ones_f", [128, 128], F32)
    ones_b = P.sb("ones_b", [128, 128], BF16)
    P.op("dve", lambda e: e.memset(ones_f[:], 1.0), w=["ones_f"])
    P.op("dve", lambda e: e.memset(ones_b[:], 1.0), w=["ones_b"])
    return ones_f, ones_b


def emit_rmsnorm(P, X, xbuf, gain_d, H, hbuf, ones_f, tag, ncols=T, ss_ps=None, ssbuf=None):
    nc = P.nc
    G = P.sb("G_" + tag, [128, KT], F32)
    P.dma("sp", G[:], gain_d.rearrange("(kt p) -> p kt", p=128), w=["G_" + tag],
          allow_slow_non_contiguous=True)
    sq = [P.sb("sq%d_%s" % (i, tag), [128, 512], BF16) for i in range(2)]
    ones_sq = P.sb("ones_sq_" + tag, [128, 128], BF16)
    P.op("dve", lambda e: e.memset(ones_sq[:], 1.0), w=["ones_sq_" + tag])
    if ss_ps is None:
        ss_ps = P.ps("ss_" + tag, [128, ncols])
    if ssbuf is None:
        ssbuf = lambda c: "ss_%s_%d" % (tag, c)
    rstd = P.sb("rstd_" + tag, [128, ncols], F32)
    nch = ncols // 512
    i = 0
    for c in range(nch):
        cs = slice(c * 512, (c + 1) * 512)
        mms = []
        for kt in range(KT):
            s = i % 2
            i += 1
            P.op("act", lambda e, kt=kt, s=s, cs=cs: e.activation(out=sq[s][:], in_=X[:, kt, cs], func=AF.Square),
                 r=[xbuf(kt)], w=["sq%d_%s" % (s, tag)])
            P.mm_group([lambda pe, st, sp_, s=s, cs=cs, kt=kt: pe.matmul(
                ss_ps[:, cs], lhsT=ones_sq[:], rhs=sq[s][:], start=(kt == 0), stop=(kt == KT - 1),
                skip_group_check=True)],
                r=["sq%d_%s" % (s, tag), "ones_sq_" + tag], w=[ssbuf(c)])
        P.op("act", lambda e, cs=cs: e.activation(out=rstd[:, cs], in_=ss_ps[:, cs], func=AF.Sqrt,
                                                  scale=1.0 / D, bias=EPS),
             r=[ssbuf(c)], w=["rstd_%s_%d" % (tag, c)])
        P.op("dve", lambda e, cs=cs: e.reciprocal(out=rstd[:, cs], in_=rstd[:, cs]),
             r=["rstd_%s_%d" % (tag, c)], w=["rstd_%s_%d" % (tag, c)])
    for kt in range(KT):
        P.op("dve", lambda e, kt=kt: e.scalar_tensor_tensor(
            out=H[:, kt, :], in0=X[:, kt, :], scalar=G[:, kt:kt + 1], in1=rstd[:],
            op0=ALU.mult, op1=ALU.mult),
            r=[xbuf(kt), "G_" + tag] + ["rstd_%s_%d" % (tag, c) for c in range(nch)], w=[hbuf(kt)])


def build_p0():
    nc = _new_nc()
    xT_d = dram_in(nc, "xT", [D, T], F32)
    g_d = dram_in(nc, "gain", [D], F32)
    hT_d = dram_out(nc, "hT", [D, T], BF16)
    P = Prog(nc)
    ones_f, ones_b = emit_consts(P)
    X = P.sb("X", [128, KT, T], F32)
    H = P.sb("H", [128, KT, T], BF16)
    xv = xT_d.rearrange("(kt p) t -> p kt t", p=128)
    for kt in range(KT):
        P.dma("sp", X[:, kt, :], xv[:, kt, :], w=["X%d" % kt])
    emit_rmsnorm(P, X, lambda kt: "X%d" % kt, g_d, H, lambda kt: "H%d" % kt, ones_f, "n0")
    hv = hT_d.rearrange("(kt p) t -> p kt t", p=128)
    for kt in range(KT):
        P.dma("sp", hv[:, kt, :], H[:, kt, :], r=["H%d" % kt], w=["hT_out"])
    P.finish()
    return nc


def f_p3(nc, io, final):
    xT_d, h2_d, halo_d = io["xT"], io["h2T"], io["halo"]
    wup_d, cw_d, wdn_d, gn_d = io["w_up"], io["conv_w"], io["w_down"], io["gain_next"]
    xo_d = None if final else io["xT_out"]
    ho_d = io["hT_out"]

    P = Prog(nc)
    ones_f, ones_b = emit_consts(P)
    X = P.sb("X", [128, KT, T], F32)
    H = P.sb("H", [128, KT, T], BF16)
    HH = P.sb("HH", [128, KT, 2], BF16)
    xv = xT_d.rearrange("(kt p) t -> p kt t", p=128)
    hv = h2_d.rearrange("(kt p) t -> p kt t", p=128)
    for kt in range(KT):
        P.dma("sp", H[:, kt, :], hv[:, kt, :], w=["H%d" % kt])
    P.dma("sp", HH[:], halo_d.rearrange("(kt p) t -> p kt t", p=128), w=["HH"])
    HM = P.sb("HM", [128, 1], F32)
    P.dma("sp", HM[:], io["hmul"], w=["HM"])
    P.op("dve", lambda e: e.tensor_scalar(out=HH[:], in0=HH[:], scalar1=HM[:, 0:1], scalar2=None, op0=ALU.mult),
         r=["HH", "HM"], w=["HH"])
    for kt in range(KT):
        P.dma("sp", X[:, kt, :], xv[:, kt, :], w=["X%d" % kt])
    CW = P.sb("CW", [128, 3, NFT], F32)
    for j in range(3):
        P.dma("sp", CW[:, j, :], cw_d[j, :].rearrange("(ft p) -> p ft", p=128), w=["CW"],
              allow_slow_non_contiguous=True)

    GW = 2
    NG = NFT // GW
    NBUF = 2
    Wg = [P.sb("Wg%d" % i, [128, KT, 128 * GW], BF16) for i in range(NBUF)]
    Wu = [P.sb("Wu%d" % i, [128, KT, 128 * GW], BF16) for i in range(NBUF)]
    Wd = [P.sb("Wd%d" % i, [128, GW, D], BF16) for i in range(NBUF)]
    M = [P.sb("M%d" % i, [128, GW, T], BF16) for i in range(2)]
    Gs = [P.sb("Gs%d" % i, [128, T + 2], F32) for i in range(2)]
    Tm = [P.sb("Tm%d" % i, [128, T], F32) for i in range(2)]
    Sl = [P.sb("Sl%d" % i, [128, T], F32) for i in range(2)]
    g_ps = P.ps("g_ps", [128, T])
    u_ps = P.ps("u_ps", [128, T])
    h_ps = P.ps("h_ps", [128, 512])
    y_ps = [P.ps("y_ps%d" % i, [128, 512]) for i in range(3)]

    wupv = wup_d.rearrange("(kt p) n -> p kt n", p=128)
    wdnv = wdn_d.rearrange("(ft p) n -> p ft n", p=128)

    def load_up(g):
        b = g % NBUF
        c0 = g * 128 * GW
        P.dma("pool", Wg[b][:], wupv[:, :, c0:c0 + 128 * GW], w=["Wg%d" % b])
        P.dma("pool", Wu[b][:], wupv[:, :, DFF + c0:DFF + c0 + 128 * GW], w=["Wu%d" % b])

    def load_dn(g):
        b = g % NBUF
        P.dma("pool", Wd[b][:], wdnv[:, g * GW:(g + 1) * GW, :], w=["Wd%d" % b])

    def up_group(g):
        b = g % NBUF
        mb = g % 2
        for i in range(GW):
            ft = g * GW + i
            s = ft % 2
            hbufs = ["H%d" % kt for kt in range(KT)]
            fns = []
            for kt in range(KT):
                for c in range(2):
                    fns.append(lambda pe, st, sp_, kt=kt, c=c, b=b, i=i: pe.matmul(
                        g_ps[:, c * 512:(c + 1) * 512], lhsT=Wg[b][:, kt, i * 128:(i + 1) * 128],
                        rhs=H[:, kt, c * 512:(c + 1) * 512], start=(kt == 0), stop=(kt == KT - 1),
                        skip_group_check=True))
                fns.append(lambda pe, st, sp_, kt=kt, b=b, i=i: pe.matmul(
                    h_ps[:, 0:2], lhsT=Wg[b][:, kt, i * 128:(i + 1) * 128], rhs=HH[:, kt, :],
                    start=(kt == 0), stop=(kt == KT - 1), skip_group_check=True))
            P.mm_group(fns, r=hbufs + ["HH", "Wg%d" % b], w=["g_ps0", "g_ps1", "h_ps"])
            for c in range(2):
                cs = slice(c * 512, (c + 1) * 512)
                P.mm_group([lambda pe, st, sp_, kt=kt, cs=cs, b=b, i=i: pe.matmul(
                    u_ps[:, cs], lhsT=Wu[b][:, kt, i * 128:(i + 1) * 128], rhs=H[:, kt, cs], start=st, stop=sp_)
                    for kt in range(KT)], r=hbufs + ["Wu%d" % b], w=["u_ps%d" % c])
            P.op("act", lambda e, s=s: e.activation(out=Gs[s][:, 0:2], in_=h_ps[:, 0:2], func=AF.Copy),
                 r=["h_ps"], w=["Gs%d_h" % s])
            for c in range(2):
                P.op("act", lambda e, s=s, c=c: e.activation(
                    out=Gs[s][:, 2 + c * 512:2 + (c + 1) * 512], in_=g_ps[:, c * 512:(c + 1) * 512], func=AF.Copy),
                    r=["g_ps%d" % c], w=["Gs%d_%d" % (s, c)])
            gsb = ["Gs%d_h" % s, "Gs%d_0" % s, "Gs%d_1" % s]
            P.op("dve", lambda e, s=s, ft=ft: e.tensor_scalar(
                out=Tm[s][:], in0=Gs[s][:, 2:T + 2], scalar1=CW[:, 2, ft:ft + 1], scalar2=None, op0=ALU.mult),
                r=gsb + ["CW"], w=["Tm%d" % s])
            P.op("dve", lambda e, s=s, ft=ft: e.scalar_tensor_tensor(
                out=Tm[s][:], in0=Gs[s][:, 1:T + 1], scalar=CW[:, 1, ft:ft + 1], in1=Tm[s][:],
                op0=ALU.mult, op1=ALU.add), r=gsb + ["CW", "Tm%d" % s], w=["Tm%d" % s])
            P.op("dve", lambda e, s=s, ft=ft: e.scalar_tensor_tensor(
                out=Tm[s][:], in0=Gs[s][:, 0:T], scalar=CW[:, 0, ft:ft + 1], in1=Tm[s][:],
                op0=ALU.mult, op1=ALU.add), r=gsb + ["CW", "Tm%d" % s], w=["Tm%d" % s])
            P.op("act", lambda e, s=s: e.activation(out=Sl[s][:], in_=Tm[s][:], func=AF.Silu),
                 r=["Tm%d" % s], w=["Sl%d" % s])
            for c in range(2):
                cs = slice(c * 512, (c + 1) * 512)
                P.op("dve", lambda e, s=s, cs=cs, mb=mb, i=i: e.tensor_tensor(
                    out=M[mb][:, i, cs], in0=u_ps[:, cs], in1=Sl[s][:, cs], op=ALU.mult),
                    r=["u_ps%d" % c, "Sl%d" % s], w=["M%d_%d_%d" % (mb, i, c)])

    ycount = [0]

    def down_group(g):
        b = g % NBUF
        mb = g % 2
        for nt in range(KT):
            for c in range(2):
                cs = slice(c * 512, (c + 1) * 512)
                yb = ycount[0] % 3
                ycount[0] += 1
                P.mm_group([lambda pe, st, sp_, i=i, cs=cs, b=b, mb=mb, nt=nt, yb=yb: pe.matmul(
                    y_ps[yb][:], lhsT=Wd[b][:, i, nt * 128:(nt + 1) * 128], rhs=M[mb][:, i, cs], start=st, stop=sp_)
                    for i in range(GW)],
                    r=["M%d_%d_%d" % (mb, i, c) for i in range(GW)] + ["Wd%d" % b], w=["y_ps%d" % yb])
                P.op("dve", lambda e, nt=nt, cs=cs, yb=yb: e.tensor_tensor(
                    out=X[:, nt, cs], in0=y_ps[yb][:], in1=X[:, nt, cs], op=ALU.add),
                    r=["y_ps%d" % yb, "X%d" % nt], w=["X%d" % nt])

    load_up(0)
    load_dn(0)
    for g in range(NG):
        if g + 1 < NG:
            load_up(g + 1)
        up_group(g)
        if g >= 1:
            down_group(g - 1)
        if g + 1 < NG:
            load_dn(g + 1)
    down_group(NG - 1)

    gp = lambda c: "g_ps%d" % c
    xb = lambda kt: "X%d" % kt
    hov = ho_d.rearrange("(kt p) t -> p kt t", p=128)
    if final:
        emit_rmsnorm(P, X, xb, gn_d, X, xb, ones_f, "nn", ss_ps=g_ps, ssbuf=gp)
        for kt in range(KT):
            P.dma("sp", hov[:, kt, :], X[:, kt, :], r=["X%d" % kt], w=["ho"])
    else:
        xov = xo_d.rearrange("(kt p) t -> p kt t", p=128)
        for kt in range(KT):
            P.dma("sp", xov[:, kt, :], X[:, kt, :], r=["X%d" % kt], w=["xo"])
        emit_rmsnorm(P, X, xb, gn_d, H, lambda kt: "H%d" % kt, ones_f, "nn", ss_ps=g_ps, ssbuf=gp)
        for kt in range(KT):
            P.dma("sp", hov[:, kt, :], H[:, kt, :], r=["H%d" % kt], w=["ho"])
    P.finish()
    return nc


HD = 128
SCALE = HD ** -0.5
TWO_PI = float(2.0 * np.pi)
CW1 = 6.28125
CW2 = float(2.0 * np.pi - 6.28125)


def rope_consts():
    half = 16
    inv = np.power(np.float32(500000.0), -np.arange(half, dtype=np.float32) * np.float32(2.0 / 32)).astype(np.float32)
    c = np.zeros((32, 3), np.float32)
    c[:, 0] = np.concatenate([inv, inv])
    c[:16, 1] = -1.0
    c[16:, 1] = 1.0
    pm = np.zeros((32, 32), np.float32)
    for e2 in range(32):
        pm[(e2 + 16) % 32, e2] = 1.0
    return c, pm


def emit_rope_tables(P, pos_d, rc_d, tag="rp"):
    nc = P.nc
    posi = P.sb("posi", [32, T], I32)
    P.dma("sp", posi[:], pos_d.partition_broadcast(32) if hasattr(pos_d, "partition_broadcast") else
          bass.AP(pos_d.tensor, pos_d.offset, [[0, 32], [1, T]]), w=["posi"])
    RC = P.sb("RC", [32, 3], F32)
    P.dma("sp", RC[:], rc_d, w=["RC"])
    posf = P.sb("posf", [32, T], F32)
    P.op("dve", lambda e: e.tensor_copy(out=posf[:], in_=posi[:]), r=["posi"], w=["posf"])
    ang = P.sb("ang", [32, T], F32)
    P.op("dve", lambda e: e.tensor_scalar(out=ang[:], in0=posf[:], scalar1=RC[:, 0:1], scalar2=None, op0=ALU.mult),
         r=["posf", "RC"], w=["ang"])
    tabs = {}
    ni = P.sb("rp_ni", [32, T], I32)
    nf = P.sb("rp_nf", [32, T], F32)
    rr = P.sb("rp_r", [32, T], F32)
    mm = P.sb("rp_m", [32, T], F32)
    for name in ("sin", "cos"):
        if name == "sin":
            P.op("dve", lambda e: e.tensor_scalar(out=nf[:], in0=ang[:], scalar1=1.0 / TWO_PI, scalar2=None,
                                                  op0=ALU.mult), r=["ang"], w=["rp_nf"])
            P.op("dve", lambda e: e.tensor_copy(out=ni[:], in_=nf[:]), r=["rp_nf"], w=["rp_ni"])
            P.op("dve", lambda e: e.tensor_copy(out=nf[:], in_=ni[:]), r=["rp_ni"], w=["rp_nf"])
            P.op("dve", lambda e: e.scalar_tensor_tensor(out=rr[:], in0=nf[:], scalar=-CW1, in1=ang[:],
                                                         op0=ALU.mult, op1=ALU.add), r=["rp_nf", "ang"], w=["rp_r"])
            P.op("dve", lambda e: e.scalar_tensor_tensor(out=rr[:], in0=nf[:], scalar=-CW2, in1=rr[:],
                                                         op0=ALU.mult, op1=ALU.add), r=["rp_nf", "rp_r"], w=["rp_r"])
        else:
            P.op("dve", lambda e: e.tensor_scalar(out=rr[:], in0=rr[:], scalar1=float(np.pi / 2), scalar2=None,
                                                  op0=ALU.add), r=["rp_r"], w=["rp_r"])
        P.op("dve", lambda e: e.tensor_scalar(out=mm[:], in0=rr[:], scalar1=float(np.pi), scalar2=-TWO_PI,
                                              op0=ALU.is_gt, op1=ALU.mult), r=["rp_r"], w=["rp_m"])
        P.op("dve", lambda e: e.tensor_tensor(out=rr[:], in0=rr[:], in1=mm[:], op=ALU.add),
             r=["rp_r", "rp_m"], w=["rp_r"])
        P.op("dve", lambda e: e.tensor_scalar(out=mm[:], in0=rr[:], scalar1=float(-np.pi), scalar2=TWO_PI,
                                              op0=ALU.is_lt, op1=ALU.mult), r=["rp_r"], w=["rp_m"])
        P.op("dve", lambda e: e.tensor_tensor(out=rr[:], in0=rr[:], in1=mm[:], op=ALU.add),
             r=["rp_r", "rp_m"], w=["rp_r"])
        P.op("dve", lambda e: e.tensor_scalar(out=rr[:], in0=rr[:], scalar1=3.1415925, scalar2=-3.1415925,
                                              op0=ALU.min, op1=ALU.max), r=["rp_r"], w=["rp_r"])
        tk = P.sb("tab_" + name + "k", [32, T], F32)
        tq = P.sb("tab_" + name + "q", [32, T], F32)
        P.op("act", lambda e, tk=tk: e.activation(out=tk[:], in_=rr[:], func=AF.Sin), r=["rp_r"], w=["tab_" + name + "k"])
        if name == "sin":
            P.op("dve", lambda e, tk=tk: e.tensor_scalar(out=tk[:], in0=tk[:], scalar1=RC[:, 1:2], scalar2=None,
                                                         op0=ALU.mult), r=["tab_sink", "RC"], w=["tab_sink"])
        P.op("dve", lambda e, tk=tk, tq=tq: e.tensor_scalar(out=tq[:], in0=tk[:], scalar1=SCALE, scalar2=None,
                                                            op0=ALU.mult), r=["tab_" + name + "k"], w=["tab_" + name + "q"])
        tabs[name + "k"] = tk
        tabs[name + "q"] = tq
    return tabs


def build_p1(even):
    import os
    nc = _new_nc()
    NIN = 5120 if even else 6160
    NH = 8 if even else 16
    DA = NH * HD
    hT_d = dram_in(nc, "hT", [D, T], BF16)
    win_d = dram_in(nc, "w_in", [D, NIN], F32)
    qT_d = dram_out(nc, "qT", [DA, T], BF16)
    kT_d = dram_out(nc, "kT", [DA, T], BF16)
    V_d = dram_out(nc, "V", [T, DA], BF16)
    if even:
        pos_d = dram_in(nc, "pos", [1, T], I32)
        rc_d = dram_in(nc, "rope_c", [32, 3], F32)
        pm_d = dram_in(nc, "rope_pm", [32, 32], F32)
        glu_d = dram_out(nc, "gluT", [1024, T], F32)
    else:
        bf_d = dram_in(nc, "b_f", [16, 1], F32)
        lf_d = dram_out(nc, "lf", [16, T], F32)

    P = Prog(nc)
    H = P.sb("H", [128, KT, T], BF16)
    hv = hT_d.rearrange("(kt p) t -> p kt t", p=128)
    for kt in range(KT):
        P.dma("sp", H[:, kt, :], hv[:, kt, :], w=["H%d" % kt])
    hbufs = ["H%d" % kt for kt in range(KT)]
    winv = win_d.rearrange("(kt p) n -> p kt n", p=128)

    NWB = 3
    W = [P.sb("W%d" % i, [128, KT, 512], BF16) for i in range(NWB)]
    wcount = [0]

    def load_cols(c0, ncol):
        b = wcount[0] % NWB
        wcount[0] += 1
        P.dma("pool", W[b][:, :, 0:ncol], winv[:, :, c0:c0 + ncol], w=["W%d" % b])
        return b

    ps = [P.ps("ps%d" % i, [128, T]) for i in range(3)]
    pcount = [0]
    st = [P.sb("st%d" % i, [128, T], BF16) for i in range(3)]
    scount = [0]

    if even:
        tabs = emit_rope_tables(P, pos_d, rc_d)
        Pm = P.sb("Pm", [32, 32], F32)
        P.dma("sp", Pm[:], pm_d, w=["Pm"])
        qs32 = [P.sb("qs32_%d" % i, [32, T], F32) for i in range(2)]
        t1 = [P.sb("rt1_%d" % i, [32, T], F32) for i in range(2)]
        t2 = [P.sb("rt2_%d" % i, [32, T], F32) for i in range(2)]
        sw_ps = P.ps("sw_ps", [32, T])
        rcount = [0]

    def feat_tile(b, i):
        pi = pcount[0] % 3
        pcount[0] += 1
        for c in range(2):
            cs = slice(c * 512, (c + 1) * 512)
            P.mm_group([lambda pe, s_, e_, kt=kt, cs=cs, b=b, i=i, pi=pi: pe.matmul(
                ps[pi][:, cs], lhsT=W[b][:, kt, i * 128:(i + 1) * 128], rhs=H[:, kt, cs], start=s_, stop=e_)
                for kt in range(KT)], r=hbufs + ["W%d" % b], w=["ps%d_%d" % (pi, c)])
        return pi

    def psb(pi):
        return ["ps%d_0" % pi, "ps%d_1" % pi]

    for which, out_d in (("q", qT_d), ("k", kT_d)):
        col0 = 0 if which == "q" else DA
        for g in range(DA // 512):
            b = load_cols(col0 + g * 512, 512)
            for i in range(4):
                h = g * 4 + i
                pi = feat_tile(b, i)
                si = scount[0] % 3
                scount[0] += 1
                sc = SCALE if which == "q" else 1.0
                if even and os.environ.get("K_SKIP2", "") != "rope":
                    ri = rcount[0] % 2
                    rcount[0] += 1
                    P.op("dve", lambda e, pi=pi, ri=ri: e.tensor_copy(out=qs32[ri][:], in_=ps[pi][0:32, :]),
                         r=psb(pi), w=["qs32_%d" % ri])
                    for c in range(2):
                        cs = slice(c * 512, (c + 1) * 512)
                        P.mm_group([lambda pe, s_, e_, cs=cs, ri=ri: pe.matmul(
                            sw_ps[:, cs], lhsT=Pm[:], rhs=qs32[ri][:, cs], start=True, stop=True)],
                            r=["qs32_%d" % ri, "Pm"], w=["sw_ps%d" % c])
                    ct = tabs["cos" + which]
                    sn = tabs["sin" + which]
                    P.op("dve", lambda e, ri=ri, ct=ct: e.tensor_tensor(out=t1[ri][:], in0=qs32[ri][:], in1=ct[:], op=ALU.mult),
                         r=["qs32_%d" % ri, "tab_cos" + which], w=["rt1_%d" % ri])
                    P.op("dve", lambda e, ri=ri, sn=sn: e.tensor_tensor(out=t2[ri][:], in0=sw_ps[:], in1=sn[:], op=ALU.mult),
                         r=["sw_ps0", "sw_ps1", "tab_sin" + which], w=["rt2_%d" % ri])
                    P.op("act", lambda e, pi=pi, si=si, sc=sc: e.activation(out=st[si][:], in_=ps[pi][:],
                                                                            func=AF.Copy, scale=sc),
                         r=psb(pi), w=["st%d_lo" % si, "st%d_hi" % si])
                    P.op("dve", lambda e, ri=ri, si=si: e.tensor_tensor(out=st[si][0:32, :], in0=t1[ri][:], in1=t2[ri][:], op=ALU.add),
                         r=["rt1_%d" % ri, "rt2_%d" % ri], w=["st%d_lo" % si])
                    P.dma("sp", out_d[h * 128:(h + 1) * 128, :], st[si][:], r=["st%d_lo" % si, "st%d_hi" % si],
                          w=[which + "T_out"])
                else:
                    P.op("act", lambda e, pi=pi, si=si, sc=sc: e.activation(out=st[si][:], in_=ps[pi][:],
                                                                            func=AF.Copy, scale=sc),
                         r=psb(pi), w=["st%d_lo" % si, "st%d_hi" % si])
                    P.dma("sp", out_d[h * 128:(h + 1) * 128, :], st[si][:], r=["st%d_lo" % si, "st%d_hi" % si],
                          w=[which + "T_out"])

    vst = [P.sb("vst%d" % i, [128, 512], BF16) for i in range(3)]
    vcount = [0]
    for g in range(DA // 512):
        b = load_cols(2 * DA + g * 512, 512)
        for tt in range(T // 128):
            pi = pcount[0] % 3
            pcount[0] += 1
            P.mm_group([lambda pe, s_, e_, kt=kt, b=b, tt=tt, pi=pi: pe.matmul(
                ps[pi][:, 0:512], lhsT=H[:, kt, tt * 128:(tt + 1) * 128], rhs=W[b][:, kt, :], start=s_, stop=e_)
                for kt in range(KT)], r=hbufs + ["W%d" % b], w=["ps%d_0" % pi])
            vi = vcount[0] % 3
            vcount[0] += 1
            P.op("dve", lambda e, pi=pi, vi=vi: e.tensor_copy(out=vst[vi][:], in_=ps[pi][:, 0:512]),
                 r=["ps%d_0" % pi], w=["vst%d" % vi])
            P.dma("sp", V_d[tt * 128:(tt + 1) * 128, g * 512:(g + 1) * 512], vst[vi][:], r=["vst%d" % vi], w=["V_out"])

    if even and os.environ.get("K_SKIP", "") == "glu":
        pass
    elif even:
        sg = [P.sb("sg%d" % i, [128, T], F32) for i in range(2)]
        gl = [P.sb("gl%d" % i, [128, T], F32) for i in range(2)]
        for g in range(2):
            ba = load_cols(3 * DA + g * 512, 512)
            bg = load_cols(3 * DA + 1024 + g * 512, 512)
            for i in range(4):
                ct_ = g * 4 + i
                pa = feat_tile(ba, i)
                pg = feat_tile(bg, i)
                s = ct_ % 2
                P.op("act", lambda e, pg=pg, s=s: e.activation(out=sg[s][:], in_=ps[pg][:], func=AF.Sigmoid),
                     r=psb(pg), w=["sg%d" % s])
                P.op("dve", lambda e, pa=pa, s=s: e.tensor_tensor(out=gl[s][:], in0=ps[pa][:], in1=sg[s][:], op=ALU.mult),
                     r=psb(pa) + ["sg%d" % s], w=["gl%d" % s])
                P.dma("sp", glu_d[ct_ * 128:(ct_ + 1) * 128, :], gl[s][:], r=["gl%d" % s], w=["glu_out"])
    else:
        b = load_cols(3 * DA, 16)
        BFt = P.sb("BFt", [16, 1], F32)
        P.dma("sp", BFt[:], bf_d, w=["BFt"])
        NB = P.sb("NB", [16, 1], F32)
        P.op("dve", lambda e: e.tensor_scalar(out=NB[:], in0=BFt[:], scalar1=-1.0, scalar2=None, op0=ALU.mult),
             r=["BFt"], w=["NB"])
        pi = pcount[0] % 3
        pcount[0] += 1
        for c in range(2):
            cs = slice(c * 512, (c + 1) * 512)
            P.mm_group([lambda pe, s_, e_, kt=kt, cs=cs, b=b, pi=pi: pe.matmul(
                ps[pi][0:16, cs], lhsT=W[b][:, kt, 0:16], rhs=H[:, kt, cs], start=s_, stop=e_)
                for kt in range(KT)], r=hbufs + ["W%d" % b], w=["ps%d_%d" % (pi, c)])
        e1 = P.sb("e1", [16, T], F32)
        l1 = P.sb("l1", [16, T], F32)
        P.op("act", lambda e, pi=pi: e.activation(out=e1[:], in_=ps[pi][0:16, :], func=AF.Exp, scale=-1.0, bias=NB[:]),
             r=psb(pi) + ["NB"], w=["e1"])
        P.op("act", lambda e: e.activation(out=l1[:], in_=e1[:], func=AF.Ln, bias=1.0), r=["e1"], w=["l1"])
        P.op("dve", lambda e: e.tensor_scalar(out=l1[:], in0=l1[:], scalar1=-1.0, scalar2=None, op0=ALU.mult),
             r=["l1"], w=["l1"])
        P.dma("sp", lf_d, l1[:], r=["l1"], w=["lf_out"])
    P.finish()
    return nc


def emit_outproj_norm(P, OT, xT_d, wout_d, gain_d, xo_d, h2_d, y_ps, ybuf, ones_f, X=None, halo=None):
    if X is None:
        X = P.sb("X", [128, KT, T], F32)
    xv = xT_d.rearrange("(kt p) t -> p kt t", p=128)
    for kt in range(KT):
        P.dma("sp", X[:, kt, :], xv[:, kt, :], w=["X%d" % kt])
    wov = wout_d.rearrange("(kt p) n -> p kt n", p=128)
    Wo = [P.sb("Wo%d" % i, [128, KT, 256], BF16) for i in range(2)]
    otb = ["OT%d" % kt for kt in range(KT)]
    for g in range(D // 256):
        b = g % 2
        P.dma("pool", Wo[b][:], wov[:, :, g * 256:(g + 1) * 256], w=["Wo%d" % b])
        for i in range(2):
            nt = g * 2 + i
            for c in range(2):
                cs = slice(c * 512, (c + 1) * 512)
                P.mm_group([lambda pe, s_, e_, kt=kt, cs=cs, b=b, i=i: pe.matmul(
                    y_ps[:, cs], lhsT=Wo[b][:, kt, i * 128:(i + 1) * 128], rhs=OT[:, kt, cs], start=s_, stop=e_)
                    for kt in range(KT)], r=otb + ["Wo%d" % b], w=[ybuf(c)])
                P.op("dve", lambda e, nt=nt, cs=cs: e.tensor_tensor(
                    out=X[:, nt, cs], in0=y_ps[:, cs], in1=X[:, nt, cs], op=ALU.add),
                    r=[ybuf(c), "X%d" % nt], w=["X%d" % nt])
    xov = xo_d.rearrange("(kt p) t -> p kt t", p=128)
    for kt in range(KT):
        P.dma("sp", xov[:, kt, :], X[:, kt, :], r=["X%d" % kt], w=["xo"])
    emit_rmsnorm(P, X, lambda kt: "X%d" % kt, gain_d, OT, lambda kt: "OT%d" % kt, ones_f, "n2",
                 ss_ps=y_ps, ssbuf=ybuf)
    hov = h2_d.rearrange("(kt p) t -> p kt t", p=128)
    for kt in range(KT):
        P.dma("sp", hov[:, kt, :], OT[:, kt, :], r=["OT%d" % kt], w=["ho"])
    if halo is not None:
        hl_d, hlg_d = halo
        P.dma("sp", hl_d.rearrange("(kt p) t -> p kt t", p=128), OT[:, :, T - 2:T],
              r=["OT%d" % kt for kt in range(KT)], w=["hl"])
        P.coll(hl_d, hlg_d, r=["hl"], w=["hlg"])


def emit_attention(P, nheads, qT_d, ksrc, vsrc, OT, ones_b, kbias_d, tri_d, fox=None, mask=None, per_head=None,
                   pre_head=None):
    nc = P.nc
    T2 = 2 * T
    NKT = T2 // 128
    tri = P.sb("tri", [128, 128], BF16)
    P.dma("sp", tri[:], tri_d, w=["tri"])
    if fox is not None:
        ident = P.sb("ident", [128, 128], BF16)
        P.dma("sp", ident[:], fox["ident"], w=["ident"])
    KB = P.sb("KB", [128, NKT], F32)
    P.dma("sp", KB[:], kbias_d, w=["KB"])
    kTh = [P.sb("kTh%d" % i, [128, T2], BF16) for i in range(2)]
    Vh = [P.sb("Vh%d" % i, [128, NKT, 128], BF16) for i in range(2)]
    qh = [P.sb("qh%d" % i, [128, T], BF16) for i in range(2)]
    if fox is not None:
        qaug = [P.sb("qaug%d" % i, [6, T], BF16) for i in range(2)]
        kaug = [P.sb("kaug%d" % i, [6, T2], BF16) for i in range(2)]
        for i in range(2):
            P.op("dve", lambda e, i=i: e.memset(qaug[i][:], 1.0), w=["qaug%d" % i])
            P.op("dve", lambda e, i=i: e.memset(kaug[i][:], 1.0), w=["kaug%d" % i])
    NST, NPT = 3, 4
    PT = [P.sb("PT%d" % i, [128, 512], BF16) for i in range(NPT)]
    rec = [P.sb("rec%d" % i, [128, 512], F32) for i in range(2)]
    st_ps = [P.ps("st_ps%d" % i, [128, 512]) for i in range(NST)]
    o_ps = [P.ps("o_ps%d" % i, [128, 512]) for i in range(2)]
    d_ps = [P.ps("d_ps%d" % i, [128, 512]) for i in range(1)]
    cnt = {"s": 0, "p": 0, "o": 0}
    for h in range(nheads):
        hs = h % 2
        for (csl, src) in ksrc(h):
            P.dma("sp", kTh[hs][:, csl], src, w=["kTh%d" % hs])
        for (ksl, src) in vsrc(h):
            P.dma("sp", Vh[hs][:, ksl, :], src, w=["Vh%d" % hs])
        P.dma("sp", qh[hs][:], qT_d[h * 128:(h + 1) * 128, :], w=["qh%d" % hs])
        if pre_head is not None:
            pre_head(h)
        rd = ["kTh%d" % hs, "qh%d" % hs]
        if fox is not None:
            FS = fox["FS"]
            P.dma("sp", qaug[hs][0:3, :], FS[0:3, h, T:T2], r=["FS"], w=["qaug%d" % hs])
            P.dma("sp", kaug[hs][3:6, :], FS[3:6, h, :], r=["FS"], w=["kaug%d" % hs])
            rd = rd + ["qaug%d" % hs, "kaug%d" % hs, "ident", "tri"]
        for qc in range(2):
            oi = cnt["o"] % 2
            cnt["o"] += 1
            nk = 8 + 4 * qc + 4
            q0 = qc * 512

            def s_mm(kt, hs=hs, qc=qc, q0=q0, rd=rd):
                si = cnt["s"] % NST
                cnt["s"] += 1
                c_lo = max(0, kt - 8 - 4 * qc) * 128
                mms = [lambda pe, s_, e_, kt=kt, c_lo=c_lo, si=si, hs=hs, q0=q0: pe.matmul(
                    st_ps[si][:, c_lo:512], lhsT=kTh[hs][:, kt * 128:(kt + 1) * 128],
                    rhs=qh[hs][:, q0 + c_lo:q0 + 512], start=s_, stop=e_)]
                if fox is not None:
                    mms.append(lambda pe, s_, e_, kt=kt, c_lo=c_lo, si=si, hs=hs, q0=q0: pe.matmul(
                        st_ps[si][:, c_lo:512], lhsT=kaug[hs][:, kt * 128:(kt + 1) * 128],
                        rhs=qaug[hs][:, q0 + c_lo:q0 + 512], start=s_, stop=e_))
                    if kt - 8 - 4 * qc >= 0:
                        mms.append(lambda pe, s_, e_, c_lo=c_lo, si=si: pe.matmul(
                            st_ps[si][:, c_lo:c_lo + 128], lhsT=ident[:], rhs=tri[:], start=s_, stop=e_))
                P.mm_group(mms, r=rd, w=["st_ps%d" % si])
                return si, c_lo

            def rest(kt, si, c_lo, hs=hs, qc=qc, oi=oi, nk=nk):
                pi = cnt["p"] % NPT
                cnt["p"] += 1
                P.op("act", lambda e, si=si, pi=pi, c_lo=c_lo, kt=kt: e.activation(
                    out=PT[pi][:, c_lo:512], in_=st_ps[si][:, c_lo:512], func=AF.Exp, bias=KB[:, kt:kt + 1]),
                    r=["st_ps%d" % si, "KB"], w=["PT%d" % pi])
                j0 = 8 + 4 * qc - kt
                if mask is not None:
                    jlo = j0 + c_lo // 128
                    ncol = 512 - c_lo
                    P.op("dve", lambda e, pi=pi, c_lo=c_lo, jlo=jlo, ncol=ncol: e.tensor_tensor(
                        out=PT[pi][:, c_lo:512], in0=PT[pi][:, c_lo:512],
                        in1=mask[:, jlo * 128:jlo * 128 + ncol], op=ALU.mult),
                        r=["PT%d" % pi, "mask"], w=["PT%d" % pi])
                first, last = (kt == 0), (kt == nk - 1)
                P.mm_group([lambda pe, s_, e_, kt=kt, pi=pi, c_lo=c_lo, oi=oi, hs=hs, first=first, last=last: pe.matmul(
                    o_ps[oi][:, c_lo:512], lhsT=Vh[hs][:, kt, :], rhs=PT[pi][:, c_lo:512],
                    start=first, stop=last, skip_group_check=True)],
                    r=["Vh%d" % hs, "PT%d" % pi], w=["o_ps%d" % oi])
                P.mm_group([lambda pe, s_, e_, kt=kt, pi=pi, c_lo=c_lo, first=first, last=last: pe.matmul(
                    d_ps[0][:, c_lo:512], lhsT=ones_b[:], rhs=PT[pi][:, c_lo:512],
                    start=first, stop=last, skip_group_check=True)],
                    r=["ones_b", "PT%d" % pi], w=["d_ps0"])

            pend = [s_mm(0), s_mm(1)]
            for kt in range(nk):
                if kt + 2 < nk:
                    pend.append(s_mm(kt + 2))
                rest(kt, *pend.pop(0))
            ri = oi
            P.op("dve", lambda e, ri=ri: e.reciprocal(out=rec[ri][:], in_=d_ps[0][:]),
                 r=["d_ps0"], w=["rec%d" % ri])
            P.op("dve", lambda e, ri=ri, oi=oi, h=h, q0=q0: e.tensor_tensor(
                out=OT[:, h, q0:q0 + 512], in0=o_ps[oi][:], in1=rec[ri][:], op=ALU.mult),
                r=["o_ps%d" % oi, "rec%d" % ri], w=["OT%d" % h])
        if per_head is not None:
            per_head(h)
    return dict(st_ps=st_ps, o_ps=o_ps, d_ps=d_ps)


def kv_sources(kT_d, V_d, Kg, Vg, vrows):
    Vv = V_d.rearrange("(kt p) (hh e) -> p kt hh e", p=128, e=128)
    nvt = vrows // 128

    def ksrc(h):
        j, i = (h * 128) // 1024, (h * 128) % 1024
        return [(slice(0, T), Kg[j][i:i + 128, :]), (slice(T, 2 * T), kT_d[h * 128:(h + 1) * 128, :])]

    def vsrc(h):
        out = []
        for j, g in enumerate(Vg):
            gv = g[0:vrows, :].rearrange("(kt p) (hh e) -> p kt hh e", p=128, e=128)
            out.append((slice(j * nvt, (j + 1) * nvt), gv[:, :, h, :]))
        out.append((slice(8, 16), Vv[:, :, h, :]))
        return out
    return ksrc, vsrc


def build_p2_odd():
    nc = _new_nc()
    T2 = 2 * T
    xT_d = dram_in(nc, "xT", [D, T], F32)
    qT_d = dram_in(nc, "qT", [D, T], BF16)
    kT_d = dram_in(nc, "kT", [D, T2], BF16)
    V_d = dram_in(nc, "V", [T2, D], BF16)
    lf_d = dram_in(nc, "lf", [16, T2], F32)
    kvalid_d = dram_in(nc, "kvalid", [128, 16], F32)
    tri_d = dram_in(nc, "negtri", [128, 128], BF16)
    ident_d = dram_in(nc, "ident", [128, 128], BF16)
    wout_d = dram_in(nc, "w_out", [D, D], F32)
    gain_d = dram_in(nc, "gain", [D], F32)
    xo_d = dram_out(nc, "xT_out", [D, T], F32)
    h2_d = dram_out(nc, "h2T", [D, T], BF16)
    FS = nc.dram_tensor("FS", [6, 16, T2], BF16).ap()

    P = Prog(nc)
    ones_f, ones_b = emit_consts(P)
    A = P.sb("fA", [16, T], F32)
    B = P.sb("fB", [16, T], F32)
    C = P.sb("fC", [16, T], F32)
    carry = P.sb("fcarry", [16, 1], F32)
    P.op("dve", lambda e: e.memset(carry[:], 0.0), w=["fcarry"])
    parts = [P.sb("fp%d" % i, [16, T], BF16) for i in range(6)]
    for half in range(2):
        hsl = slice(half * T, (half + 1) * T)
        P.dma("sp", A[:], lf_d[:, hsl], w=["fA"])
        P.op("dve", lambda e: e.memset(B[:], 1.0), w=["fB"])
        P.op("dve", lambda e: e.tensor_tensor_scan(out=C[:], data0=B[:], data1=A[:], initial=carry[:],
                                                   op0=ALU.mult, op1=ALU.add),
             r=["fA", "fB", "fcarry"], w=["fC"])
        P.op("dve", lambda e: e.tensor_copy(out=carry[:], in_=C[:, T - 1:T]), r=["fC"], w=["fcarry"])
        P.op("dve", lambda e: e.tensor_copy(out=parts[0][:], in_=C[:]), r=["fC"], w=["fp0"])
        P.op("dve", lambda e: e.tensor_copy(out=A[:], in_=parts[0][:]), r=["fp0"], w=["fA"])
        P.op("dve", lambda e: e.tensor_tensor(out=B[:], in0=C[:], in1=A[:], op=ALU.subtract), r=["fC", "fA"], w=["fB"])
        P.op("dve", lambda e: e.tensor_copy(out=parts[1][:], in_=B[:]), r=["fB"], w=["fp1"])
        P.op("dve", lambda e: e.tensor_copy(out=A[:], in_=parts[1][:]), r=["fp1"], w=["fA"])
        P.op("dve", lambda e: e.tensor_tensor(out=C[:], in0=B[:], in1=A[:], op=ALU.subtract), r=["fB", "fA"], w=["fC"])
        P.op("dve", lambda e: e.tensor_copy(out=parts[2][:], in_=C[:]), r=["fC"], w=["fp2"])
        for i in range(3):
            P.op("dve", lambda e, i=i: e.tensor_scalar(out=parts[3 + i][:], in0=parts[i][:], scalar1=-1.0,
                                                       scalar2=None, op0=ALU.mult), r=["fp%d" % i], w=["fp%d" % (3 + i)])
        for i in range(6):
            P.dma("sp", FS[i, :, hsl], parts[i][:], r=["fp%d" % i], w=["FS"])

    OT = P.sb("OT", [128, KT, T], BF16)
    y_ps = P.ps("y_ps", [128, T])
    emit_attention(P, 16, qT_d, kT_d, V_d, OT, ones_b, kvalid_d, tri_d, None, fox=dict(FS=FS, ident=ident_d))
    emit_outproj_norm(P, OT, xT_d, wout_d, gain_d, xo_d, h2_d, y_ps, lambda c: "y_ps%d" % c, ones_f)
    P.finish()
    return nc


def mask_strip():
    k = np.arange(128)[:, None]
    cols = np.arange(16 * 128)[None, :]
    dl = cols - k
    m = ((dl >= 0) & (dl <= 128)).astype(np.float32)
    m += ((dl >= 0) & (dl % 4 == 0) & (dl <= 512))
    m += ((dl >= 0) & (dl % 16 == 0) & (dl <= 2048))
    return m


def build_p2_even():
    nc = _new_nc()
    T2 = 2 * T
    DA = 1024
    xT_d = dram_in(nc, "xT", [D, T], F32)
    qT_d = dram_in(nc, "qT", [DA, T], BF16)
    kT_d = dram_in(nc, "kT", [DA, T2], BF16)
    V_d = dram_in(nc, "V", [T2, DA], BF16)
    glu_d = dram_in(nc, "glu", [1024, T], F32)
    gh_d = dram_in(nc, "glu_halo", [1024, 30], F32)
    cw_d = dram_in(nc, "conv_w", [31, 1024], F32)
    cb_d = dram_in(nc, "conv_b", [1024], F32)
    lg_d = dram_in(nc, "ln_g", [1024], F32)
    lb_d = dram_in(nc, "ln_b", [1024], F32)
    kvalid_d = dram_in(nc, "kvalid", [128, 16], F32)
    mask_d = dram_in(nc, "mask", [128, 2048], BF16)
    tri_d = dram_in(nc, "tri", [128, 128], BF16)
    wout_d = dram_in(nc, "w_out", [D, D], F32)
    gain_d = dram_in(nc, "gain", [D], F32)
    xo_d = dram_out(nc, "xT_out", [D, T], F32)
    h2_d = dram_out(nc, "h2T", [D, T], BF16)

    P = Prog(nc)
    ones_f, ones_b = emit_consts(P)
    X = P.sb("X", [128, KT, T], F32)
    OT = P.sb("OT", [128, KT, T], BF16)
    y_ps = P.ps("y_ps", [128, T])
    mask = P.sb("mask", [128, 2048], BF16)
    P.dma("sp", mask[:], mask_d, w=["mask"])
    CWt = P.sb("CWt", [128, 31, 8], F32)
    for k in range(31):
        P.dma("sp", CWt[:, k, :], cw_d[k, :].rearrange("(ct p) -> p ct", p=128), w=["CWt"],
              allow_slow_non_contiguous=True)
    CP = P.sb("CP", [128, 3, 8], F32)
    for j, dd in enumerate((cb_d, lg_d, lb_d)):
        P.dma("sp", CP[:, j, :], dd.rearrange("(ct p) -> p ct", p=128), w=["CP"], allow_slow_non_contiguous=True)
    Gt = [P.sb("Gt%d" % i, [128, T + 30], F32) for i in range(2)]

    def conv_tile(ct):
        eng = "dve"
        gi = ct % 2
        P.dma("sp", Gt[gi][:, 0:30], gh_d[ct * 128:(ct + 1) * 128, :], w=["Gt%d_h" % gi])
        P.dma("sp", Gt[gi][:, 30:T + 30], glu_d[ct * 128:(ct + 1) * 128, :], w=["Gt%d" % gi])
        gb = ["Gt%d_h" % gi, "Gt%d" % gi]
        ab = "X%d" % (8 + ct)
        P.op(eng, lambda e, gi=gi, ct=ct: e.tensor_scalar(
            out=X[:, 8 + ct, :], in0=Gt[gi][:, 0:T], scalar1=CWt[:, 0, ct:ct + 1], scalar2=CP[:, 0, ct:ct + 1],
            op0=ALU.mult, op1=ALU.add), r=gb + ["CWt", "CP"], w=[ab])
        for k in range(1, 31):
            P.op(eng, lambda e, gi=gi, ct=ct, k=k: e.scalar_tensor_tensor(
                out=X[:, 8 + ct, :], in0=Gt[gi][:, k:k + T], scalar=CWt[:, k, ct:ct + 1], in1=X[:, 8 + ct, :],
                op0=ALU.mult, op1=ALU.add), r=gb + ["CWt", ab], w=[ab])

    ps = emit_attention(P, 8, qT_d, kT_d, V_d, OT, ones_b, kvalid_d, tri_d, None, mask=mask, per_head=conv_tile, pre_head=conv_prep)
    st_ps = ps["st_ps"]
    sqb = [P.sb("lsq%d" % i, [128, 512], F32) for i in range(2)]
    i = 0
    for c in range(2):
        cs = slice(c * 512, (c + 1) * 512)
        for ct in range(8):
            s_ = i % 2
            i += 1
            P.op("act", lambda e, ct=ct, s_=s_, cs=cs: e.activation(out=sqb[s_][:], in_=X[:, 8 + ct, cs], func=AF.Square),
                 r=["X%d" % (8 + ct)], w=["lsq%d" % s_])
            P.mm_group([lambda pe, a_, b_, ct=ct, cs=cs: pe.matmul(
                y_ps[:, cs], lhsT=ones_f[:], rhs=X[:, 8 + ct, cs], start=(ct == 0), stop=(ct == 7),
                skip_group_check=True)], r=["X%d" % (8 + ct), "ones_f"], w=["y_ps%d" % c])
            P.mm_group([lambda pe, a_, b_, ct=ct, s_=s_, c=c: pe.matmul(
                st_ps[c][:], lhsT=ones_f[:], rhs=sqb[s_][:], start=(ct == 0), stop=(ct == 7),
                skip_group_check=True)], r=["lsq%d" % s_, "ones_f"], w=["st_ps%d" % c])
    mean = P.sb("ln_mean", [128, T], F32)
    var = P.sb("ln_var", [128, T], F32)
    for c in range(2):
        cs = slice(c * 512, (c + 1) * 512)
        P.op("dve", lambda e, cs=cs: e.tensor_scalar(out=mean[:, cs], in0=y_ps[:, cs], scalar1=1.0 / 1024, scalar2=None,
                                                     op0=ALU.mult), r=["y_ps%d" % c], w=["ln_mean%d" % c])
        P.op("dve", lambda e, cs=cs: e.tensor_tensor(out=var[:, cs], in0=mean[:, cs], in1=mean[:, cs], op=ALU.mult),
             r=["ln_mean%d" % c], w=["ln_var%d" % c])
        P.op("dve", lambda e, cs=cs, c=c: e.scalar_tensor_tensor(
            out=var[:, cs], in0=st_ps[c][:], scalar=1.0 / 1024, in1=var[:, cs], op0=ALU.mult, op1=ALU.subtract),
            r=["st_ps%d" % c, "ln_var%d" % c], w=["ln_var%d" % c])
        P.op("act", lambda e, cs=cs: e.activation(out=var[:, cs], in_=var[:, cs], func=AF.Sqrt, bias=EPS),
             r=["ln_var%d" % c], w=["ln_var%d" % c])
        P.op("dve", lambda e, cs=cs: e.reciprocal(out=var[:, cs], in_=var[:, cs]),
             r=["ln_var%d" % c], w=["ln_var%d" % c])
    lt = [P.sb("lt%d" % i, [128, T], F32) for i in range(2)]
    stat = ["ln_mean0", "ln_mean1", "ln_var0", "ln_var1"]
    for ct in range(8):
        li = ct % 2
        P.op("dve", lambda e, ct=ct, li=li: e.tensor_tensor(out=lt[li][:], in0=X[:, 8 + ct, :], in1=mean[:], op=ALU.subtract),
             r=["X%d" % (8 + ct)] + stat, w=["lt%d" % li])
        P.op("dve", lambda e, li=li: e.tensor_tensor(out=lt[li][:], in0=lt[li][:], in1=var[:], op=ALU.mult),
             r=["lt%d" % li] + stat, w=["lt%d" % li])
        P.op("act", lambda e, ct=ct, li=li: e.activation(out=OT[:, 8 + ct, :], in_=lt[li][:], func=AF.Silu,
                                                         scale=CP[:, 1, ct:ct + 1], bias=CP[:, 2, ct:ct + 1]),
             r=["lt%d" % li, "CP"], w=["OT%d" % (8 + ct)])
    emit_outproj_norm(P, OT, xT_d, wout_d, gain_d, xo_d, h2_d, y_ps, lambda c: "y_ps%d" % c, ones_f, X=X)
    P.finish()
    return nc


def build_fused(upto=None):
    nc = bass.Bass("TRN2", target_bir_lowering=False)
    itn = lambda name, shape, dt: nc.dram_tensor(name, list(shape), dt).ap()
    specs = dict(
        xT=([D, T], F32), pos=([1, T], I32), kbias=([128, 16], F32), hmul=([128, 1], F32),
        rope_c=([32, 3], F32), rope_pm=([32, 32], F32), negtri=([128, 128], BF16), tri=([128, 128], BF16),
        ident=([128, 128], BF16), mask=([128, 2048], BF16),
        norm_mix=([4, D], F32), norm_ffn=([4, D], F32), norm_final=([D], F32))
    for e_ in range(2):
        specs.update({"ev_w_in_%d" % e_: ([D, 5120], F32), "ev_conv_w_%d" % e_: ([31, 1024], F32),
                      "ev_conv_b_%d" % e_: ([1024], F32), "ev_ln_g_%d" % e_: ([1024], F32),
                      "ev_ln_b_%d" % e_: ([1024], F32), "ev_w_out_%d" % e_: ([D, D], F32),
                      "od_w_in_%d" % e_: ([D, 6160], F32), "od_b_f_%d" % e_: ([16], F32),
                      "od_w_out_%d" % e_: ([D, D], F32)})
    for l_ in range(4):
        specs.update({"ffn_w_up_%d" % l_: ([D, 2 * DFF], F32), "ffn_conv_w_%d" % l_: ([3, DFF], F32),
                      "ffn_w_down_%d" % l_: ([DFF, D], F32)})

    class _Lazy(dict):
        def __missing__(self, name):
            shape, dt = specs[name]
            ap = nc.dram_tensor(name, list(shape), dt, kind="ExternalInput").ap()
            self[name] = ap
            return ap
    E = _Lazy()
    nc._ext_names = E
    out_d = nc.dram_tensor("out", [D, T], F32, kind="ExternalOutput").ap()
    XM, XN = itn("XM", [D, T], F32), itn("XN", [D, T], F32)
    HT, H2 = itn("HT", [D, T], BF16), itn("H2", [D, T], BF16)
    HL, HLG = itn("HL", [D, 2], BF16), itn("HLG", [2 * D, 2], BF16)
    QTo, KTo, Vo = itn("QTo", [2048, T], BF16), itn("KTo", [2048, T], BF16), itn("Vo", [T, 2048], BF16)
    QTe, KTe, Ve = itn("QTe", [1024, T], BF16), itn("KTe", [1024, T], BF16), itn("Ve", [T, 1024], BF16)
    KG = [itn("KG%d" % j, [2048, T], BF16) for j in range(2)]
    VGo = [itn("VGo%d" % j, [1024, 2048], BF16) for j in range(2)]
    VGe = itn("VGe", [2048, 1024], BF16)
    LF, LFG = itn("LF", [16, T], F32), itn("LFG", [32, T], F32)
    GLU, GHL, GHLG = itn("GLU", [1024, T], F32), itn("GHL", [1024, 32], F32), itn("GHLG", [2048, 32], F32)
    FS = itn("FS", [6, 16, 2 * T], BF16)

    def dump(src):
        P = Prog(nc)
        P.dma("sp", out_d[0:src.shape[0], :], src, w=["dump"])
        P.finish()
        return nc

    f_p0(nc, dict(xT=E["xT"], gain=E["norm_mix"][0], hT=HT))
    xcur = E["xT"]
    for l in range(4):
        e = l // 2
        if upto == "p1_%d" % l:
            f_p1(nc, dict(hT=HT, w_in=E["ev_w_in_%d" % e], qT=QTe, kT=KTe, V=Ve, pos=E["pos"], rope_c=E["rope_c"],
                          rope_pm=E["rope_pm"], gluT=GLU, glu_hl=GHL, glu_hlg=GHLG,
                          k_xchg=[(KTe, KG[0])], v_xchg=[(Ve, VGe)]), True) if l % 2 == 0 else \
                f_p1(nc, dict(hT=HT, w_in=E["od_w_in_%d" % e], qT=QTo, kT=KTo, V=Vo,
                              b_f=E["od_b_f_%d" % e].rearrange("(h o) -> h o", o=1), lf=LF, lfg=LFG,
                              k_xchg=[(KTo[0:1024, :], KG[0]), (KTo[1024:2048, :], KG[1])],
                              v_xchg=[(Vo[0:512, :], VGo[0]), (Vo[512:1024, :], VGo[1])]), False)
            return dump(GLU if l % 2 == 0 else XM)
        if l % 2 == 0:
            f_p1(nc, dict(hT=HT, w_in=E["ev_w_in_%d" % e], qT=QTe, kT=KTe, V=Ve, pos=E["pos"], rope_c=E["rope_c"],
                          rope_pm=E["rope_pm"], gluT=GLU, glu_hl=GHL, glu_hlg=GHLG,
                          k_xchg=[(KTe, KG[0])], v_xchg=[(Ve, VGe)]), True)
            f_p2_even(nc, dict(xT=xcur, qT=QTe, kT=KTe, V=Ve, Kg=[KG[0]], Vg=[VGe], gluT=GLU, glu_hlg=GHLG,
                               conv_w=E["ev_conv_w_%d" % e], conv_b=E["ev_conv_b_%d" % e], ln_g=E["ev_ln_g_%d" % e],
                               ln_b=E["ev_ln_b_%d" % e], kbias=E["kbias"], hmul=E["hmul"], mask=E["mask"], tri=E["tri"],
                               ident=E["ident"],
                               w_out=E["ev_w_out_%d" % e], gain=E["norm_ffn"][l], xT_out=XM, h2T=H2, hl=HL, hlg=HLG))
        else:
            f_p1(nc, dict(hT=HT, w_in=E["od_w_in_%d" % e], qT=QTo, kT=KTo, V=Vo,
                          b_f=E["od_b_f_%d" % e].rearrange("(h o) -> h o", o=1), lf=LF, lfg=LFG,
                          k_xchg=[(KTo[0:1024, :], KG[0]), (KTo[1024:2048, :], KG[1])],
                          v_xchg=[(Vo[0:512, :], VGo[0]), (Vo[512:1024, :], VGo[1])]), False)
            f_p2_odd(nc, dict(xT=xcur, qT=QTo, kT=KTo, V=Vo, Kg=KG, Vg=VGo, lf=LF, lfg=LFG, FS=FS,
                              kbias=E["kbias"], negtri=E["negtri"], ident=E["ident"],
                              w_out=E["od_w_out_%d" % e], gain=E["norm_ffn"][l], xT_out=XM, h2T=H2, hl=HL, hlg=HLG))
        if upto == "p2_%d" % l:
            return dump(XM)
        final = (l == 3)
        f_p3(nc, dict(xT=XM, h2T=H2, halo=HLG[0:D, :], hmul=E["hmul"], w_up=E["ffn_w_up_%d" % l],
                      conv_w=E["ffn_conv_w_%d" % l], w_down=E["ffn_w_down_%d" % l],
                      gain_next=(E["norm_final"] if final else E["norm_mix"][l + 1]),
                      xT_out=XN, hT_out=(out_d if final else HT)), final)
        xcur = XN
        if upto == "p3_%d" % l:
            return dump(XN)
    return nc


_NC_CACHE = {}


def kernel(x, positions, norm_mix, norm_ffn, norm_final, ev_w_in, ev_conv_w, ev_conv_b, ev_ln_g, ev_ln_b,
           ev_w_out, od_w_in, od_b_f, od_w_out, ffn_w_up, ffn_conv_w, ffn_w_down):
    import ml_dtypes
    bf = ml_dtypes.bfloat16
    f32 = np.float32
    x = np.asarray(x, f32)
    positions = np.asarray(positions, np.int32)
    A = lambda a: np.ascontiguousarray(np.asarray(a, f32))
    if "fused" not in _NC_CACHE:
        _NC_CACHE["fused"] = build_fused()
    nc = _NC_CACHE["fused"]
    rc, pm = rope_consts()
    shared = dict(
        rope_c=rc, rope_pm=pm,
        negtri=np.where(np.arange(128)[None, :] >= np.arange(128)[:, None], 0.0, -30000.0).astype(bf),
        tri=(np.arange(128)[None, :] >= np.arange(128)[:, None]).astype(bf),
        ident=np.eye(128).astype(bf), mask=mask_strip().astype(bf),
        norm_mix=A(norm_mix), norm_ffn=A(norm_ffn), norm_final=A(norm_final))
    for e_ in range(2):
        for nm, arr in (("ev_w_in", ev_w_in), ("ev_conv_w", ev_conv_w), ("ev_conv_b", ev_conv_b), ("ev_ln_g", ev_ln_g),
                        ("ev_ln_b", ev_ln_b), ("ev_w_out", ev_w_out), ("od_w_in", od_w_in), ("od_b_f", od_b_f),
                        ("od_w_out", od_w_out)):
            shared["%s_%d" % (nm, e_)] = A(np.asarray(arr)[e_])
    for l_ in range(4):
        for nm, arr in (("ffn_w_up", ffn_w_up), ("ffn_conv_w", ffn_conv_w), ("ffn_w_down", ffn_w_down)):
            shared["%s_%d" % (nm, l_)] = A(np.asarray(arr)[l_])
    kb_a = np.concatenate([np.full((128, 8), -30000.0, f32), np.zeros((128, 8), f32)], 1)
    kb_b = np.zeros((128, 16), f32)
    in_maps = []
    for c in range(NCORES):
        b, h = c // 2, c % 2
        m = dict(shared)
        m["xT"] = np.ascontiguousarray(x[b, h * T:(h + 1) * T, :].T)
        m["pos"] = np.ascontiguousarray(positions[b:b + 1, h * T:(h + 1) * T])
        m["kbias"] = kb_b if h == 1 else kb_a
        m["hmul"] = np.full((128, 1), float(h), f32)
        in_maps.append(m)
    used = set(nc._ext_names.keys())
    in_maps = [{k: v for k, v in m.items() if k in used} for m in in_maps]
    res = run_bass_kernel_spmd(nc, in_maps, core_ids=list(range(NCORES)))
    out = np.empty((4, 2 * T, D), f32)
    for c in range(NCORES):
        b, h = c // 2, c % 2
        out[b, h * T:(h + 1) * T, :] = np.asarray(res.results[c]["out"]).T
    return out


def f_p0(nc, io):
    P = Prog(nc)
    ones_f, ones_b = emit_consts(P)
    X = P.sb("X", [128, KT, T], F32)
    H = P.sb("H", [128, KT, T], BF16)
    xv = io["xT"].rearrange("(kt p) t -> p kt t", p=128)
    for kt in range(KT):
        P.dma("sp", X[:, kt, :], xv[:, kt, :], w=["X%d" % kt])
    emit_rmsnorm(P, X, lambda kt: "X%d" % kt, io["gain"], H, lambda kt: "H%d" % kt, ones_f, "n0")
    hv = io["hT"].rearrange("(kt p) t -> p kt t", p=128)
    for kt in range(KT):
        P.dma("sp", hv[:, kt, :], H[:, kt, :], r=["H%d" % kt], w=["hT_out"])
    P.finish()


def f_p1(nc, io, even):
    NH = 8 if even else 16
    DA = NH * HD
    hT_d, win_d = io["hT"], io["w_in"]
    qT_d, kT_d, V_d = io["qT"], io["kT"], io["V"]
    P = Prog(nc)
    H = P.sb("H", [128, KT, T], BF16)
    hv = hT_d.rearrange("(kt p) t -> p kt t", p=128)
    for kt in range(KT):
        P.dma("sp", H[:, kt, :], hv[:, kt, :], w=["H%d" % kt])
    hbufs = ["H%d" % kt for kt in range(KT)]
    winv = win_d.rearrange("(kt p) n -> p kt n", p=128)
    NWB = 4 if even else 6
    W = [P.sb("W%d" % i, [128, KT, 512], BF16) for i in range(NWB)]
    ps = [P.ps("ps%d" % i, [128, T]) for i in range(3)]
    pcount = [0]
    st = [P.sb("st%d" % i, [128, T], BF16) for i in range(3)]
    scount = [0]
    if even:
        tabs = emit_rope_tables(P, io["pos"], io["rope_c"])
        Pm = P.sb("Pm", [32, 32], F32)
        P.dma("sp", Pm[:], io["rope_pm"], w=["Pm"])
        qs32 = [P.sb("qs32_%d" % i, [32, T], F32) for i in range(2)]
        t1 = [P.sb("rt1_%d" % i, [32, T], F32) for i in range(2)]
        t2 = [P.sb("rt2_%d" % i, [32, T], F32) for i in range(2)]
        sw_ps = P.ps("sw_ps", [32, T])
        rcount = [0]

    groups = []
    for g in range(DA // 512):
        groups.append(("k", DA + g * 512, 512, g))
    for g in range(DA // 512):
        groups.append(("v", 2 * DA + g * 512, 512, g))
    groups.append(("xchg", 0, 0, 0))
    for g in range(DA // 512):
        groups.append(("q", g * 512, 512, g))
    if even:
        for g in range(2):
            groups.append(("glu_a", 3 * DA + g * 512, 512, g))
            groups.append(("glu_g", 3 * DA + 1024 + g * 512, 512, g))
    else:
        groups.append(("f", 3 * DA, 16, 0))
    wgroups = [g for g in groups if g[0] != "xchg"]
    slot_of = {}

    def load(i):
        if i >= len(wgroups):
            return
        kind, c0, ncol, _ = wgroups[i]
        b = i % NWB
        slot_of[i] = b
        P.dma("pool", W[b][:, :, 0:ncol], winv[:, :, c0:c0 + ncol], w=["W%d" % b])

    def feat_tile(b, i):
        pi = pcount[0] % 3
        pcount[0] += 1
        for c in range(2):
            cs = slice(c * 512, (c + 1) * 512)
            P.mm_group([lambda pe, s_, e_, kt=kt, cs=cs, b=b, i=i, pi=pi: pe.matmul(
                ps[pi][:, cs], lhsT=W[b][:, kt, i * 128:(i + 1) * 128], rhs=H[:, kt, cs], start=s_, stop=e_)
                for kt in range(KT)], r=hbufs + ["W%d" % b], w=["ps%d_%d" % (pi, c)])
        return pi

    def psb(pi):
        return ["ps%d_0" % pi, "ps%d_1" % pi]

    def qk_group(which, b, g):
        out_d = qT_d if which == "q" else kT_d
        for i in range(4):
            h = g * 4 + i
            pi = feat_tile(b, i)
            si = scount[0] % 3
            scount[0] += 1
            sc = SCALE if which == "q" else 1.0
            if even:
                ri = rcount[0] % 2
                rcount[0] += 1
                P.op("dve", lambda e, pi=pi, ri=ri: e.tensor_copy(out=qs32[ri][:], in_=ps[pi][0:32, :]),
                     r=psb(pi), w=["qs32_%d" % ri])
                for c in range(2):
                    cs = slice(c * 512, (c + 1) * 512)
                    P.mm_group([lambda pe, s_, e_, cs=cs, ri=ri: pe.matmul(
                        sw_ps[:, cs], lhsT=Pm[:], rhs=qs32[ri][:, cs], start=True, stop=True)],
                        r=["qs32_%d" % ri, "Pm"], w=["sw_ps%d" % c])
                ct = tabs["cos" + which]
                sn = tabs["sin" + which]
                P.op("dve", lambda e, ri=ri, ct=ct: e.tensor_tensor(out=t1[ri][:], in0=qs32[ri][:], in1=ct[:], op=ALU.mult),
                     r=["qs32_%d" % ri, "tab_cos" + which], w=["rt1_%d" % ri])
                P.op("dve", lambda e, ri=ri, sn=sn: e.tensor_tensor(out=t2[ri][:], in0=sw_ps[:], in1=sn[:], op=ALU.mult),
                     r=["sw_ps0", "sw_ps1", "tab_sin" + which], w=["rt2_%d" % ri])
                P.op("act", lambda e, pi=pi, si=si, sc=sc: e.activation(out=st[si][:], in_=ps[pi][:], func=AF.Copy, scale=sc),
                     r=psb(pi), w=["st%d_lo" % si, "st%d_hi" % si])
                P.op("dve", lambda e, ri=ri, si=si: e.tensor_tensor(out=st[si][0:32, :], in0=t1[ri][:], in1=t2[ri][:], op=ALU.add),
                     r=["rt1_%d" % ri, "rt2_%d" % ri], w=["st%d_lo" % si])
            else:
                P.op("act", lambda e, pi=pi, si=si, sc=sc: e.activation(out=st[si][:], in_=ps[pi][:], func=AF.Copy, scale=sc),
                     r=psb(pi), w=["st%d_lo" % si, "st%d_hi" % si])
            P.dma("sp", out_d[h * 128:(h + 1) * 128, :], st[si][:], r=["st%d_lo" % si, "st%d_hi" % si],
                  w=[which + "T_out"])

    vst = [P.sb("vst%d" % i, [128, 512], BF16) for i in range(3)]
    vcount = [0]

    def v_group(b, g):
        for tt in range(T // 128):
            pi = pcount[0] % 3
            pcount[0] += 1
            P.mm_group([lambda pe, s_, e_, kt=kt, b=b, tt=tt, pi=pi: pe.matmul(
                ps[pi][:, 0:512], lhsT=H[:, kt, tt * 128:(tt + 1) * 128], rhs=W[b][:, kt, :], start=s_, stop=e_)
                for kt in range(KT)], r=hbufs + ["W%d" % b], w=["ps%d_0" % pi])
            vi = vcount[0] % 3
            vcount[0] += 1
            P.op("dve", lambda e, pi=pi, vi=vi: e.tensor_copy(out=vst[vi][:], in_=ps[pi][:, 0:512]),
                 r=["ps%d_0" % pi], w=["vst%d" % vi])
            P.dma("sp", V_d[tt * 128:(tt + 1) * 128, g * 512:(g + 1) * 512], vst[vi][:], r=["vst%d" % vi], w=["V_out"])

    if even:
        sg = [P.sb("sg%d" % i, [128, T], F32) for i in range(2)]
        gl = [P.sb("gl%d" % i, [128, T], F32) for i in range(2)]

    def glu_group(ba, bg, g):
        for i in range(4):
            ct_ = g * 4 + i
            pa = feat_tile(ba, i)
            pg = feat_tile(bg, i)
            s_ = ct_ % 2
            P.op("act", lambda e, pg=pg, s_=s_: e.activation(out=sg[s_][:], in_=ps[pg][:], func=AF.Sigmoid),
                 r=psb(pg), w=["sg%d" % s_])
            P.op("dve", lambda e, pa=pa, s_=s_: e.tensor_tensor(out=gl[s_][:], in0=ps[pa][:], in1=sg[s_][:], op=ALU.mult),
                 r=psb(pa) + ["sg%d" % s_], w=["gl%d" % s_])
            P.dma("sp", io["gluT"][ct_ * 128:(ct_ + 1) * 128, :], gl[s_][:], r=["gl%d" % s_], w=["glu_out"])
            P.dma("sp", io["glu_hl"][ct_ * 128:(ct_ + 1) * 128, :], gl[s_][:, T - 32:T], r=["gl%d" % s_], w=["gluh_out"])

    def f_group(b):
        BFt = P.sb("BFt", [16, 1], F32)
        P.dma("sp", BFt[:], io["b_f"], w=["BFt"])
        NB = P.sb("NB", [16, 1], F32)
        P.op("dve", lambda e: e.tensor_scalar(out=NB[:], in0=BFt[:], scalar1=-1.0, scalar2=None, op0=ALU.mult),
             r=["BFt"], w=["NB"])
        pi = pcount[0] % 3
        pcount[0] += 1
        for c in range(2):
            cs = slice(c * 512, (c + 1) * 512)
            P.mm_group([lambda pe, s_, e_, kt=kt, cs=cs, b=b, pi=pi: pe.matmul(
                ps[pi][0:16, cs], lhsT=W[b][:, kt, 0:16], rhs=H[:, kt, cs], start=s_, stop=e_)
                for kt in range(KT)], r=hbufs + ["W%d" % b], w=["ps%d_%d" % (pi, c)])
        e1 = P.sb("e1", [16, T], F32)
        l1 = P.sb("l1", [16, T], F32)
        P.op("act", lambda e, pi=pi: e.activation(out=e1[:], in_=ps[pi][0:16, :], func=AF.Exp, scale=-1.0, bias=NB[:]),
             r=psb(pi) + ["NB"], w=["e1"])
        P.op("act", lambda e: e.activation(out=l1[:], in_=e1[:], func=AF.Ln, bias=1.0), r=["e1"], w=["l1"])
        P.op("dve", lambda e: e.tensor_scalar(out=l1[:], in0=l1[:], scalar1=-1.0, scalar2=None, op0=ALU.mult),
             r=["l1"], w=["l1"])
        P.dma("sp", io["lf"], l1[:], r=["l1"], w=["lf_out"])

    def xchg():
        for j, (src, dst) in enumerate(io["k_xchg"]):
            P.coll(src, dst, r=["kT_out"], w=["Kg%d" % j])
        for j, (src, dst) in enumerate(io["v_xchg"]):
            P.coll(src, dst, r=["V_out"], w=["Vg%d" % j])

    PF = NWB - 1
    for i in range(PF):
        load(i)
    wi = 0
    pend_a = None
    xdone = False
    for grp in groups:
        kind = grp[0]
        if kind == "xchg":
            xchg()
            continue
        load(wi + PF)
        b = slot_of[wi]
        if kind in ("q", "k"):
            qk_group(kind, b, grp[3])
        elif kind == "v":
            v_group(b, grp[3])
        elif kind == "glu_a":
            pend_a = b
        elif kind == "glu_g":
            glu_group(pend_a, b, grp[3])
        elif kind == "f":
            f_group(b)
        wi += 1
    if even:
        P.coll(io["glu_hl"], io["glu_hlg"], r=["gluh_out"], w=["gluhg"])
    else:
        P.coll(io["lf"], io["lfg"], r=["lf_out"], w=["lfg"])
    P.finish()


def f_p2_odd(nc, io):
    P = Prog(nc)
    ones_f, ones_b = emit_consts(P)
    FS = io["FS"]
    A = P.sb("fA", [16, T], F32)
    B = P.sb("fB", [16, T], F32)
    C = P.sb("fC", [16, T], F32)
    carry = P.sb("fcarry", [16, 1], F32)
    P.op("dve", lambda e: e.memset(carry[:], 0.0), w=["fcarry"])
    parts = [P.sb("fp%d" % i, [16, T], BF16) for i in range(6)]
    for half in range(2):
        hsl = slice(half * T, (half + 1) * T)
        src = io["lfg"][0:16, :] if half == 0 else io["lf"]
        P.dma("sp", A[:], src, w=["fA"])
        P.op("dve", lambda e: e.memset(B[:], 1.0), w=["fB"])
        P.op("dve", lambda e: e.tensor_tensor_scan(out=C[:], data0=B[:], data1=A[:], initial=carry[:],
                                                   op0=ALU.mult, op1=ALU.add),
             r=["fA", "fB", "fcarry"], w=["fC"])
        P.op("dve", lambda e: e.tensor_copy(out=carry[:], in_=C[:, T - 1:T]), r=["fC"], w=["fcarry"])
        P.op("dve", lambda e: e.tensor_copy(out=parts[0][:], in_=C[:]), r=["fC"], w=["fp0"])
        P.op("dve", lambda e: e.tensor_copy(out=A[:], in_=parts[0][:]), r=["fp0"], w=["fA"])
        P.op("dve", lambda e: e.tensor_tensor(out=B[:], in0=C[:], in1=A[:], op=ALU.subtract), r=["fC", "fA"], w=["fB"])
        P.op("dve", lambda e: e.tensor_copy(out=parts[1][:], in_=B[:]), r=["fB"], w=["fp1"])
        P.op("dve", lambda e: e.tensor_copy(out=A[:], in_=parts[1][:]), r=["fp1"], w=["fA"])
        P.op("dve", lambda e: e.tensor_tensor(out=C[:], in0=B[:], in1=A[:], op=ALU.subtract), r=["fB", "fA"], w=["fC"])
        P.op("dve", lambda e: e.tensor_copy(out=parts[2][:], in_=C[:]), r=["fC"], w=["fp2"])
        for i in range(3):
            P.op("dve", lambda e, i=i: e.tensor_scalar(out=parts[3 + i][:], in0=parts[i][:], scalar1=-1.0,
                                                       scalar2=None, op0=ALU.mult), r=["fp%d" % i], w=["fp%d" % (3 + i)])
        for i in range(6):
            P.dma("sp", FS[i, :, hsl], parts[i][:], r=["fp%d" % i], w=["FS"])
    OT = P.sb("OT", [128, KT, T], BF16)
    y_ps = P.ps("y_ps", [128, T])
    ksrc, vsrc = kv_sources(io["kT"], io["V"], io["Kg"], io["Vg"], 512)
    emit_attention(P, 16, io["qT"], ksrc, vsrc, OT, ones_b, io["kbias"], io["negtri"],
                   fox=dict(FS=FS, ident=io["ident"]))
    emit_outproj_norm(P, OT, io["xT"], io["w_out"], io["gain"], io["xT_out"], io["h2T"], y_ps,
                      lambda c: "y_ps%d" % c, ones_f, halo=(io["hl"], io["hlg"]))
    P.finish()


def f_p2_even(nc, io):
    P = Prog(nc)
    ones_f, ones_b = emit_consts(P)
    X = P.sb("X", [128, KT, T], F32)
    OT = P.sb("OT", [128, KT, T], BF16)
    y_ps = P.ps("y_ps", [128, T])
    mask = P.sb("mask", [128, 2048], BF16)
    P.dma("sp", mask[:], io["mask"], w=["mask"])
    HM = P.sb("HM", [128, 1], F32)
    P.dma("sp", HM[:], io["hmul"], w=["HM"])
    CWt = P.sb("CWt", [128, 31, 8], F32)
    for k in range(31):
        P.dma("sp", CWt[:, k, :], io["conv_w"][k, :].rearrange("(ct p) -> p ct", p=128), w=["CWt"],
              allow_slow_non_contiguous=True)
    CP = P.sb("CP", [128, 3, 8], F32)
    for j, dd in enumerate((io["conv_b"], io["ln_g"], io["ln_b"])):
        P.dma("sp", CP[:, j, :], dd.rearrange("(ct p) -> p ct", p=128), w=["CP"], allow_slow_non_contiguous=True)
    Gt = [P.sb("Gt%d" % i, [128, T + 30], BF16) for i in range(2)]
    Dg = [P.sb("Dg%d" % i, [128, 31, 128], BF16) for i in range(2)]
    identc = P.sb("identc", [128, 128], BF16)
    P.dma("sp", identc[:], io["ident"], w=["identc"])
    glu_d, ghg = io["gluT"], io["glu_hlg"]

    def conv_prep(ct):
        gi = ct % 2
        P.dma("pool", Gt[gi][:, 0:30], ghg[ct * 128:(ct + 1) * 128, 2:32], w=["Gt%d_h" % gi])
        P.dma("pool", Gt[gi][:, 30:T + 30], glu_d[ct * 128:(ct + 1) * 128, :], w=["Gt%d" % gi])
        P.op("dve", lambda e, gi=gi: e.tensor_scalar(out=Gt[gi][:, 0:30], in0=Gt[gi][:, 0:30], scalar1=HM[:, 0:1],
                                                     scalar2=None, op0=ALU.mult), r=["Gt%d_h" % gi, "HM"], w=["Gt%d_h" % gi])
        for k in range(31):
            P.op("dve", lambda e, gi=gi, ct=ct, k=k: e.tensor_scalar(
                out=Dg[gi][:, k, :], in0=identc[:], scalar1=CWt[:, k, ct:ct + 1], scalar2=None, op0=ALU.mult),
                r=["identc", "CWt"], w=["Dg%d" % gi])

    def conv_tile(ct):
        gi = ct % 2
        gb = ["Gt%d_h" % gi, "Gt%d" % gi, "Dg%d" % gi]
        ab = "X%d" % (8 + ct)
        for c in range(2):
            P.mm_group([lambda pe, s_, e_, gi=gi, k=k, c=c: pe.matmul(
                y_ps[:, c * 512:(c + 1) * 512], lhsT=Dg[gi][:, k, :], rhs=Gt[gi][:, k + c * 512:k + c * 512 + 512],
                start=s_, stop=e_) for k in range(31)], r=gb, w=["y_ps%d" % c])
            P.op("act", lambda e, ct=ct, c=c: e.activation(
                out=X[:, 8 + ct, c * 512:(c + 1) * 512], in_=y_ps[:, c * 512:(c + 1) * 512], func=AF.Identity,
                bias=CP[:, 0, ct:ct + 1]), r=["y_ps%d" % c, "CP"], w=[ab])

    ksrc, vsrc = kv_sources(io["kT"], io["V"], io["Kg"], io["Vg"], 1024)
    ps = emit_attention(P, 8, io["qT"], ksrc, vsrc, OT, ones_b, io["kbias"], io["tri"], mask=mask, per_head=conv_tile, pre_head=conv_prep)
    st_ps = ps["st_ps"]
    sqb = [P.sb("lsq%d" % i, [128, 512], F32) for i in range(2)]
    i = 0
    for c in range(2):
        cs = slice(c * 512, (c + 1) * 512)
        for ct in range(8):
            s_ = i % 2
            i += 1
            P.op("act", lambda e, ct=ct, s_=s_, cs=cs: e.activation(out=sqb[s_][:], in_=X[:, 8 + ct, cs], func=AF.Square),
                 r=["X%d" % (8 + ct)], w=["lsq%d" % s_])
            P.mm_group([lambda pe, a_, b_, ct=ct, cs=cs: pe.matmul(
                y_ps[:, cs], lhsT=ones_f[:], rhs=X[:, 8 + ct, cs], start=(ct == 0), stop=(ct == 7),
                skip_group_check=True)], r=["X%d" % (8 + ct), "ones_f"], w=["y_ps%d" % c])
            P.mm_group([lambda pe, a_, b_, ct=ct, s_=s_, c=c: pe.matmul(
                st_ps[c][:], lhsT=ones_f[:], rhs=sqb[s_][:], start=(ct == 0), stop=(ct == 7),
                skip_group_check=True)], r=["lsq%d" % s_, "ones_f"], w=["st_ps%d" % c])
    mean = P.sb("ln_mean", [128, T], F32)
    var = P.sb("ln_var", [128, T], F32)
    for c in range(2):
        cs = slice(c * 512, (c + 1) * 512)
        P.op("dve", lambda e, cs=cs: e.tensor_scalar(out=mean[:, cs], in0=y_ps[:, cs], scalar1=1.0 / 1024, scalar2=None,
                                                     op0=ALU.mult), r=["y_ps%d" % c], w=["ln_mean%d" % c])
        P.op("dve", lambda e, cs=cs: e.tensor_tensor(out=var[:, cs], in0=mean[:, cs], in1=mean[:, cs], op=ALU.mult),
             r=["ln_mean%d" % c], w=["ln_var%d" % c])
        P.op("dve", lambda e, cs=cs, c=c: e.scalar_tensor_tensor(
            out=var[:, cs], in0=st_ps[c][:], scalar=1.0 / 1024, in1=var[:, cs], op0=ALU.mult, op1=ALU.subtract),
            r=["st_ps%d" % c, "ln_var%d" % c], w=["ln_var%d" % c])
        P.op("act", lambda e, cs=cs: e.activation(out=var[:, cs], in_=var[:, cs], func=AF.Sqrt, bias=EPS),
             r=["ln_var%d" % c], w=["ln_var%d" % c])
        P.op("dve", lambda e, cs=cs: e.reciprocal(out=var[:, cs], in_=var[:, cs]),
             r=["ln_var%d" % c], w=["ln_var%d" % c])
    lt = [P.sb("lt%d" % i, [128, T], F32) for i in range(2)]
    stat = ["ln_mean0", "ln_mean1", "ln_var0", "ln_var1"]
    for ct in range(8):
        li = ct % 2
        P.op("dve", lambda e, ct=ct, li=li: e.tensor_tensor(out=lt[li][:], in0=X[:, 8 + ct, :], in1=mean[:], op=ALU.subtract),
             r=["X%d" % (8 + ct)] + stat, w=["lt%d" % li])
        P.op("dve", lambda e, li=li: e.tensor_tensor(out=lt[li][:], in0=lt[li][:], in1=var[:], op=ALU.mult),
             r=["lt%d" % li] + stat, w=["lt%d" % li])
        P.op("act", lambda e, ct=ct, li=li: e.activation(out=OT[:, 8 + ct, :], in_=lt[li][:], func=AF.Silu,
                                                         scale=CP[:, 1, ct:ct + 1], bias=CP[:, 2, ct:ct + 1]),
             r=["lt%d" % li, "CP"], w=["OT%d" % (8 + ct)])
    emit_outproj_norm(P, OT, io["xT"], io["w_out"], io["gain"], io["xT_out"], io["h2T"], y_ps,
                      lambda c: "y_ps%d" % c, ones_f, X=X, halo=((io["hl"], io["hlg"]) if "hl" in io else None))
    P.finish()
```

```python
import numpy as np
from contextlib import ExitStack
import concourse.bass as bass
import concourse.mybir as mybir
from concourse.bass_utils import run_bass_kernel_spmd

F32 = mybir.dt.float32
BF16 = mybir.dt.bfloat16
I32 = mybir.dt.int32
AF = mybir.ActivationFunctionType
ALU = mybir.AluOpType

D = 2048
T = 1024
KT = D // 128
DFF = 5632
NFT = DFF // 128
EPS = 1e-6
NCORES = 8


class Prog:
    ENGS = ("pe", "act", "dve", "pool", "sp")
    UID = [0]
    POOL = {}

    def __init__(self, nc):
        self.nc = nc
        Prog.UID[0] += 1
        self.uid = "p%d_" % Prog.UID[0]
        self.coll_cnt = {}
        self.ops = {e: [] for e in self.ENGS}
        self.nsig = {e: 0 for e in self.ENGS}
        self.dma_cnt = {}
        self.writer = {}
        self.readers = {}
        self.stack = ExitStack()

    def sb(self, name, shape, dt):
        return self.stack.enter_context(self.nc.sbuf_tensor(self.uid + "sb_" + name, list(shape), dt))

    def ps(self, name, shape, dt=F32):
        return self.stack.enter_context(self.nc.psum_tensor(self.uid + "pp_" + name, list(shape), dt))

    @staticmethod
    def _is_psum(b):
        return ("ps" in b) or b.startswith("ss_")

    def _deps(self, r, w):
        toks = []
        for b in r:
            if b in self.writer:
                toks.append(self.writer[b])
            if self._is_psum(b):
                toks.extend(self.readers.get(b, []))
        for b in w:
            if b in self.writer:
                toks.append(self.writer[b])
            toks.extend(self.readers.get(b, []))
        return toks

    def _commit(self, tok, r, w):
        for b in r:
            self.readers.setdefault(b, []).append(tok)
        for b in w:
            self.writer[b] = tok
            self.readers[b] = []

    def op(self, eng, fn, r=(), w=(), extra=()):
        waits = self._deps(r, w) + list(extra)
        self.nsig[eng] += 1
        tok = ("E:" + eng, self.nsig[eng])
        self.ops[eng].append((fn, waits, tok))
        self._commit(tok, r, w)
        return tok

    def mm_group(self, mms, r=(), w=()):
        waits = self._deps(r, w)
        n = len(mms)
        self.nsig["pe"] += 1
        tok = ("E:pe", self.nsig["pe"])
        for i, fn in enumerate(mms):
            f = (lambda pe, fn=fn, i=i: fn(pe, i == 0, i == n - 1))
            self.ops["pe"].append((f, waits if i == 0 else [], tok if i == n - 1 else None))
        self._commit(tok, r, w)
        return tok

    def dma(self, queue, out, in_, r=(), w=(), key=None, **kw):
        waits = self._deps(r, w)
        if key is None:
            key = w[0] if w else r[0]
        self.dma_cnt[key] = self.dma_cnt.get(key, 0) + 1
        tok = ("D:" + str(key), 16 * self.dma_cnt[key])
        fn = (lambda e: e.dma_start(out=out, in_=in_, **kw))
        self.ops[queue].append((fn, waits, tok))
        self._commit(tok, r, w)
        return tok

    def coll(self, in_ap, out_ap, r=(), w=(), key=None):
        waits = self._deps(r, w)
        if key is None:
            key = w[0]
        self.coll_cnt[key] = self.coll_cnt.get(key, 0) + 1
        tok = ("C:" + str(key), self.coll_cnt[key])
        fn = (lambda e: e.collective_compute("AllGather", ALU.bypass, replica_groups=PAIRS,
                                             ins=[in_ap], outs=[out_ap]))
        self.ops["pool"].append((fn, waits, tok))
        self._commit(tok, r, w)
        return tok

    def check_deadlock(self):
        pos = {e: 0 for e in self.ENGS}
        val = {}
        progress = True
        while progress:
            progress = False
            for e in self.ENGS:
                ops = self.ops[e]
                while pos[e] < len(ops):
                    fn, waits, tok = ops[pos[e]]
                    ok = all((k == "E:pe" and e == "pe") or val.get(k, 0) >= v for (k, v) in waits)
                    if not ok:
                        break
                    if tok is not None:
                        k, v = tok
                        val[k] = val.get(k, 0) + (16 if k.startswith("D:") else 1)
                    pos[e] += 1
                    progress = True
        stuck = {e: (pos[e], len(self.ops[e])) for e in self.ENGS if pos[e] < len(self.ops[e])}
        if stuck:
            msg = []
            for e, (p, n) in stuck.items():
                fn, waits, tok = self.ops[e][p]
                bad = [(k, v, val.get(k, 0)) for (k, v) in waits if val.get(k, 0) < v]
                msg.append("%s stuck at %d/%d waiting %s (tok %s)" % (e, p, n, bad, tok))
            raise RuntimeError("DEADLOCK: " + "; ".join(msg))

    def finish(self, final_tokens=None):
        self.check_deadlock()
        nc = self.nc
        semkeys = (["E:" + e for e in self.ENGS] + ["D:" + str(k) for k in self.dma_cnt]
                   + ["C:" + str(k) for k in self.coll_cnt])
        pool = Prog.POOL
        if pool.get("nc") is not nc:
            pool.clear()
            pool.update(nc=nc, handles=[], base=[], stack=ExitStack())
        while len(pool["handles"]) < len(semkeys):
            pool["handles"].append(pool["stack"].enter_context(nc.semaphore("gs%d" % len(pool["handles"]))))
            pool["base"].append(0)
        slot = {k: i for i, k in enumerate(semkeys)}
        base = {k: pool["base"][slot[k]] for k in semkeys}

        class _Sems(dict):
            pass
        sems = {k: pool["handles"][slot[k]] for k in semkeys}
        totals = {}
        for e_ in self.ENGS:
            totals["E:" + e_] = self.nsig[e_]
        for k, c in self.dma_cnt.items():
            totals["D:" + str(k)] = 16 * c
        for k, c in self.coll_cnt.items():
            totals["C:" + str(k)] = c
        fin = [("D:" + str(k), 16 * c) for k, c in self.dma_cnt.items()]
        fin += [("C:" + str(k), c) for k, c in self.coll_cnt.items()]
        eng_of = {"pe": "tensor", "act": "scalar", "dve": "vector", "pool": "gpsimd", "sp": "sync"}
        with nc.Block() as block:
            for ename in self.ENGS:
                ops = self.ops[ename]
                extra_fin = fin if ename == "sp" else []

                def body(e, ops=ops, ename=ename, extra_fin=extra_fin):
                    seen = {}
                    for fn, waits, tok in ops:
                        for (k, v) in waits:
                            if k == "E:pe" and ename == "pe":
                                continue
                            if seen.get(k, 0) >= v:
                                continue
                            e.wait_ge(sems[k], base[k] + v)
                            seen[k] = v
                        ins = fn(e)
                        if tok is not None:
                            k, v = tok
                            ins.then_inc(sems[k], 16 if k.startswith("D:") else 1)
                    for (k, v) in extra_fin:
                        if seen.get(k, 0) < v:
                            e.wait_ge(sems[k], base[k] + v)

                getattr(block, eng_of[ename])(body)
        for k in semkeys:
            pool["base"][slot[k]] += totals[k]
        self.stack.close()


_FUSED = {"nc": None, "io": {}}


def _new_nc():
    if _FUSED["nc"] is not None:
        return _FUSED["nc"]
    return bass.Bass("TRN2", target_bir_lowering=False)


def dram_in(nc, name, shape, dt):
    if _FUSED["nc"] is not None:
        ap = _FUSED["io"][name]
        assert list(ap.shape) == list(shape), (name, ap.shape, shape)
        return ap
    return nc.dram_tensor(name, list(shape), dt, kind="ExternalInput").ap()


def dram_out(nc, name, shape, dt):
    if _FUSED["nc"] is not None:
        ap = _FUSED["io"][name]
        assert list(ap.shape) == list(shape), (name, ap.shape, shape)
        return ap
    return nc.dram_tensor(name, list(shape), dt, kind="ExternalOutput").ap()


PAIRS = [[0, 1], [2, 3], [4, 5], [6, 7]]


def emit_consts(P):
    ones_f = P.sb("ones_f", [128, 128], F32)
    ones_b = P.sb("ones_b", [128, 128], BF16)
    P.op("dve", lambda e: e.memset(ones_f[:], 1.0), w=["ones_f"])
    P.op("dve", lambda e: e.memset(ones_b[:], 1.0), w=["ones_b"])
    return ones_f, ones_b


def emit_rmsnorm(P, X, xbuf, gain_d, H, hbuf, ones_f, tag, ncols=T, ss_ps=None, ssbuf=None):
    nc = P.nc
    G = P.sb("G_" + tag, [128, KT], F32)
    P.dma("sp", G[:], gain_d, w=["G_" + tag])
    sq = [P.sb("sq%d_%s" % (i, tag), [128, 512], BF16) for i in range(2)]
    ones_sq = P.sb("ones_sq_" + tag, [128, 128], BF16)
    P.op("dve", lambda e: e.memset(ones_sq[:], 1.0), w=["ones_sq_" + tag])
    if ss_ps is None:
        ss_ps = P.ps("ss_" + tag, [128, ncols])
    if ssbuf is None:
        ssbuf = lambda c: "ss_%s_%d" % (tag, c)
    rstd = P.sb("rstd_" + tag, [128, ncols], F32)
    nch = ncols // 512
    i = 0
    for c in range(nch):
        cs = slice(c * 512, (c + 1) * 512)
        mms = []
        for kt in range(KT):
            s = i % 2
            i += 1
            P.op("act", lambda e, kt=kt, s=s, cs=cs: e.activation(out=sq[s][:], in_=X[:, kt, cs], func=AF.Square),
                 r=[xbuf(kt)], w=["sq%d_%s" % (s, tag)])
            P.mm_group([lambda pe, st, sp_, s=s, cs=cs, kt=kt: pe.matmul(
                ss_ps[:, cs], lhsT=ones_sq[:], rhs=sq[s][:], start=(kt == 0), stop=(kt == KT - 1),
                skip_group_check=True)],
                r=["sq%d_%s" % (s, tag), "ones_sq_" + tag], w=[ssbuf(c)])
        P.op("act", lambda e, cs=cs: e.activation(out=rstd[:, cs], in_=ss_ps[:, cs], func=AF.Sqrt,
                                                  scale=1.0 / D, bias=EPS),
             r=[ssbuf(c)], w=["rstd_%s_%d" % (tag, c)])
        P.op("dve", lambda e, cs=cs: e.reciprocal(out=rstd[:, cs], in_=rstd[:, cs]),
             r=["rstd_%s_%d" % (tag, c)], w=["rstd_%s_%d" % (tag, c)])
    for kt in range(KT):
        P.op("dve", lambda e, kt=kt: e.scalar_tensor_tensor(
            out=H[:, kt, :], in0=X[:, kt, :], scalar=G[:, kt:kt + 1], in1=rstd[:],
            op0=ALU.mult, op1=ALU.mult),
            r=[xbuf(kt), "G_" + tag] + ["rstd_%s_%d" % (tag, c) for c in range(nch)], w=[hbuf(kt)])


def build_p0():
    nc = _new_nc()
    xT_d = dram_in(nc, "xT", [D, T], F32)
    g_d = dram_in(nc, "gain", [D], F32)
    hT_d = dram_out(nc, "hT", [D, T], BF16)
    P = Prog(nc)
    ones_f, ones_b = emit_consts(P)
    X = P.sb("X", [128, KT, T], F32)
    H = P.sb("H", [128, KT, T], BF16)
    xv = xT_d.rearrange("(kt p) t -> p kt t", p=128)
    for kt in range(KT):
        P.dma("sp", X[:, kt, :], xv[:, kt, :], w=["X%d" % kt])
    emit_rmsnorm(P, X, lambda kt: "X%d" % kt, g_d, H, lambda kt: "H%d" % kt, ones_f, "n0")
    hv = hT_d.rearrange("(kt p) t -> p kt t", p=128)
    for kt in range(KT):
        P.dma("sp", hv[:, kt, :], H[:, kt, :], r=["H%d" % kt], w=["hT_out"])
    P.finish()
    return nc


def f_p3(nc, io, final):
    xT_d, h2_d, halo_d = io["xT"], io["h2T"], io["halo"]
    wup_d, cw_d, wdn_d, gn_d = io["w_up"], io["conv_w"], io["w_down"], io["gain_next"]
    xo_d = None if final else io["xT_out"]
    ho_d = io["hT_out"]

    P = Prog(nc)
    ones_f, ones_b = emit_consts(P)
    X = P.sb("X", [128, KT, T], F32)
    H = P.sb("H", [128, KT, T], BF16)
    HH = P.sb("HH", [128, KT, 2], BF16)
    xv = xT_d.rearrange("(kt p) t -> p kt t", p=128)
    hv = h2_d.rearrange("(kt p) t -> p kt t", p=128)
    for kt in range(KT):
        P.dma("sp", H[:, kt, :], hv[:, kt, :], w=["H%d" % kt])
    P.dma("sp", HH[:], halo_d.rearrange("p (kt t) -> p kt t", t=2), w=["HH"])
    HM = P.sb("HM", [128, 1], F32)
    P.dma("sp", HM[:], io["hmul"], w=["HM"])
    P.op("dve", lambda e: e.tensor_scalar(out=HH[:], in0=HH[:], scalar1=HM[:, 0:1], scalar2=None, op0=ALU.mult),
         r=["HH", "HM"], w=["HH"])
    for kt in range(KT):
        P.dma("sp", X[:, kt, :], xv[:, kt, :], w=["X%d" % kt])
    CW = P.sb("CW", [128, 3, NFT], F32)
    P.dma("sp", CW[:], cw_d, w=["CW"])

    GW = 2
    NG = NFT // GW
    NBUF = 2
    Wg = [P.sb("Wg%d" % i, [128, KT, 128 * GW], BF16) for i in range(NBUF)]
    Wu = [P.sb("Wu%d" % i, [128, KT, 128 * GW], BF16) for i in range(NBUF)]
    Wd = [P.sb("Wd%d" % i, [128, GW, D], BF16) for i in range(NBUF)]
    M = [P.sb("M%d" % i, [128, GW, T], BF16) for i in range(2)]
    Gs = [P.sb("Gs%d" % i, [128, T + 2], F32) for i in range(2)]
    Tm = [P.sb("Tm%d" % i, [128, T], F32) for i in range(2)]
    Sl = [P.sb("Sl%d" % i, [128, T], F32) for i in range(2)]
    g_ps = P.ps("g_ps", [128, T])
    u_ps = P.ps("u_ps", [128, T])
    h_ps = P.ps("h_ps", [128, 512])
    y_ps = [P.ps("y_ps%d" % i, [128, 512]) for i in range(3)]

    wupv = wup_d.rearrange("(kt p) n -> p kt n", p=128)
    wdnv = wdn_d.rearrange("(ft p) n -> p ft n", p=128)

    def load_up(g):
        b = g % NBUF
        c0 = g * 128 * GW
        P.dma("pool", Wg[b][:], wupv[:, :, c0:c0 + 128 * GW], w=["Wg%d" % b])
        P.dma("pool", Wu[b][:], wupv[:, :, DFF + c0:DFF + c0 + 128 * GW], w=["Wu%d" % b])

    def load_dn(g):
        b = g % NBUF
        P.dma("pool", Wd[b][:], wdnv[:, g * GW:(g + 1) * GW, :], w=["Wd%d" % b])

    def up_group(g):
        b = g % NBUF
        mb = g % 2
        for i in range(GW):
            ft = g * GW + i
            s = ft % 2
            hbufs = ["H%d" % kt for kt in range(KT)]
            fns = []
            for kt in range(KT):
                for c in range(2):
                    fns.append(lambda pe, st, sp_, kt=kt, c=c, b=b, i=i: pe.matmul(
                        g_ps[:, c * 512:(c + 1) * 512], lhsT=Wg[b][:, kt, i * 128:(i + 1) * 128],
                        rhs=H[:, kt, c * 512:(c + 1) * 512], start=(kt == 0), stop=(kt == KT - 1),
                        skip_group_check=True))
                fns.append(lambda pe, st, sp_, kt=kt, b=b, i=i: pe.matmul(
                    h_ps[:, 0:2], lhsT=Wg[b][:, kt, i * 128:(i + 1) * 128], rhs=HH[:, kt, :],
                    start=(kt == 0), stop=(kt == KT - 1), skip_group_check=True))
            P.mm_group(fns, r=hbufs + ["HH", "Wg%d" % b], w=["g_ps0", "g_ps1", "h_ps"])
            for c in range(2):
                cs = slice(c * 512, (c + 1) * 512)
                P.mm_group([lambda pe, st, sp_, kt=kt, cs=cs, b=b, i=i: pe.matmul(
                    u_ps[:, cs], lhsT=Wu[b][:, kt, i * 128:(i + 1) * 128], rhs=H[:, kt, cs], start=st, stop=sp_)
                    for kt in range(KT)], r=hbufs + ["Wu%d" % b], w=["u_ps%d" % c])
            P.op("act", lambda e, s=s: e.activation(out=Gs[s][:, 0:2], in_=h_ps[:, 0:2], func=AF.Copy),
                 r=["h_ps"], w=["Gs%d_h" % s])
            for c in range(2):
                P.op("act", lambda e, s=s, c=c: e.activation(
                    out=Gs[s][:, 2 + c * 512:2 + (c + 1) * 512], in_=g_ps[:, c * 512:(c + 1) * 512], func=AF.Copy),
                    r=["g_ps%d" % c], w=["Gs%d_%d" % (s, c)])
            gsb = ["Gs%d_h" % s, "Gs%d_0" % s, "Gs%d_1" % s]
            P.op("dve", lambda e, s=s, ft=ft: e.tensor_scalar(
                out=Tm[s][:], in0=Gs[s][:, 2:T + 2], scalar1=CW[:, 2, ft:ft + 1], scalar2=None, op0=ALU.mult),
                r=gsb + ["CW"], w=["Tm%d" % s])
            P.op("dve", lambda e, s=s, ft=ft: e.scalar_tensor_tensor(
                out=Tm[s][:], in0=Gs[s][:, 1:T + 1], scalar=CW[:, 1, ft:ft + 1], in1=Tm[s][:],
                op0=ALU.mult, op1=ALU.add), r=gsb + ["CW", "Tm%d" % s], w=["Tm%d" % s])
            P.op("dve", lambda e, s=s, ft=ft: e.scalar_tensor_tensor(
                out=Tm[s][:], in0=Gs[s][:, 0:T], scalar=CW[:, 0, ft:ft + 1], in1=Tm[s][:],
                op0=ALU.mult, op1=ALU.add), r=gsb + ["CW", "Tm%d" % s], w=["Tm%d" % s])
            P.op("act", lambda e, s=s: e.activation(out=Sl[s][:], in_=Tm[s][:], func=AF.Silu),
                 r=["Tm%d" % s], w=["Sl%d" % s])
            for c in range(2):
                cs = slice(c * 512, (c + 1) * 512)
                P.op("dve", lambda e, s=s, cs=cs, mb=mb, i=i: e.tensor_tensor(
                    out=M[mb][:, i, cs], in0=u_ps[:, cs], in1=Sl[s][:, cs], op=ALU.mult),
                    r=["u_ps%d" % c, "Sl%d" % s], w=["M%d_%d_%d" % (mb, i, c)])

    ycount = [0]

    def down_group(g):
        b = g % NBUF
        mb = g % 2
        for nt in range(KT):
            for c in range(2):
                cs = slice(c * 512, (c + 1) * 512)
                yb = ycount[0] % 3
                ycount[0] += 1
                P.mm_group([lambda pe, st, sp_, i=i, cs=cs, b=b, mb=mb, nt=nt, yb=yb: pe.matmul(
                    y_ps[yb][:], lhsT=Wd[b][:, i, nt * 128:(nt + 1) * 128], rhs=M[mb][:, i, cs], start=st, stop=sp_)
                    for i in range(GW)],
                    r=["M%d_%d_%d" % (mb, i, c) for i in range(GW)] + ["Wd%d" % b], w=["y_ps%d" % yb])
                P.op("dve", lambda e, nt=nt, cs=cs, yb=yb: e.tensor_tensor(
                    out=X[:, nt, cs], in0=y_ps[yb][:], in1=X[:, nt, cs], op=ALU.add),
                    r=["y_ps%d" % yb, "X%d" % nt], w=["X%d" % nt])

    load_up(0)
    load_dn(0)
    for g in range(NG):
        if g + 1 < NG:
            load_up(g + 1)
        up_group(g)
        if g >= 1:
            down_group(g - 1)
        if g + 1 < NG:
            load_dn(g + 1)
    down_group(NG - 1)

    gp = lambda c: "g_ps%d" % c
    xb = lambda kt: "X%d" % kt
    hov = ho_d.rearrange("(kt p) t -> p kt t", p=128)
    if final:
        emit_rmsnorm(P, X, xb, gn_d, X, xb, ones_f, "nn", ss_ps=g_ps, ssbuf=gp)
        for kt in range(KT):
            P.dma("sp", hov[:, kt, :], X[:, kt, :], r=["X%d" % kt], w=["ho"])
    else:
        xov = xo_d.rearrange("(kt p) t -> p kt t", p=128)
        for kt in range(KT):
            P.dma("sp", xov[:, kt, :], X[:, kt, :], r=["X%d" % kt], w=["xo"])
        emit_rmsnorm(P, X, xb, gn_d, H, lambda kt: "H%d" % kt, ones_f, "nn", ss_ps=g_ps, ssbuf=gp)
        for kt in range(KT):
            P.dma("sp", hov[:, kt, :], H[:, kt, :], r=["H%d" % kt], w=["ho"])
    P.finish()
    return nc


HD = 128
SCALE = HD ** -0.5
TWO_PI = float(2.0 * np.pi)
CW1 = 6.28125
CW2 = float(2.0 * np.pi - 6.28125)


def rope_consts():
    half = 16
    inv = np.power(np.float32(500000.0), -np.arange(half, dtype=np.float32) * np.float32(2.0 / 32)).astype(np.float32)
    c = np.zeros((32, 3), np.float32)
    c[:, 0] = np.concatenate([inv, inv])
    c[:16, 1] = -1.0
    c[16:, 1] = 1.0
    pm = np.zeros((32, 32), np.float32)
    for e2 in range(32):
        pm[(e2 + 16) % 32, e2] = 1.0
    return c, pm


def emit_rope_tables(P, pos_d, rc_d, tag="rp"):
    nc = P.nc
    posi = P.sb("posi", [32, T], I32)
    P.dma("sp", posi[:], pos_d.partition_broadcast(32) if hasattr(pos_d, "partition_broadcast") else
          bass.AP(pos_d.tensor, pos_d.offset, [[0, 32], [1, T]]), w=["posi"])
    RC = P.sb("RC", [32, 3], F32)
    P.dma("sp", RC[:], rc_d, w=["RC"])
    posf = P.sb("posf", [32, T], F32)
    P.op("dve", lambda e: e.tensor_copy(out=posf[:], in_=posi[:]), r=["posi"], w=["posf"])
    ang = P.sb("ang", [32, T], F32)
    P.op("dve", lambda e: e.tensor_scalar(out=ang[:], in0=posf[:], scalar1=RC[:, 0:1], scalar2=None, op0=ALU.mult),
         r=["posf", "RC"], w=["ang"])
    tabs = {}
    ni = P.sb("rp_ni", [32, T], I32)
    nf = P.sb("rp_nf", [32, T], F32)
    rr = P.sb("rp_r", [32, T], F32)
    mm = P.sb("rp_m", [32, T], F32)
    for name in ("sin", "cos"):
        if name == "sin":
            P.op("dve", lambda e: e.tensor_scalar(out=nf[:], in0=ang[:], scalar1=1.0 / TWO_PI, scalar2=None,
                                                  op0=ALU.mult), r=["ang"], w=["rp_nf"])
            P.op("dve", lambda e: e.tensor_copy(out=ni[:], in_=nf[:]), r=["rp_nf"], w=["rp_ni"])
            P.op("dve", lambda e: e.tensor_copy(out=nf[:], in_=ni[:]), r=["rp_ni"], w=["rp_nf"])
            P.op("dve", lambda e: e.scalar_tensor_tensor(out=rr[:], in0=nf[:], scalar=-CW1, in1=ang[:],
                                                         op0=ALU.mult, op1=ALU.add), r=["rp_nf", "ang"], w=["rp_r"])
            P.op("dve", lambda e: e.scalar_tensor_tensor(out=rr[:], in0=nf[:], scalar=-CW2, in1=rr[:],
                                                         op0=ALU.mult, op1=ALU.add), r=["rp_nf", "rp_r"], w=["rp_r"])
        else:
            P.op("dve", lambda e: e.tensor_scalar(out=rr[:], in0=rr[:], scalar1=float(np.pi / 2), scalar2=None,
                                                  op0=ALU.add), r=["rp_r"], w=["rp_r"])
        P.op("dve", lambda e: e.tensor_scalar(out=mm[:], in0=rr[:], scalar1=float(np.pi), scalar2=-TWO_PI,
                                              op0=ALU.is_gt, op1=ALU.mult), r=["rp_r"], w=["rp_m"])
        P.op("dve", lambda e: e.tensor_tensor(out=rr[:], in0=rr[:], in1=mm[:], op=ALU.add),
             r=["rp_r", "rp_m"], w=["rp_r"])
        P.op("dve", lambda e: e.tensor_scalar(out=mm[:], in0=rr[:], scalar1=float(-np.pi), scalar2=TWO_PI,
                                              op0=ALU.is_lt, op1=ALU.mult), r=["rp_r"], w=["rp_m"])
        P.op("dve", lambda e: e.tensor_tensor(out=rr[:], in0=rr[:], in1=mm[:], op=ALU.add),
             r=["rp_r", "rp_m"], w=["rp_r"])
        P.op("dve", lambda e: e.tensor_scalar(out=rr[:], in0=rr[:], scalar1=3.1415925, scalar2=-3.1415925,
                                              op0=ALU.min, op1=ALU.max), r=["rp_r"], w=["rp_r"])
        tk = P.sb("tab_" + name + "k", [32, T], F32)
        tq = P.sb("tab_" + name + "q", [32, T], F32)
        P.op("act", lambda e, tk=tk: e.activation(out=tk[:], in_=rr[:], func=AF.Sin), r=["rp_r"], w=["tab_" + name + "k"])
        if name == "sin":
            P.op("dve", lambda e, tk=tk: e.tensor_scalar(out=tk[:], in0=tk[:], scalar1=RC[:, 1:2], scalar2=None,
                                                         op0=ALU.mult), r=["tab_sink", "RC"], w=["tab_sink"])
        P.op("dve", lambda e, tk=tk, tq=tq: e.tensor_scalar(out=tq[:], in0=tk[:], scalar1=SCALE, scalar2=None,
                                                            op0=ALU.mult), r=["tab_" + name + "k"], w=["tab_" + name + "q"])
        tabs[name + "k"] = tk
        tabs[name + "q"] = tq
    return tabs


def build_p1(even):
    import os
    nc = _new_nc()
    NIN = 5120 if even else 6160
    NH = 8 if even else 16
    DA = NH * HD
    hT_d = dram_in(nc, "hT", [D, T], BF16)
    win_d = dram_in(nc, "w_in", [D, NIN], F32)
    qT_d = dram_out(nc, "qT", [DA, T], BF16)
    kT_d = dram_out(nc, "kT", [DA, T], BF16)
    V_d = dram_out(nc, "V", [T, DA], BF16)
    if even:
        pos_d = dram_in(nc, "pos", [1, T], I32)
        rc_d = dram_in(nc, "rope_c", [32, 3], F32)
        pm_d = dram_in(nc, "rope_pm", [32, 32], F32)
        glu_d = dram_out(nc, "gluT", [1024, T], F32)
    else:
        bf_d = dram_in(nc, "b_f", [16, 1], F32)
        lf_d = dram_out(nc, "lf", [16, T], F32)

    P = Prog(nc)
    H = P.sb("H", [128, KT, T], BF16)
    hv = hT_d.rearrange("(kt p) t -> p kt t", p=128)
    for kt in range(KT):
        P.dma("sp", H[:, kt, :], hv[:, kt, :], w=["H%d" % kt])
    hbufs = ["H%d" % kt for kt in range(KT)]
    winv = win_d.rearrange("(kt p) n -> p kt n", p=128)

    NWB = 3
    W = [P.sb("W%d" % i, [128, KT, 512], BF16) for i in range(NWB)]
    wcount = [0]

    def load_cols(c0, ncol):
        b = wcount[0] % NWB
        wcount[0] += 1
        P.dma("pool", W[b][:, :, 0:ncol], winv[:, :, c0:c0 + ncol], w=["W%d" % b])
        return b

    ps = [P.ps("ps%d" % i, [128, T]) for i in range(3)]
    pcount = [0]
    st = [P.sb("st%d" % i, [128, T], BF16) for i in range(3)]
    scount = [0]

    if even:
        tabs = emit_rope_tables(P, pos_d, rc_d)
        Pm = P.sb("Pm", [32, 32], F32)
        P.dma("sp", Pm[:], pm_d, w=["Pm"])
        qs32 = [P.sb("qs32_%d" % i, [32, T], F32) for i in range(2)]
        t1 = [P.sb("rt1_%d" % i, [32, T], F32) for i in range(2)]
        t2 = [P.sb("rt2_%d" % i, [32, T], F32) for i in range(2)]
        sw_ps = P.ps("sw_ps", [32, T])
        rcount = [0]

    def feat_tile(b, i):
        pi = pcount[0] % 3
        pcount[0] += 1
        for c in range(2):
            cs = slice(c * 512, (c + 1) * 512)
            P.mm_group([lambda pe, s_, e_, kt=kt, cs=cs, b=b, i=i, pi=pi: pe.matmul(
                ps[pi][:, cs], lhsT=W[b][:, kt, i * 128:(i + 1) * 128], rhs=H[:, kt, cs], start=s_, stop=e_)
                for kt in range(KT)], r=hbufs + ["W%d" % b], w=["ps%d_%d" % (pi, c)])
        return pi

    def psb(pi):
        return ["ps%d_0" % pi, "ps%d_1" % pi]

    for which, out_d in (("q", qT_d), ("k", kT_d)):
        col0 = 0 if which == "q" else DA
        for g in range(DA // 512):
            b = load_cols(col0 + g * 512, 512)
            for i in range(4):
                h = g * 4 + i
                pi = feat_tile(b, i)
                si = scount[0] % 3
                scount[0] += 1
                sc = SCALE if which == "q" else 1.0
                if even and os.environ.get("K_SKIP2", "") != "rope":
                    ri = rcount[0] % 2
                    rcount[0] += 1
                    P.op("dve", lambda e, pi=pi, ri=ri: e.tensor_copy(out=qs32[ri][:], in_=ps[pi][0:32, :]),
                         r=psb(pi), w=["qs32_%d" % ri])
                    for c in range(2):
                        cs = slice(c * 512, (c + 1) * 512)
                        P.mm_group([lambda pe, s_, e_, cs=cs, ri=ri: pe.matmul(
                            sw_ps[:, cs], lhsT=Pm[:], rhs=qs32[ri][:, cs], start=True, stop=True)],
                            r=["qs32_%d" % ri, "Pm"], w=["sw_ps%d" % c])
                    ct = tabs["cos" + which]
                    sn = tabs["sin" + which]
                    P.op("dve", lambda e, ri=ri, ct=ct: e.tensor_tensor(out=t1[ri][:], in0=qs32[ri][:], in1=ct[:], op=ALU.mult),
                         r=["qs32_%d" % ri, "tab_cos" + which], w=["rt1_%d" % ri])
                    P.op("dve", lambda e, ri=ri, sn=sn: e.tensor_tensor(out=t2[ri][:], in0=sw_ps[:], in1=sn[:], op=ALU.mult),
                         r=["sw_ps0", "sw_ps1", "tab_sin" + which], w=["rt2_%d" % ri])
                    P.op("act", lambda e, pi=pi, si=si, sc=sc: e.activation(out=st[si][:], in_=ps[pi][:],
                                                                            func=AF.Copy, scale=sc),
                         r=psb(pi), w=["st%d_lo" % si, "st%d_hi" % si])
                    P.op("dve", lambda e, ri=ri, si=si: e.tensor_tensor(out=st[si][0:32, :], in0=t1[ri][:], in1=t2[ri][:], op=ALU.add),
                         r=["rt1_%d" % ri, "rt2_%d" % ri], w=["st%d_lo" % si])
                    P.dma("sp", out_d[h * 128:(h + 1) * 128, :], st[si][:], r=["st%d_lo" % si, "st%d_hi" % si],
                          w=[which + "T_out"])
                else:
                    P.op("act", lambda e, pi=pi, si=si, sc=sc: e.activation(out=st[si][:], in_=ps[pi][:],
                                                                            func=AF.Copy, scale=sc),
                         r=psb(pi), w=["st%d_lo" % si, "st%d_hi" % si])
                    P.dma("sp", out_d[h * 128:(h + 1) * 128, :], st[si][:], r=["st%d_lo" % si, "st%d_hi" % si],
                          w=[which + "T_out"])

    vst = [P.sb("vst%d" % i, [128, 512], BF16) for i in range(3)]
    vcount = [0]
    for g in range(DA // 512):
        b = load_cols(2 * DA + g * 512, 512)
        for tt in range(T // 128):
            pi = pcount[0] % 3
            pcount[0] += 1
            P.mm_group([lambda pe, s_, e_, kt=kt, b=b, tt=tt, pi=pi: pe.matmul(
                ps[pi][:, 0:512], lhsT=H[:, kt, tt * 128:(tt + 1) * 128], rhs=W[b][:, kt, :], start=s_, stop=e_)
                for kt in range(KT)], r=hbufs + ["W%d" % b], w=["ps%d_0" % pi])
            vi = vcount[0] % 3
            vcount[0] += 1
            P.op("dve", lambda e, pi=pi, vi=vi: e.tensor_copy(out=vst[vi][:], in_=ps[pi][:, 0:512]),
                 r=["ps%d_0" % pi], w=["vst%d" % vi])
            P.dma("sp", V_d[tt * 128:(tt + 1) * 128, g * 512:(g + 1) * 512], vst[vi][:], r=["vst%d" % vi], w=["V_out"])

    if even and os.environ.get("K_SKIP", "") == "glu":
        pass
    elif even:
        sg = [P.sb("sg%d" % i, [128, T], F32) for i in range(2)]
        gl = [P.sb("gl%d" % i, [128, T], F32) for i in range(2)]
        for g in range(2):
            ba = load_cols(3 * DA + g * 512, 512)
            bg = load_cols(3 * DA + 1024 + g * 512, 512)
            for i in range(4):
                ct_ = g * 4 + i
                pa = feat_tile(ba, i)
                pg = feat_tile(bg, i)
                s = ct_ % 2
                P.op("act", lambda e, pg=pg, s=s: e.activation(out=sg[s][:], in_=ps[pg][:], func=AF.Sigmoid),
                     r=psb(pg), w=["sg%d" % s])
                P.op("dve", lambda e, pa=pa, s=s: e.tensor_tensor(out=gl[s][:], in0=ps[pa][:], in1=sg[s][:], op=ALU.mult),
                     r=psb(pa) + ["sg%d" % s], w=["gl%d" % s])
                P.dma("sp", glu_d[ct_ * 128:(ct_ + 1) * 128, :], gl[s][:], r=["gl%d" % s], w=["glu_out"])
    else:
        b = load_cols(3 * DA, 16)
        BFt = P.sb("BFt", [16, 1], F32)
        P.dma("sp", BFt[:], bf_d, w=["BFt"])
        NB = P.sb("NB", [16, 1], F32)
        P.op("dve", lambda e: e.tensor_scalar(out=NB[:], in0=BFt[:], scalar1=-1.0, scalar2=None, op0=ALU.mult),
             r=["BFt"], w=["NB"])
        pi = pcount[0] % 3
        pcount[0] += 1
        for c in range(2):
            cs = slice(c * 512, (c + 1) * 512)
            P.mm_group([lambda pe, s_, e_, kt=kt, cs=cs, b=b, pi=pi: pe.matmul(
                ps[pi][0:16, cs], lhsT=W[b][:, kt, 0:16], rhs=H[:, kt, cs], start=s_, stop=e_)
                for kt in range(KT)], r=hbufs + ["W%d" % b], w=["ps%d_%d" % (pi, c)])
        e1 = P.sb("e1", [16, T], F32)
        l1 = P.sb("l1", [16, T], F32)
        P.op("act", lambda e, pi=pi: e.activation(out=e1[:], in_=ps[pi][0:16, :], func=AF.Exp, scale=-1.0, bias=NB[:]),
             r=psb(pi) + ["NB"], w=["e1"])
        P.op("act", lambda e: e.activation(out=l1[:], in_=e1[:], func=AF.Ln, bias=1.0), r=["e1"], w=["l1"])
        P.op("dve", lambda e: e.tensor_scalar(out=l1[:], in0=l1[:], scalar1=-1.0, scalar2=None, op0=ALU.mult),
             r=["l1"], w=["l1"])
        P.dma("sp", lf_d, l1[:], r=["l1"], w=["lf_out"])
    P.finish()
    return nc


def emit_outproj_norm(P, OT, xT_d, wout_d, gain_d, xo_d, h2_d, y_ps, ybuf, ones_f, X=None, halo=None):
    if X is None:
        X = P.sb("X", [128, KT, T], F32)
    xv = xT_d.rearrange("(kt p) t -> p kt t", p=128)
    for kt in range(KT):
        P.dma("sp", X[:, kt, :], xv[:, kt, :], w=["X%d" % kt])
    wov = wout_d.rearrange("(kt p) n -> p kt n", p=128)
    Wo = [P.sb("Wo%d" % i, [128, KT, 256], BF16) for i in range(2)]
    otb = ["OT%d" % kt for kt in range(KT)]
    for g in range(D // 256):
        b = g % 2
        P.dma("pool", Wo[b][:], wov[:, :, g * 256:(g + 1) * 256], w=["Wo%d" % b])
        for i in range(2):
            nt = g * 2 + i
            for c in range(2):
                cs = slice(c * 512, (c + 1) * 512)
                P.mm_group([lambda pe, s_, e_, kt=kt, cs=cs, b=b, i=i: pe.matmul(
                    y_ps[:, cs], lhsT=Wo[b][:, kt, i * 128:(i + 1) * 128], rhs=OT[:, kt, cs], start=s_, stop=e_)
                    for kt in range(KT)], r=otb + ["Wo%d" % b], w=[ybuf(c)])
                P.op("dve", lambda e, nt=nt, cs=cs: e.tensor_tensor(
                    out=X[:, nt, cs], in0=y_ps[:, cs], in1=X[:, nt, cs], op=ALU.add),
                    r=[ybuf(c), "X%d" % nt], w=["X%d" % nt])
    xov = xo_d.rearrange("(kt p) t -> p kt t", p=128)
    for kt in range(KT):
        P.dma("sp", xov[:, kt, :], X[:, kt, :], r=["X%d" % kt], w=["xo"])
    emit_rmsnorm(P, X, lambda kt: "X%d" % kt, gain_d, OT, lambda kt: "OT%d" % kt, ones_f, "n2",
                 ss_ps=y_ps, ssbuf=ybuf)
    hov = h2_d.rearrange("(kt p) t -> p kt t", p=128)
    for kt in range(KT):
        P.dma("sp", hov[:, kt, :], OT[:, kt, :], r=["OT%d" % kt], w=["ho"])
    if halo is not None:
        hl_d, hlg_d = halo
        P.dma("sp", hl_d.rearrange("p (kt t) -> p kt t", t=2), OT[:, :, T - 2:T],
              r=["OT%d" % kt for kt in range(KT)], w=["hl"])
        P.coll(hl_d, hlg_d, r=["hl"], w=["hlg"])


def emit_attention(P, nheads, qT_d, ksrc, vsrc, OT, ones_b, kbias_d, tri_d, fox=None, mask=None, per_head=None,
                   pre_head=None):
    nc = P.nc
    T2 = 2 * T
    NKT = T2 // 128
    tri = P.sb("tri", [128, 128], BF16)
    P.dma("sp", tri[:], tri_d, w=["tri"])
    if fox is not None:
        ident = P.sb("ident", [128, 128], BF16)
        P.dma("sp", ident[:], fox["ident"], w=["ident"])
    KB = P.sb("KB", [128, NKT], F32)
    P.dma("sp", KB[:], kbias_d, w=["KB"])
    kTh = [P.sb("kTh%d" % i, [128, T2], BF16) for i in range(2)]
    Vh = [P.sb("Vh%d" % i, [128, NKT, 128], BF16) for i in range(2)]
    qh = [P.sb("qh%d" % i, [128, T], BF16) for i in range(2)]
    if fox is not None:
        qaug = [P.sb("qaug%d" % i, [6, T], BF16) for i in range(2)]
        kaug = [P.sb("kaug%d" % i, [6, T2], BF16) for i in range(2)]
        for i in range(2):
            P.op("dve", lambda e, i=i: e.memset(qaug[i][:], 1.0), w=["qaug%d" % i])
            P.op("dve", lambda e, i=i: e.memset(kaug[i][:], 1.0), w=["kaug%d" % i])
    NST, NPT = 3, 4
    PT = [P.sb("PT%d" % i, [128, 512], BF16) for i in range(NPT)]
    rec = [P.sb("rec%d" % i, [128, 512], F32) for i in range(2)]
    st_ps = [P.ps("st_ps%d" % i, [128, 512]) for i in range(NST)]
    o_ps = [P.ps("o_ps%d" % i, [128, 512]) for i in range(2)]
    d_ps = [P.ps("d_ps%d" % i, [128, 512]) for i in range(1)]
    cnt = {"s": 0, "p": 0, "o": 0}
    for h in range(nheads):
        hs = h % 2
        for (csl, src) in ksrc(h):
            P.dma("sp", kTh[hs][:, csl], src, w=["kTh%d" % hs])
        for (ksl, src) in vsrc(h):
            P.dma("sp", Vh[hs][:, ksl, :], src, w=["Vh%d" % hs])
        P.dma("sp", qh[hs][:], qT_d[h * 128:(h + 1) * 128, :], w=["qh%d" % hs])
        if pre_head is not None:
            pre_head(h)
        rd = ["kTh%d" % hs, "qh%d" % hs]
        if fox is not None:
            FS = fox["FS"]
            P.dma("sp", qaug[hs][0:3, :], FS[0:3, h, T:T2], r=["FS"], w=["qaug%d" % hs])
            P.dma("sp", kaug[hs][3:6, :], FS[3:6, h, :], r=["FS"], w=["kaug%d" % hs])
            rd = rd + ["qaug%d" % hs, "kaug%d" % hs, "ident", "tri"]
        for qc in range(2):
            oi = cnt["o"] % 2
            cnt["o"] += 1
            nk = 8 + 4 * qc + 4
            q0 = qc * 512

            def s_mm(kt, hs=hs, qc=qc, q0=q0, rd=rd):
                si = cnt["s"] % NST
                cnt["s"] += 1
                c_lo = max(0, kt - 8 - 4 * qc) * 128
                mms = [lambda pe, s_, e_, kt=kt, c_lo=c_lo, si=si, hs=hs, q0=q0: pe.matmul(
                    st_ps[si][:, c_lo:512], lhsT=kTh[hs][:, kt * 128:(kt + 1) * 128],
                    rhs=qh[hs][:, q0 + c_lo:q0 + 512], start=s_, stop=e_)]
                if fox is not None:
                    mms.append(lambda pe, s_, e_, kt=kt, c_lo=c_lo, si=si, hs=hs, q0=q0: pe.matmul(
                        st_ps[si][:, c_lo:512], lhsT=kaug[hs][:, kt * 128:(kt + 1) * 128],
                        rhs=qaug[hs][:, q0 + c_lo:q0 + 512], start=s_, stop=e_))
                    if kt - 8 - 4 * qc >= 0:
                        mms.append(lambda pe, s_, e_, c_lo=c_lo, si=si: pe.matmul(
                            st_ps[si][:, c_lo:c_lo + 128], lhsT=ident[:], rhs=tri[:], start=s_, stop=e_))
                P.mm_group(mms, r=rd, w=["st_ps%d" % si])
                return si, c_lo

            def rest(kt, si, c_lo, hs=hs, qc=qc, oi=oi, nk=nk):
                pi = cnt["p"] % NPT
                cnt["p"] += 1
                P.op("act", lambda e, si=si, pi=pi, c_lo=c_lo, kt=kt: e.activation(
                    out=PT[pi][:, c_lo:512], in_=st_ps[si][:, c_lo:512], func=AF.Exp, bias=KB[:, kt:kt + 1]),
                    r=["st_ps%d" % si, "KB"], w=["PT%d" % pi])
                j0 = 8 + 4 * qc - kt
                if mask is not None:
                    jlo = j0 + c_lo // 128
                    ncol = 512 - c_lo
                    P.op("dve", lambda e, pi=pi, c_lo=c_lo, jlo=jlo, ncol=ncol: e.tensor_tensor(
                        out=PT[pi][:, c_lo:512], in0=PT[pi][:, c_lo:512],
                        in1=mask[:, jlo * 128:jlo * 128 + ncol], op=ALU.mult),
                        r=["PT%d" % pi, "mask"], w=["PT%d" % pi])
                first, last = (kt == 0), (kt == nk - 1)
                P.mm_group([lambda pe, s_, e_, kt=kt, pi=pi, c_lo=c_lo, oi=oi, hs=hs, first=first, last=last: pe.matmul(
                    o_ps[oi][:, c_lo:512], lhsT=Vh[hs][:, kt, :], rhs=PT[pi][:, c_lo:512],
                    start=first, stop=last, skip_group_check=True)],
                    r=["Vh%d" % hs, "PT%d" % pi], w=["o_ps%d" % oi])
                P.mm_group([lambda pe, s_, e_, kt=kt, pi=pi, c_lo=c_lo, first=first, last=last: pe.matmul(
                    d_ps[0][:, c_lo:512], lhsT=ones_b[:], rhs=PT[pi][:, c_lo:512],
                    start=first, stop=last, skip_group_check=True)],
                    r=["ones_b", "PT%d" % pi], w=["d_ps0"])

            pend = [s_mm(0), s_mm(1)]
            for kt in range(nk):
                if kt + 2 < nk:
                    pend.append(s_mm(kt + 2))
                rest(kt, *pend.pop(0))
            ri = oi
            P.op("dve", lambda e, ri=ri: e.reciprocal(out=rec[ri][:], in_=d_ps[0][:]),
                 r=["d_ps0"], w=["rec%d" % ri])
            P.op("dve", lambda e, ri=ri, oi=oi, h=h, q0=q0: e.tensor_tensor(
                out=OT[:, h, q0:q0 + 512], in0=o_ps[oi][:], in1=rec[ri][:], op=ALU.mult),
                r=["o_ps%d" % oi, "rec%d" % ri], w=["OT%d" % h])
        if per_head is not None:
            per_head(h)
    return dict(st_ps=st_ps, o_ps=o_ps, d_ps=d_ps)


def kv_sources(kT_d, V_d, Kg, Vg, vrows):
    Vv = V_d.rearrange("(kt p) (hh e) -> p kt hh e", p=128, e=128)
    nvt = vrows // 128

    def ksrc(h):
        j, i = (h * 128) // 1024, (h * 128) % 1024
        return [(slice(0, T), Kg[j][i:i + 128, :]), (slice(T, 2 * T), kT_d[h * 128:(h + 1) * 128, :])]

    def vsrc(h):
        out = []
        for j, g in enumerate(Vg):
            gv = g[0:vrows, :].rearrange("(kt p) (hh e) -> p kt hh e", p=128, e=128)
            out.append((slice(j * nvt, (j + 1) * nvt), gv[:, :, h, :]))
        out.append((slice(8, 16), Vv[:, :, h, :]))
        return out
    return ksrc, vsrc


def build_p2_odd():
    nc = _new_nc()
    T2 = 2 * T
    xT_d = dram_in(nc, "xT", [D, T], F32)
    qT_d = dram_in(nc, "qT", [D, T], BF16)
    kT_d = dram_in(nc, "kT", [D, T2], BF16)
    V_d = dram_in(nc, "V", [T2, D], BF16)
    lf_d = dram_in(nc, "lf", [16, T2], F32)
    kvalid_d = dram_in(nc, "kvalid", [128, 16], F32)
    tri_d = dram_in(nc, "negtri", [128, 128], BF16)
    ident_d = dram_in(nc, "ident", [128, 128], BF16)
    wout_d = dram_in(nc, "w_out", [D, D], F32)
    gain_d = dram_in(nc, "gain", [D], F32)
    xo_d = dram_out(nc, "xT_out", [D, T], F32)
    h2_d = dram_out(nc, "h2T", [D, T], BF16)
    FS = nc.dram_tensor("FS", [6, 16, T2], BF16).ap()

    P = Prog(nc)
    ones_f, ones_b = emit_consts(P)
    A = P.sb("fA", [16, T], F32)
    B = P.sb("fB", [16, T], F32)
    C = P.sb("fC", [16, T], F32)
    carry = P.sb("fcarry", [16, 1], F32)
    P.op("dve", lambda e: e.memset(carry[:], 0.0), w=["fcarry"])
    parts = [P.sb("fp%d" % i, [16, T], BF16) for i in range(6)]
    for half in range(2):
        hsl = slice(half * T, (half + 1) * T)
        P.dma("sp", A[:], lf_d[:, hsl], w=["fA"])
        P.op("dve", lambda e: e.memset(B[:], 1.0), w=["fB"])
        P.op("dve", lambda e: e.tensor_tensor_scan(out=C[:], data0=B[:], data1=A[:], initial=carry[:],
                                                   op0=ALU.mult, op1=ALU.add),
             r=["fA", "fB", "fcarry"], w=["fC"])
        P.op("dve", lambda e: e.tensor_copy(out=carry[:], in_=C[:, T - 1:T]), r=["fC"], w=["fcarry"])
        P.op("dve", lambda e: e.tensor_copy(out=parts[0][:], in_=C[:]), r=["fC"], w=["fp0"])
        P.op("dve", lambda e: e.tensor_copy(out=A[:], in_=parts[0][:]), r=["fp0"], w=["fA"])
        P.op("dve", lambda e: e.tensor_tensor(out=B[:], in0=C[:], in1=A[:], op=ALU.subtract), r=["fC", "fA"], w=["fB"])
        P.op("dve", lambda e: e.tensor_copy(out=parts[1][:], in_=B[:]), r=["fB"], w=["fp1"])
        P.op("dve", lambda e: e.tensor_copy(out=A[:], in_=parts[1][:]), r=["fp1"], w=["fA"])
        P.op("dve", lambda e: e.tensor_tensor(out=C[:], in0=B[:], in1=A[:], op=ALU.subtract), r=["fB", "fA"], w=["fC"])
        P.op("dve", lambda e: e.tensor_copy(out=parts[2][:], in_=C[:]), r=["fC"], w=["fp2"])
        for i in range(3):
            P.op("dve", lambda e, i=i: e.tensor_scalar(out=parts[3 + i][:], in0=parts[i][:], scalar1=-1.0,
                                                       scalar2=None, op0=ALU.mult), r=["fp%d" % i], w=["fp%d" % (3 + i)])
        for i in range(6):
            P.dma("sp", FS[i, :, hsl], parts[i][:], r=["fp%d" % i], w=["FS"])

    OT = P.sb("OT", [128, KT, T], BF16)
    y_ps = P.ps("y_ps", [128, T])
    emit_attention(P, 16, qT_d, kT_d, V_d, OT, ones_b, kvalid_d, tri_d, None, fox=dict(FS=FS, ident=ident_d))
    emit_outproj_norm(P, OT, xT_d, wout_d, gain_d, xo_d, h2_d, y_ps, lambda c: "y_ps%d" % c, ones_f)
    P.finish()
    return nc


def mask_strip():
    k = np.arange(128)[:, None]
    cols = np.arange(16 * 128)[None, :]
    dl = cols - k
    m = ((dl >= 0) & (dl <= 128)).astype(np.float32)
    m += ((dl >= 0) & (dl % 4 == 0) & (dl <= 512))
    m += ((dl >= 0) & (dl % 16 == 0) & (dl <= 2048))
    return m


def build_p2_even():
    nc = _new_nc()
    T2 = 2 * T
    DA = 1024
    xT_d = dram_in(nc, "xT", [D, T], F32)
    qT_d = dram_in(nc, "qT", [DA, T], BF16)
    kT_d = dram_in(nc, "kT", [DA, T2], BF16)
    V_d = dram_in(nc, "V", [T2, DA], BF16)
    glu_d = dram_in(nc, "glu", [1024, T], F32)
    gh_d = dram_in(nc, "glu_halo", [1024, 30], F32)
    cw_d = dram_in(nc, "conv_w", [31, 1024], F32)
    cb_d = dram_in(nc, "conv_b", [1024], F32)
    lg_d = dram_in(nc, "ln_g", [1024], F32)
    lb_d = dram_in(nc, "ln_b", [1024], F32)
    kvalid_d = dram_in(nc, "kvalid", [128, 16], F32)
    mask_d = dram_in(nc, "mask", [128, 2048], BF16)
    tri_d = dram_in(nc, "tri", [128, 128], BF16)
    wout_d = dram_in(nc, "w_out", [D, D], F32)
    gain_d = dram_in(nc, "gain", [D], F32)
    xo_d = dram_out(nc, "xT_out", [D, T], F32)
    h2_d = dram_out(nc, "h2T", [D, T], BF16)

    P = Prog(nc)
    ones_f, ones_b = emit_consts(P)
    X = P.sb("X", [128, KT, T], F32)
    OT = P.sb("OT", [128, KT, T], BF16)
    y_ps = P.ps("y_ps", [128, T])
    mask = P.sb("mask", [128, 2048], BF16)
    P.dma("sp", mask[:], mask_d, w=["mask"])
    CWt = P.sb("CWt", [128, 31, 8], F32)
    for k in range(31):
        P.dma("sp", CWt[:, k, :], cw_d[k, :].rearrange("(ct p) -> p ct", p=128), w=["CWt"],
              allow_slow_non_contiguous=True)
    CP = P.sb("CP", [128, 3, 8], F32)
    for j, dd in enumerate((cb_d, lg_d, lb_d)):
        P.dma("sp", CP[:, j, :], dd.rearrange("(ct p) -> p ct", p=128), w=["CP"], allow_slow_non_contiguous=True)
    Gt = [P.sb("Gt%d" % i, [128, T + 30], F32) for i in range(2)]

    def conv_tile(ct):
        eng = "dve"
        gi = ct % 2
        P.dma("sp", Gt[gi][:, 0:30], gh_d[ct * 128:(ct + 1) * 128, :], w=["Gt%d_h" % gi])
        P.dma("sp", Gt[gi][:, 30:T + 30], glu_d[ct * 128:(ct + 1) * 128, :], w=["Gt%d" % gi])
        gb = ["Gt%d_h" % gi, "Gt%d" % gi]
        ab = "X%d" % (8 + ct)
        P.op(eng, lambda e, gi=gi, ct=ct: e.tensor_scalar(
            out=X[:, 8 + ct, :], in0=Gt[gi][:, 0:T], scalar1=CWt[:, 0, ct:ct + 1], scalar2=CP[:, 0, ct:ct + 1],
            op0=ALU.mult, op1=ALU.add), r=gb + ["CWt", "CP"], w=[ab])
        for k in range(1, 31):
            P.op(eng, lambda e, gi=gi, ct=ct, k=k: e.scalar_tensor_tensor(
                out=X[:, 8 + ct, :], in0=Gt[gi][:, k:k + T], scalar=CWt[:, k, ct:ct + 1], in1=X[:, 8 + ct, :],
                op0=ALU.mult, op1=ALU.add), r=gb + ["CWt", ab], w=[ab])

    ps = emit_attention(P, 8, qT_d, kT_d, V_d, OT, ones_b, kvalid_d, tri_d, None, mask=mask, per_head=conv_tile, pre_head=conv_prep)
    st_ps = ps["st_ps"]
    sqb = [P.sb("lsq%d" % i, [128, 512], F32) for i in range(2)]
    i = 0
    for c in range(2):
        cs = slice(c * 512, (c + 1) * 512)
        for ct in range(8):
            s_ = i % 2
            i += 1
            P.op("act", lambda e, ct=ct, s_=s_, cs=cs: e.activation(out=sqb[s_][:], in_=X[:, 8 + ct, cs], func=AF.Square),
                 r=["X%d" % (8 + ct)], w=["lsq%d" % s_])
            P.mm_group([lambda pe, a_, b_, ct=ct, cs=cs: pe.matmul(
                y_ps[:, cs], lhsT=ones_f[:], rhs=X[:, 8 + ct, cs], start=(ct == 0), stop=(ct == 7),
                skip_group_check=True)], r=["X%d" % (8 + ct), "ones_f"], w=["y_ps%d" % c])
            P.mm_group([lambda pe, a_, b_, ct=ct, s_=s_, c=c: pe.matmul(
                st_ps[c][:], lhsT=ones_f[:], rhs=sqb[s_][:], start=(ct == 0), stop=(ct == 7),
                skip_group_check=True)], r=["lsq%d" % s_, "ones_f"], w=["st_ps%d" % c])
    mean = P.sb("ln_mean", [128, T], F32)
    var = P.sb("ln_var", [128, T], F32)
    for c in range(2):
        cs = slice(c * 512, (c + 1) * 512)
        P.op("dve", lambda e, cs=cs: e.tensor_scalar(out=mean[:, cs], in0=y_ps[:, cs], scalar1=1.0 / 1024, scalar2=None,
                                                     op0=ALU.mult), r=["y_ps%d" % c], w=["ln_mean%d" % c])
        P.op("dve", lambda e, cs=cs: e.tensor_tensor(out=var[:, cs], in0=mean[:, cs], in1=mean[:, cs], op=ALU.mult),
             r=["ln_mean%d" % c], w=["ln_var%d" % c])
        P.op("dve", lambda e, cs=cs, c=c: e.scalar_tensor_tensor(
            out=var[:, cs], in0=st_ps[c][:], scalar=1.0 / 1024, in1=var[:, cs], op0=ALU.mult, op1=ALU.subtract),
            r=["st_ps%d" % c, "ln_var%d" % c], w=["ln_var%d" % c])
        P.op("act", lambda e, cs=cs: e.activation(out=var[:, cs], in_=var[:, cs], func=AF.Sqrt, bias=EPS),
             r=["ln_var%d" % c], w=["ln_var%d" % c])
        P.op("dve", lambda e, cs=cs: e.reciprocal(out=var[:, cs], in_=var[:, cs]),
             r=["ln_var%d" % c], w=["ln_var%d" % c])
    lt = [P.sb("lt%d" % i, [128, T], F32) for i in range(2)]
    stat = ["ln_mean0", "ln_mean1", "ln_var0", "ln_var1"]
    for ct in range(8):
        li = ct % 2
        P.op("dve", lambda e, ct=ct, li=li: e.tensor_tensor(out=lt[li][:], in0=X[:, 8 + ct, :], in1=mean[:], op=ALU.subtract),
             r=["X%d" % (8 + ct)] + stat, w=["lt%d" % li])
        P.op("dve", lambda e, li=li: e.tensor_tensor(out=lt[li][:], in0=lt[li][:], in1=var[:], op=ALU.mult),
             r=["lt%d" % li] + stat, w=["lt%d" % li])
        P.op("act", lambda e, ct=ct, li=li: e.activation(out=OT[:, 8 + ct, :], in_=lt[li][:], func=AF.Silu,
                                                         scale=CP[:, 1, ct:ct + 1], bias=CP[:, 2, ct:ct + 1]),
             r=["lt%d" % li, "CP"], w=["OT%d" % (8 + ct)])
    emit_outproj_norm(P, OT, xT_d, wout_d, gain_d, xo_d, h2_d, y_ps, lambda c: "y_ps%d" % c, ones_f, X=X)
    P.finish()
    return nc


def build_fused(upto=None):
    nc = bass.Bass("TRN2", target_bir_lowering=False)
    itn = lambda name, shape, dt: nc.dram_tensor(name, list(shape), dt).ap()
    specs = dict(
        xT=([D, T], F32), pos=([1, T], I32), kbias=([128, 16], F32), hmul=([128, 1], F32),
        rope_c=([32, 3], F32), rope_pm=([32, 32], F32), negtri=([128, 128], BF16), tri=([128, 128], BF16),
        ident=([128, 128], BF16), mask=([128, 2048], BF16),
        norm_mix=([4, 128, KT], F32), norm_ffn=([4, 128, KT], F32), norm_final=([128, KT], F32))
    for e_ in range(2):
        specs.update({"ev_w_in_%d" % e_: ([D, 5120], F32), "ev_conv_w_%d" % e_: ([128, 31, 8], F32),
                      "ev_conv_b_%d" % e_: ([128, 8], F32), "ev_ln_g_%d" % e_: ([128, 8], F32),
                      "ev_ln_b_%d" % e_: ([128, 8], F32), "ev_w_out_%d" % e_: ([D, D], F32),
                      "od_w_in_%d" % e_: ([D, 6160], F32), "od_b_f_%d" % e_: ([16], F32),
                      "od_w_out_%d" % e_: ([D, D], F32)})
    for l_ in range(4):
        specs.update({"ffn_w_up_%d" % l_: ([D, 2 * DFF], F32), "ffn_conv_w_%d" % l_: ([128, 3, NFT], F32),
                      "ffn_w_down_%d" % l_: ([DFF, D], F32)})

    class _Lazy(dict):
        def __missing__(self, name):
            shape, dt = specs[name]
            ap = nc.dram_tensor(name, list(shape), dt, kind="ExternalInput").ap()
            self[name] = ap
            return ap
    E = _Lazy()
    nc._ext_names = E
    out_d = nc.dram_tensor("out", [D, T], F32, kind="ExternalOutput").ap()
    XM, XN = itn("XM", [D, T], F32), itn("XN", [D, T], F32)
    HT, H2 = itn("HT", [D, T], BF16), itn("H2", [D, T], BF16)
    HL, HLG = itn("HL", [128, 2 * KT], BF16), itn("HLG", [256, 2 * KT], BF16)
    QTo, KTo, Vo = itn("QTo", [2048, T], BF16), itn("KTo", [2048, T], BF16), itn("Vo", [T, 2048], BF16)
    QTe, KTe, Ve = itn("QTe", [1024, T], BF16), itn("KTe", [1024, T], BF16), itn("Ve", [T, 1024], BF16)
    KG = [itn("KG%d" % j, [2048, T], BF16) for j in range(2)]
    VGo = [itn("VGo%d" % j, [1024, 2048], BF16) for j in range(2)]
    VGe = itn("VGe", [2048, 1024], BF16)
    LF, LFG = itn("LF", [16, T], F32), itn("LFG", [32, T], F32)
    GLU, GHL, GHLG = itn("GLU", [1024, T], F32), itn("GHL", [1024, 32], F32), itn("GHLG", [2048, 32], F32)
    FS = itn("FS", [6, 16, 2 * T], BF16)

    def dump(src):
        P = Prog(nc)
        P.dma("sp", out_d[0:src.shape[0], :], src, w=["dump"])
        P.finish()
        return nc

    f_p0(nc, dict(xT=E["xT"], gain=E["norm_mix"][0], hT=HT))
    xcur = E["xT"]
    for l in range(4):
        e = l // 2
        if upto == "p1_%d" % l:
            f_p1(nc, dict(hT=HT, w_in=E["ev_w_in_%d" % e], qT=QTe, kT=KTe, V=Ve, pos=E["pos"], rope_c=E["rope_c"],
                          rope_pm=E["rope_pm"], gluT=GLU, glu_hl=GHL, glu_hlg=GHLG,
                          k_xchg=[(KTe, KG[0])], v_xchg=[(Ve, VGe)]), True) if l % 2 == 0 else \
                f_p1(nc, dict(hT=HT, w_in=E["od_w_in_%d" % e], qT=QTo, kT=KTo, V=Vo,
                              b_f=E["od_b_f_%d" % e].rearrange("(h o) -> h o", o=1), lf=LF, lfg=LFG,
                              k_xchg=[(KTo[0:1024, :], KG[0]), (KTo[1024:2048, :], KG[1])],
                              v_xchg=[(Vo[0:512, :], VGo[0]), (Vo[512:1024, :], VGo[1])]), False)
            return dump(GLU if l % 2 == 0 else XM)
        if l % 2 == 0:
            f_p1(nc, dict(hT=HT, w_in=E["ev_w_in_%d" % e], qT=QTe, kT=KTe, V=Ve, pos=E["pos"], rope_c=E["rope_c"],
                          rope_pm=E["rope_pm"], gluT=GLU, glu_hl=GHL, glu_hlg=GHLG,
                          k_xchg=[(KTe, KG[0])], v_xchg=[(Ve, VGe)]), True)
            f_p2_even(nc, dict(xT=xcur, qT=QTe, kT=KTe, V=Ve, Kg=[KG[0]], Vg=[VGe], gluT=GLU, glu_hlg=GHLG,
                               conv_w=E["ev_conv_w_%d" % e], conv_b=E["ev_conv_b_%d" % e], ln_g=E["ev_ln_g_%d" % e],
                               ln_b=E["ev_ln_b_%d" % e], kbias=E["kbias"], hmul=E["hmul"], mask=E["mask"], tri=E["tri"],
                               ident=E["ident"],
                               w_out=E["ev_w_out_%d" % e], gain=E["norm_ffn"][l], xT_out=XM, h2T=H2, hl=HL, hlg=HLG))
        else:
            f_p1(nc, dict(hT=HT, w_in=E["od_w_in_%d" % e], qT=QTo, kT=KTo, V=Vo,
                          b_f=E["od_b_f_%d" % e].rearrange("(h o) -> h o", o=1), lf=LF, lfg=LFG,
                          k_xchg=[(KTo[0:1024, :], KG[0]), (KTo[1024:2048, :], KG[1])],
                          v_xchg=[(Vo[0:512, :], VGo[0]), (Vo[512:1024, :], VGo[1])]), False)
            f_p2_odd(nc, dict(xT=xcur, qT=QTo, kT=KTo, V=Vo, Kg=KG, Vg=VGo, lf=LF, lfg=LFG, FS=FS,
                              kbias=E["kbias"], negtri=E["negtri"], ident=E["ident"],
                              w_out=E["od_w_out_%d" % e], gain=E["norm_ffn"][l], xT_out=XM, h2T=H2, hl=HL, hlg=HLG))
        if upto == "p2_%d" % l:
            return dump(XM)
        final = (l == 3)
        f_p3(nc, dict(xT=XM, h2T=H2, halo=HLG[0:128, :], hmul=E["hmul"], w_up=E["ffn_w_up_%d" % l],
                      conv_w=E["ffn_conv_w_%d" % l], w_down=E["ffn_w_down_%d" % l],
                      gain_next=(E["norm_final"] if final else E["norm_mix"][l + 1]),
                      xT_out=XN, hT_out=(out_d if final else HT)), final)
        xcur = XN
        if upto == "p3_%d" % l:
            return dump(XN)
    return nc


_NC_CACHE = {}


def kernel(x, positions, norm_mix, norm_ffn, norm_final, ev_w_in, ev_conv_w, ev_conv_b, ev_ln_g, ev_ln_b,
           ev_w_out, od_w_in, od_b_f, od_w_out, ffn_w_up, ffn_conv_w, ffn_w_down):
    import ml_dtypes
    bf = ml_dtypes.bfloat16
    f32 = np.float32
    x = np.asarray(x, f32)
    positions = np.asarray(positions, np.int32)
    A = lambda a: np.ascontiguousarray(np.asarray(a, f32))
    if "fused" not in _NC_CACHE:
        _NC_CACHE["fused"] = build_fused()
    nc = _NC_CACHE["fused"]
    rc, pm = rope_consts()
    shared = dict(
        rope_c=rc, rope_pm=pm,
        negtri=np.where(np.arange(128)[None, :] >= np.arange(128)[:, None], 0.0, -30000.0).astype(bf),
        tri=(np.arange(128)[None, :] >= np.arange(128)[:, None]).astype(bf),
        ident=np.eye(128).astype(bf), mask=mask_strip().astype(bf),
        norm_mix=A(np.asarray(norm_mix, f32).reshape(4, KT, 128).transpose(0, 2, 1)),
        norm_ffn=A(np.asarray(norm_ffn, f32).reshape(4, KT, 128).transpose(0, 2, 1)),
        norm_final=A(np.asarray(norm_final, f32).reshape(KT, 128).T))
    pvec = lambda v: A(np.asarray(v, f32).reshape(8, 128).T)
    for e_ in range(2):
        for nm, arr in (("ev_w_in", ev_w_in), ("ev_conv_w", ev_conv_w), ("ev_conv_b", ev_conv_b), ("ev_ln_g", ev_ln_g),
                        ("ev_ln_b", ev_ln_b), ("ev_w_out", ev_w_out), ("od_w_in", od_w_in), ("od_b_f", od_b_f),
                        ("od_w_out", od_w_out)):
            a_ = np.asarray(arr)[e_]
            if nm == "ev_conv_w":
                a_ = np.asarray(a_, f32).reshape(31, 8, 128).transpose(2, 0, 1)
            elif nm in ("ev_conv_b", "ev_ln_g", "ev_ln_b"):
                a_ = pvec(a_)
            shared["%s_%d" % (nm, e_)] = A(a_)
    for l_ in range(4):
        for nm, arr in (("ffn_w_up", ffn_w_up), ("ffn_conv_w", ffn_conv_w), ("ffn_w_down", ffn_w_down)):
            a_ = np.asarray(arr)[l_]
            if nm == "ffn_conv_w":
                a_ = np.asarray(a_, f32).reshape(3, NFT, 128).transpose(2, 0, 1)
            shared["%s_%d" % (nm, l_)] = A(a_)
    kb_a = np.concatenate([np.full((128, 8), -30000.0, f32), np.zeros((128, 8), f32)], 1)
    kb_b = np.zeros((128, 16), f32)
    in_maps = []
    for c in range(NCORES):
        b, h = c // 2, c % 2
        m = dict(shared)
        m["xT"] = np.ascontiguousarray(x[b, h * T:(h + 1) * T, :].T)
        m["pos"] = np.ascontiguousarray(positions[b:b + 1, h * T:(h + 1) * T])
        m["kbias"] = kb_b if h == 1 else kb_a
        m["hmul"] = np.full((128, 1), float(h), f32)
        in_maps.append(m)
    used = set(nc._ext_names.keys())
    in_maps = [{k: v for k, v in m.items() if k in used} for m in in_maps]
    res = run_bass_kernel_spmd(nc, in_maps, core_ids=list(range(NCORES)))
    out = np.empty((4, 2 * T, D), f32)
    for c in range(NCORES):
        b, h = c // 2, c % 2
        out[b, h * T:(h + 1) * T, :] = np.asarray(res.results[c]["out"]).T
    return out


def f_p0(nc, io):
    P = Prog(nc)
    ones_f, ones_b = emit_consts(P)
    X = P.sb("X", [128, KT, T], F32)
    H = P.sb("H", [128, KT, T], BF16)
    xv = io["xT"].rearrange("(kt p) t -> p kt t", p=128)
    for kt in range(KT):
        P.dma("sp", X[:, kt, :], xv[:, kt, :], w=["X%d" % kt])
    emit_rmsnorm(P, X, lambda kt: "X%d" % kt, io["gain"], H, lambda kt: "H%d" % kt, ones_f, "n0")
    hv = io["hT"].rearrange("(kt p) t -> p kt t", p=128)
    for kt in range(KT):
        P.dma("sp", hv[:, kt, :], H[:, kt, :], r=["H%d" % kt], w=["hT_out"])
    P.finish()


def f_p1(nc, io, even):
    NH = 8 if even else 16
    DA = NH * HD
    hT_d, win_d = io["hT"], io["w_in"]
    qT_d, kT_d, V_d = io["qT"], io["kT"], io["V"]
    P = Prog(nc)
    H = P.sb("H", [128, KT, T], BF16)
    hv = hT_d.rearrange("(kt p) t -> p kt t", p=128)
    for kt in range(KT):
        P.dma("sp", H[:, kt, :], hv[:, kt, :], w=["H%d" % kt])
    hbufs = ["H%d" % kt for kt in range(KT)]
    winv = win_d.rearrange("(kt p) n -> p kt n", p=128)
    NWB = 4 if even else 6
    W = [P.sb("W%d" % i, [128, KT, 512], BF16) for i in range(NWB)]
    ps = [P.ps("ps%d" % i, [128, T]) for i in range(3)]
    pcount = [0]
    st = [P.sb("st%d" % i, [128, T], BF16) for i in range(3)]
    scount = [0]
    if even:
        tabs = emit_rope_tables(P, io["pos"], io["rope_c"])
        Pm = P.sb("Pm", [32, 32], F32)
        P.dma("sp", Pm[:], io["rope_pm"], w=["Pm"])
        qs32 = [P.sb("qs32_%d" % i, [32, T], F32) for i in range(2)]
        t1 = [P.sb("rt1_%d" % i, [32, T], F32) for i in range(2)]
        t2 = [P.sb("rt2_%d" % i, [32, T], F32) for i in range(2)]
        sw_ps = P.ps("sw_ps", [32, T])
        rcount = [0]

    groups = []
    for g in range(DA // 512):
        groups.append(("k", DA + g * 512, 512, g))
    for g in range(DA // 512):
        groups.append(("v", 2 * DA + g * 512, 512, g))
    groups.append(("xchg", 0, 0, 0))
    for g in range(DA // 512):
        groups.append(("q", g * 512, 512, g))
    if even:
        for g in range(2):
            groups.append(("glu_a", 3 * DA + g * 512, 512, g))
            groups.append(("glu_g", 3 * DA + 1024 + g * 512, 512, g))
    else:
        groups.append(("f", 3 * DA, 16, 0))
    wgroups = [g for g in groups if g[0] != "xchg"]
    slot_of = {}

    def load(i):
        if i >= len(wgroups):
            return
        kind, c0, ncol, _ = wgroups[i]
        b = i % NWB
        slot_of[i] = b
        P.dma("pool", W[b][:, :, 0:ncol], winv[:, :, c0:c0 + ncol], w=["W%d" % b])

    def feat_tile(b, i):
        pi = pcount[0] % 3
        pcount[0] += 1
        for c in range(2):
            cs = slice(c * 512, (c + 1) * 512)
            P.mm_group([lambda pe, s_, e_, kt=kt, cs=cs, b=b, i=i, pi=pi: pe.matmul(
                ps[pi][:, cs], lhsT=W[b][:, kt, i * 128:(i + 1) * 128], rhs=H[:, kt, cs], start=s_, stop=e_)
                for kt in range(KT)], r=hbufs + ["W%d" % b], w=["ps%d_%d" % (pi, c)])
        return pi

    def psb(pi):
        return ["ps%d_0" % pi, "ps%d_1" % pi]

    def qk_group(which, b, g):
        out_d = qT_d if which == "q" else kT_d
        for i in range(4):
            h = g * 4 + i
            pi = feat_tile(b, i)
            si = scount[0] % 3
            scount[0] += 1
            sc = SCALE if which == "q" else 1.0
            if even:
                ri = rcount[0] % 2
                rcount[0] += 1
                P.op("dve", lambda e, pi=pi, ri=ri: e.tensor_copy(out=qs32[ri][:], in_=ps[pi][0:32, :]),
                     r=psb(pi), w=["qs32_%d" % ri])
                for c in range(2):
                    cs = slice(c * 512, (c + 1) * 512)
                    P.mm_group([lambda pe, s_, e_, cs=cs, ri=ri: pe.matmul(
                        sw_ps[:, cs], lhsT=Pm[:], rhs=qs32[ri][:, cs], start=True, stop=True)],
                        r=["qs32_%d" % ri, "Pm"], w=["sw_ps%d" % c])
                ct = tabs["cos" + which]
                sn = tabs["sin" + which]
                P.op("dve", lambda e, ri=ri, ct=ct: e.tensor_tensor(out=t1[ri][:], in0=qs32[ri][:], in1=ct[:], op=ALU.mult),
                     r=["qs32_%d" % ri, "tab_cos" + which], w=["rt1_%d" % ri])
                P.op("dve", lambda e, ri=ri, sn=sn: e.tensor_tensor(out=t2[ri][:], in0=sw_ps[:], in1=sn[:], op=ALU.mult),
                     r=["sw_ps0", "sw_ps1", "tab_sin" + which], w=["rt2_%d" % ri])
                P.op("act", lambda e, pi=pi, si=si, sc=sc: e.activation(out=st[si][:], in_=ps[pi][:], func=AF.Copy, scale=sc),
                     r=psb(pi), w=["st%d_lo" % si, "st%d_hi" % si])
                P.op("dve", lambda e, ri=ri, si=si: e.tensor_tensor(out=st[si][0:32, :], in0=t1[ri][:], in1=t2[ri][:], op=ALU.add),
                     r=["rt1_%d" % ri, "rt2_%d" % ri], w=["st%d_lo" % si])
            else:
                P.op("act", lambda e, pi=pi, si=si, sc=sc: e.activation(out=st[si][:], in_=ps[pi][:], func=AF.Copy, scale=sc),
                     r=psb(pi), w=["st%d_lo" % si, "st%d_hi" % si])
            P.dma("sp", out_d[h * 128:(h + 1) * 128, :], st[si][:], r=["st%d_lo" % si, "st%d_hi" % si],
                  w=[which + "T_out"])

    vst = [P.sb("vst%d" % i, [128, 512], BF16) for i in range(3)]
    vcount = [0]

    def v_group(b, g):
        for tt in range(T // 128):
            pi = pcount[0] % 3
            pcount[0] += 1
            P.mm_group([lambda pe, s_, e_, kt=kt, b=b, tt=tt, pi=pi: pe.matmul(
                ps[pi][:, 0:512], lhsT=H[:, kt, tt * 128:(tt + 1) * 128], rhs=W[b][:, kt, :], start=s_, stop=e_)
                for kt in range(KT)], r=hbufs + ["W%d" % b], w=["ps%d_0" % pi])
            vi = vcount[0] % 3
            vcount[0] += 1
            P.op("dve", lambda e, pi=pi, vi=vi: e.tensor_copy(out=vst[vi][:], in_=ps[pi][:, 0:512]),
                 r=["ps%d_0" % pi], w=["vst%d" % vi])
            P.dma("sp", V_d[tt * 128:(tt + 1) * 128, g * 512:(g + 1) * 512], vst[vi][:], r=["vst%d" % vi], w=["V_out"])

    if even:
        sg = [P.sb("sg%d" % i, [128, T], F32) for i in range(2)]
        gl = [P.sb("gl%d" % i, [128, T], F32) for i in range(2)]

    def glu_group(ba, bg, g):
        for i in range(4):
            ct_ = g * 4 + i
            pa = feat_tile(ba, i)
            pg = feat_tile(bg, i)
            s_ = ct_ % 2
            P.op("act", lambda e, pg=pg, s_=s_: e.activation(out=sg[s_][:], in_=ps[pg][:], func=AF.Sigmoid),
                 r=psb(pg), w=["sg%d" % s_])
            P.op("dve", lambda e, pa=pa, s_=s_: e.tensor_tensor(out=gl[s_][:], in0=ps[pa][:], in1=sg[s_][:], op=ALU.mult),
                 r=psb(pa) + ["sg%d" % s_], w=["gl%d" % s_])
            P.dma("sp", io["gluT"][ct_ * 128:(ct_ + 1) * 128, :], gl[s_][:], r=["gl%d" % s_], w=["glu_out"])
            P.dma("sp", io["glu_hl"][ct_ * 128:(ct_ + 1) * 128, :], gl[s_][:, T - 32:T], r=["gl%d" % s_], w=["gluh_out"])

    def f_group(b):
        BFt = P.sb("BFt", [16, 1], F32)
        P.dma("sp", BFt[:], io["b_f"], w=["BFt"])
        NB = P.sb("NB", [16, 1], F32)
        P.op("dve", lambda e: e.tensor_scalar(out=NB[:], in0=BFt[:], scalar1=-1.0, scalar2=None, op0=ALU.mult),
             r=["BFt"], w=["NB"])
        pi = pcount[0] % 3
        pcount[0] += 1
        for c in range(2):
            cs = slice(c * 512, (c + 1) * 512)
            P.mm_group([lambda pe, s_, e_, kt=kt, cs=cs, b=b, pi=pi: pe.matmul(
                ps[pi][0:16, cs], lhsT=W[b][:, kt, 0:16], rhs=H[:, kt, cs], start=s_, stop=e_)
                for kt in range(KT)], r=hbufs + ["W%d" % b], w=["ps%d_%d" % (pi, c)])
        e1 = P.sb("e1", [16, T], F32)
        l1 = P.sb("l1", [16, T], F32)
        P.op("act", lambda e, pi=pi: e.activation(out=e1[:], in_=ps[pi][0:16, :], func=AF.Exp, scale=-1.0, bias=NB[:]),
             r=psb(pi) + ["NB"], w=["e1"])
        P.op("act", lambda e: e.activation(out=l1[:], in_=e1[:], func=AF.Ln, bias=1.0), r=["e1"], w=["l1"])
        P.op("dve", lambda e: e.tensor_scalar(out=l1[:], in0=l1[:], scalar1=-1.0, scalar2=None, op0=ALU.mult),
             r=["l1"], w=["l1"])
        P.dma("sp", io["lf"], l1[:], r=["l1"], w=["lf_out"])

    def xchg():
        for j, (src, dst) in enumerate(io["k_xchg"]):
            P.coll(src, dst, r=["kT_out"], w=["Kg%d" % j])
        for j, (src, dst) in enumerate(io["v_xchg"]):
            P.coll(src, dst, r=["V_out"], w=["Vg%d" % j])

    PF = NWB - 1
    for i in range(PF):
        load(i)
    wi = 0
    pend_a = None
    xdone = False
    for grp in groups:
        kind = grp[0]
        if kind == "xchg":
            xchg()
            continue
        load(wi + PF)
        b = slot_of[wi]
        if kind in ("q", "k"):
            qk_group(kind, b, grp[3])
        elif kind == "v":
            v_group(b, grp[3])
        elif kind == "glu_a":
            pend_a = b
        elif kind == "glu_g":
            glu_group(pend_a, b, grp[3])
        elif kind == "f":
            f_group(b)
        wi += 1
    if even:
        P.coll(io["glu_hl"], io["glu_hlg"], r=["gluh_out"], w=["gluhg"])
    else:
        P.coll(io["lf"], io["lfg"], r=["lf_out"], w=["lfg"])
    P.finish()


def f_p2_odd(nc, io):
    P = Prog(nc)
    ones_f, ones_b = emit_consts(P)
    FS = io["FS"]
    A = P.sb("fA", [16, T], F32)
    B = P.sb("fB", [16, T], F32)
    C = P.sb("fC", [16, T], F32)
    carry = P.sb("fcarry", [16, 1], F32)
    P.op("dve", lambda e: e.memset(carry[:], 0.0), w=["fcarry"])
    parts = [P.sb("fp%d" % i, [16, T], BF16) for i in range(6)]
    for half in range(2):
        hsl = slice(half * T, (half + 1) * T)
        src = io["lfg"][0:16, :] if half == 0 else io["lf"]
        P.dma("sp", A[:], src, w=["fA"])
        P.op("dve", lambda e: e.memset(B[:], 1.0), w=["fB"])
        P.op("dve", lambda e: e.tensor_tensor_scan(out=C[:], data0=B[:], data1=A[:], initial=carry[:],
                                                   op0=ALU.mult, op1=ALU.add),
             r=["fA", "fB", "fcarry"], w=["fC"])
        P.op("dve", lambda e: e.tensor_copy(out=carry[:], in_=C[:, T - 1:T]), r=["fC"], w=["fcarry"])
        P.op("dve", lambda e: e.tensor_copy(out=parts[0][:], in_=C[:]), r=["fC"], w=["fp0"])
        P.op("dve", lambda e: e.tensor_copy(out=A[:], in_=parts[0][:]), r=["fp0"], w=["fA"])
        P.op("dve", lambda e: e.tensor_tensor(out=B[:], in0=C[:], in1=A[:], op=ALU.subtract), r=["fC", "fA"], w=["fB"])
        P.op("dve", lambda e: e.tensor_copy(out=parts[1][:], in_=B[:]), r=["fB"], w=["fp1"])
        P.op("dve", lambda e: e.tensor_copy(out=A[:], in_=parts[1][:]), r=["fp1"], w=["fA"])
        P.op("dve", lambda e: e.tensor_tensor(out=C[:], in0=B[:], in1=A[:], op=ALU.subtract), r=["fB", "fA"], w=["fC"])
        P.op("dve", lambda e: e.tensor_copy(out=parts[2][:], in_=C[:]), r=["fC"], w=["fp2"])
        for i in range(3):
            P.op("dve", lambda e, i=i: e.tensor_scalar(out=parts[3 + i][:], in0=parts[i][:], scalar1=-1.0,
                                                       scalar2=None, op0=ALU.mult), r=["fp%d" % i], w=["fp%d" % (3 + i)])
        for i in range(6):
            P.dma("sp", FS[i, :, hsl], parts[i][:], r=["fp%d" % i], w=["FS"])
    OT = P.sb("OT", [128, KT, T], BF16)
    y_ps = P.ps("y_ps", [128, T])
    ksrc, vsrc = kv_sources(io["kT"], io["V"], io["Kg"], io["Vg"], 512)
    emit_attention(P, 16, io["qT"], ksrc, vsrc, OT, ones_b, io["kbias"], io["negtri"],
                   fox=dict(FS=FS, ident=io["ident"]))
    emit_outproj_norm(P, OT, io["xT"], io["w_out"], io["gain"], io["xT_out"], io["h2T"], y_ps,
                      lambda c: "y_ps%d" % c, ones_f, halo=(io["hl"], io["hlg"]))
    P.finish()


def f_p2_even(nc, io):
    P = Prog(nc)
    ones_f, ones_b = emit_consts(P)
    X = P.sb("X", [128, KT, T], F32)
    OT = P.sb("OT", [128, KT, T], BF16)
    y_ps = P.ps("y_ps", [128, T])
    mask = P.sb("mask", [128, 2048], BF16)
    P.dma("sp", mask[:], io["mask"], w=["mask"])
    HM = P.sb("HM", [128, 1], F32)
    P.dma("sp", HM[:], io["hmul"], w=["HM"])
    CWt = P.sb("CWt", [128, 31, 8], F32)
    P.dma("sp", CWt[:], io["conv_w"], w=["CWt"])
    CP = P.sb("CP", [128, 3, 8], F32)
    for j, dd in enumerate((io["conv_b"], io["ln_g"], io["ln_b"])):
        P.dma("sp", CP[:, j, :], dd, w=["CP"])
    Gt = [P.sb("Gt%d" % i, [128, T + 30], BF16) for i in range(2)]
    Dg = [P.sb("Dg%d" % i, [128, 31, 128], BF16) for i in range(2)]
    identc = P.sb("identc", [128, 128], BF16)
    P.dma("sp", identc[:], io["ident"], w=["identc"])
    glu_d, ghg = io["gluT"], io["glu_hlg"]

    def conv_prep(ct):
        gi = ct % 2
        P.dma("pool", Gt[gi][:, 0:30], ghg[ct * 128:(ct + 1) * 128, 2:32], w=["Gt%d_h" % gi])
        P.dma("pool", Gt[gi][:, 30:T + 30], glu_d[ct * 128:(ct + 1) * 128, :], w=["Gt%d" % gi])
        P.op("dve", lambda e, gi=gi: e.tensor_scalar(out=Gt[gi][:, 0:30], in0=Gt[gi][:, 0:30], scalar1=HM[:, 0:1],
                                                     scalar2=None, op0=ALU.mult), r=["Gt%d_h" % gi, "HM"], w=["Gt%d_h" % gi])
        for k in range(31):
            P.op("dve", lambda e, gi=gi, ct=ct, k=k: e.tensor_scalar(
                out=Dg[gi][:, k, :], in0=identc[:], scalar1=CWt[:, k, ct:ct + 1], scalar2=None, op0=ALU.mult),
                r=["identc", "CWt"], w=["Dg%d" % gi])

    def conv_tile(ct):
        gi = ct % 2
        gb = ["Gt%d_h" % gi, "Gt%d" % gi, "Dg%d" % gi]
        ab = "X%d" % (8 + ct)
        for c in range(2):
            P.mm_group([lambda pe, s_, e_, gi=gi, k=k, c=c: pe.matmul(
                y_ps[:, c * 512:(c + 1) * 512], lhsT=Dg[gi][:, k, :], rhs=Gt[gi][:, k + c * 512:k + c * 512 + 512],
                start=s_, stop=e_) for k in range(31)], r=gb, w=["y_ps%d" % c])
            P.op("act", lambda e, ct=ct, c=c: e.activation(
                out=X[:, 8 + ct, c * 512:(c + 1) * 512], in_=y_ps[:, c * 512:(c + 1) * 512], func=AF.Identity,
                bias=CP[:, 0, ct:ct + 1]), r=["y_ps%d" % c, "CP"], w=[ab])

    ksrc, vsrc = kv_sources(io["kT"], io["V"], io["Kg"], io["Vg"], 1024)
    ps = emit_attention(P, 8, io["qT"], ksrc, vsrc, OT, ones_b, io["kbias"], io["tri"], mask=mask, per_head=conv_tile, pre_head=conv_prep)
    st_ps = ps["st_ps"]
    sqb = [P.sb("lsq%d" % i, [128, 512], F32) for i in range(2)]
    i = 0
    for c in range(2):
        cs = slice(c * 512, (c + 1) * 512)
        for ct in range(8):
            s_ = i % 2
            i += 1
            P.op("act", lambda e, ct=ct, s_=s_, cs=cs: e.activation(out=sqb[s_][:], in_=X[:, 8 + ct, cs], func=AF.Square),
                 r=["X%d" % (8 + ct)], w=["lsq%d" % s_])
            P.mm_group([lambda pe, a_, b_, ct=ct, cs=cs: pe.matmul(
                y_ps[:, cs], lhsT=ones_f[:], rhs=X[:, 8 + ct, cs], start=(ct == 0), stop=(ct == 7),
                skip_group_check=True)], r=["X%d" % (8 + ct), "ones_f"], w=["y_ps%d" % c])
            P.mm_group([lambda pe, a_, b_, ct=ct, s_=s_, c=c: pe.matmul(
                st_ps[c][:], lhsT=ones_f[:], rhs=sqb[s_][:], start=(ct == 0), stop=(ct == 7),
                skip_group_check=True)], r=["lsq%d" % s_, "ones_f"], w=["st_ps%d" % c])
    mean = P.sb("ln_mean", [128, T], F32)
    var = P.sb("ln_var", [128, T], F32)
    for c in range(2):
        cs = slice(c * 512, (c + 1) * 512)
        P.op("dve", lambda e, cs=cs: e.tensor_scalar(out=mean[:, cs], in0=y_ps[:, cs], scalar1=1.0 / 1024, scalar2=None,
                                                     op0=ALU.mult), r=["y_ps%d" % c], w=["ln_mean%d" % c])
        P.op("dve", lambda e, cs=cs: e.tensor_tensor(out=var[:, cs], in0=mean[:, cs], in1=mean[:, cs], op=ALU.mult),
             r=["ln_mean%d" % c], w=["ln_var%d" % c])
        P.op("dve", lambda e, cs=cs, c=c: e.scalar_tensor_tensor(
            out=var[:, cs], in0=st_ps[c][:], scalar=1.0 / 1024, in1=var[:, cs], op0=ALU.mult, op1=ALU.subtract),
            r=["st_ps%d" % c, "ln_var%d" % c], w=["ln_var%d" % c])
        P.op("act", lambda e, cs=cs: e.activation(out=var[:, cs], in_=var[:, cs], func=AF.Sqrt, bias=EPS),
             r=["ln_var%d" % c], w=["ln_var%d" % c])
        P.op("dve", lambda e, cs=cs: e.reciprocal(out=var[:, cs], in_=var[:, cs]),
             r=["ln_var%d" % c], w=["ln_var%d" % c])
    lt = [P.sb("lt%d" % i, [128, T], F32) for i in range(2)]
    stat = ["ln_mean0", "ln_mean1", "ln_var0", "ln_var1"]
    for ct in range(8):
        li = ct % 2
        P.op("dve", lambda e, ct=ct, li=li: e.tensor_tensor(out=lt[li][:], in0=X[:, 8 + ct, :], in1=mean[:], op=ALU.subtract),
             r=["X%d" % (8 + ct)] + stat, w=["lt%d" % li])
        P.op("dve", lambda e, li=li: e.tensor_tensor(out=lt[li][:], in0=lt[li][:], in1=var[:], op=ALU.mult),
             r=["lt%d" % li] + stat, w=["lt%d" % li])
        P.op("act", lambda e, ct=ct, li=li: e.activation(out=OT[:, 8 + ct, :], in_=lt[li][:], func=AF.Silu,
                                                         scale=CP[:, 1, ct:ct + 1], bias=CP[:, 2, ct:ct + 1]),
             r=["lt%d" % li, "CP"], w=["OT%d" % (8 + ct)])
    emit_outproj_norm(P, OT, io["xT"], io["w_out"], io["gain"], io["xT_out"], io["h2T"], y_ps,
                      lambda c: "y_ps%d" % c, ones_f, X=X, halo=((io["hl"], io["hlg"]) if "hl" in io else None))
    P.finish()
```

```python
import numpy as np
from contextlib import ExitStack
import concourse.bass as bass
import concourse.mybir as mybir
from concourse.bass_utils import run_bass_kernel_spmd

F32 = mybir.dt.float32
BF16 = mybir.dt.bfloat16
I32 = mybir.dt.int32
AF = mybir.ActivationFunctionType
ALU = mybir.AluOpType

D = 2048
T = 1024
KT = D // 128
DFF = 5632
NFT = DFF // 128
EPS = 1e-6
NCORES = 8


class Prog:
    ENGS = ("pe", "act", "dve", "pool", "sp")
    UID = [0]
    POOL = {}

    def __init__(self, nc):
        self.nc = nc
        Prog.UID[0] += 1
        self.uid = "p%d_" % Prog.UID[0]
        self.coll_cnt = {}
        self.ops = {e: [] for e in self.ENGS}
        self.nsig = {e: 0 for e in self.ENGS}
        self.dma_cnt = {}
        self.writer = {}
        self.readers = {}
        self.stack = ExitStack()

    def sb(self, name, shape, dt):
        return self.stack.enter_context(self.nc.sbuf_tensor(self.uid + "sb_" + name, list(shape), dt))

    def ps(self, name, shape, dt=F32):
        return self.stack.enter_context(self.nc.psum_tensor(self.uid + "pp_" + name, list(shape), dt))

    @staticmethod
    def _is_psum(b):
        return ("ps" in b) or b.startswith("ss_")

    def _deps(self, r, w):
        toks = []
        for b in r:
            if b in self.writer:
                toks.append(self.writer[b])
            if self._is_psum(b):
                toks.extend(self.readers.get(b, []))
        for b in w:
            if b in self.writer:
                toks.append(self.writer[b])
            toks.extend(self.readers.get(b, []))
        return toks

    def _commit(self, tok, r, w):
        for b in r:
            self.readers.setdefault(b, []).append(tok)
        for b in w:
            self.writer[b] = tok
            self.readers[b] = []

    def op(self, eng, fn, r=(), w=(), extra=()):
        waits = self._deps(r, w) + list(extra)
        self.nsig[eng] += 1
        tok = ("E:" + eng, self.nsig[eng])
        self.ops[eng].append((fn, waits, tok))
        self._commit(tok, r, w)
        return tok

    def mm_group(self, mms, r=(), w=()):
        waits = self._deps(r, w)
        n = len(mms)
        self.nsig["pe"] += 1
        tok = ("E:pe", self.nsig["pe"])
        for i, fn in enumerate(mms):
            f = (lambda pe, fn=fn, i=i: fn(pe, i == 0, i == n - 1))
            self.ops["pe"].append((f, waits if i == 0 else [], tok if i == n - 1 else None))
        self._commit(tok, r, w)
        return tok

    def dma(self, queue, out, in_, r=(), w=(), key=None, **kw):
        waits = self._deps(r, w)
        if key is None:
            key = w[0] if w else r[0]
        self.dma_cnt[key] = self.dma_cnt.get(key, 0) + 1
        tok = ("D:" + str(key), 16 * self.dma_cnt[key])
        fn = (lambda e: e.dma_start(out=out, in_=in_, **kw))
        self.ops[queue].append((fn, waits, tok))
        self._commit(tok, r, w)
        return tok

    def coll(self, in_ap, out_ap, r=(), w=(), key=None):
        waits = self._deps(r, w)
        if key is None:
            key = w[0]
        self.coll_cnt[key] = self.coll_cnt.get(key, 0) + 1
        tok = ("C:" + str(key), self.coll_cnt[key])
        fn = (lambda e: e.collective_compute("AllGather", ALU.bypass, replica_groups=PAIRS,
                                             ins=[in_ap], outs=[out_ap]))
        self.ops["pool"].append((fn, waits, tok))
        self._commit(tok, r, w)
        return tok

    def check_deadlock(self):
        pos = {e: 0 for e in self.ENGS}
        val = {}
        progress = True
        while progress:
            progress = False
            for e in self.ENGS:
                ops = self.ops[e]
                while pos[e] < len(ops):
                    fn, waits, tok = ops[pos[e]]
                    ok = all((k == "E:pe" and e == "pe") or val.get(k, 0) >= v for (k, v) in waits)
                    if not ok:
                        break
                    if tok is not None:
                        k, v = tok
                        val[k] = val.get(k, 0) + (16 if k.startswith("D:") else 1)
                    pos[e] += 1
                    progress = True
        stuck = {e: (pos[e], len(self.ops[e])) for e in self.ENGS if pos[e] < len(self.ops[e])}
        if stuck:
            msg = []
            for e, (p, n) in stuck.items():
                fn, waits, tok = self.ops[e][p]
                bad = [(k, v, val.get(k, 0)) for (k, v) in waits if val.get(k, 0) < v]
                msg.append("%s stuck at %d/%d waiting %s (tok %s)" % (e, p, n, bad, tok))
            raise RuntimeError("DEADLOCK: " + "; ".join(msg))

    def finish(self, final_tokens=None):
        self.check_deadlock()
        nc = self.nc
        semkeys = (["E:" + e for e in self.ENGS] + ["D:" + str(k) for k in self.dma_cnt]
                   + ["C:" + str(k) for k in self.coll_cnt])
        pool = Prog.POOL
        if pool.get("nc") is not nc:
            pool.clear()
            pool.update(nc=nc, handles=[], base=[], stack=ExitStack())
        while len(pool["handles"]) < len(semkeys):
            pool["handles"].append(pool["stack"].enter_context(nc.semaphore("gs%d" % len(pool["handles"]))))
            pool["base"].append(0)
        slot = {k: i for i, k in enumerate(semkeys)}
        base = {k: pool["base"][slot[k]] for k in semkeys}

        class _Sems(dict):
            pass
        sems = {k: pool["handles"][slot[k]] for k in semkeys}
        totals = {}
        for e_ in self.ENGS:
            totals["E:" + e_] = self.nsig[e_]
        for k, c in self.dma_cnt.items():
            totals["D:" + str(k)] = 16 * c
        for k, c in self.coll_cnt.items():
            totals["C:" + str(k)] = c
        fin = [("D:" + str(k), 16 * c) for k, c in self.dma_cnt.items()]
        fin += [("C:" + str(k), c) for k, c in self.coll_cnt.items()]
        eng_of = {"pe": "tensor", "act": "scalar", "dve": "vector", "pool": "gpsimd", "sp": "sync"}
        with nc.Block() as block:
            for ename in self.ENGS:
                ops = self.ops[ename]
                extra_fin = fin if ename == "sp" else []

                def body(e, ops=ops, ename=ename, extra_fin=extra_fin):
                    seen = {}
                    for fn, waits, tok in ops:
                        for (k, v) in waits:
                            if k == "E:pe" and ename == "pe":
                                continue
                            if seen.get(k, 0) >= v:
                                continue
                            e.wait_ge(sems[k], base[k] + v)
                            seen[k] = v
                        ins = fn(e)
                        if tok is not None:
                            k, v = tok
                            ins.then_inc(sems[k], 16 if k.startswith("D:") else 1)
                    for (k, v) in extra_fin:
                        if seen.get(k, 0) < v:
                            e.wait_ge(sems[k], base[k] + v)

                getattr(block, eng_of[ename])(body)
        for k in semkeys:
            pool["base"][slot[k]] += totals[k]
        self.stack.close()


_FUSED = {"nc": None, "io": {}}


def _new_nc():
    if _FUSED["nc"] is not None:
        return _FUSED["nc"]
    return bass.Bass("TRN2", target_bir_lowering=False)


def dram_in(nc, name, shape, dt):
    if _FUSED["nc"] is not None:
        ap = _FUSED["io"][name]
        assert list(ap.shape) == list(shape), (name, ap.shape, shape)
        return ap
    return nc.dram_tensor(name, list(shape), dt, kind="ExternalInput").ap()


def dram_out(nc, name, shape, dt):
    if _FUSED["nc"] is not None:
        ap = _FUSED["io"][name]
        assert list(ap.shape) == list(shape), (name, ap.shape, shape)
        return ap
    return nc.dram_tensor(name, list(shape), dt, kind="ExternalOutput").ap()


PAIRS = [[0, 1], [2, 3], [4, 5], [6, 7]]


def emit_consts(P):
    ones_f = P.sb("ones_f", [128, 128], F32)
    ones_b = P.sb("ones_b", [128, 128], BF16)
    P.op("dve", lambda e: e.memset(ones_f[:], 1.0), w=["ones_f"])
    P.op("dve", lambda e: e.memset(ones_b[:], 1.0), w=["ones_b"])
    return ones_f, ones_b


def emit_rmsnorm(P, X, xbuf, gain_d, H, hbuf, ones_f, tag, ncols=T, ss_ps=None, ssbuf=None):
    nc = P.nc
    G = P.sb("G_" + tag, [128, KT], F32)
    P.dma("sp", G[:], gain_d, w=["G_" + tag])
    sq = [P.sb("sq%d_%s" % (i, tag), [128, 512], BF16) for i in range(2)]
    ones_sq = P.sb("ones_sq_" + tag, [128, 128], BF16)
    P.op("dve", lambda e: e.memset(ones_sq[:], 1.0), w=["ones_sq_" + tag])
    if ss_ps is None:
        ss_ps = P.ps("ss_" + tag, [128, ncols])
    if ssbuf is None:
        ssbuf = lambda c: "ss_%s_%d" % (tag, c)
    rstd = P.sb("rstd_" + tag, [128, ncols], F32)
    nch = ncols // 512
    i = 0
    for c in range(nch):
        cs = slice(c * 512, (c + 1) * 512)
        mms = []
        for kt in range(KT):
            s = i % 2
            i += 1
            P.op("act", lambda e, kt=kt, s=s, cs=cs: e.activation(out=sq[s][:], in_=X[:, kt, cs], func=AF.Square),
                 r=[xbuf(kt)], w=["sq%d_%s" % (s, tag)])
            P.mm_group([lambda pe, st, sp_, s=s, cs=cs, kt=kt: pe.matmul(
                ss_ps[:, cs], lhsT=ones_sq[:], rhs=sq[s][:], start=(kt == 0), stop=(kt == KT - 1),
                skip_group_check=True)],
                r=["sq%d_%s" % (s, tag), "ones_sq_" + tag], w=[ssbuf(c)])
        P.op("act", lambda e, cs=cs: e.activation(out=rstd[:, cs], in_=ss_ps[:, cs], func=AF.Sqrt,
                                                  scale=1.0 / D, bias=EPS),
             r=[ssbuf(c)], w=["rstd_%s_%d" % (tag, c)])
        P.op("dve", lambda e, cs=cs: e.reciprocal(out=rstd[:, cs], in_=rstd[:, cs]),
             r=["rstd_%s_%d" % (tag, c)], w=["rstd_%s_%d" % (tag, c)])
    for kt in range(KT):
        P.op("dve", lambda e, kt=kt: e.scalar_tensor_tensor(
            out=H[:, kt, :], in0=X[:, kt, :], scalar=G[:, kt:kt + 1], in1=rstd[:],
            op0=ALU.mult, op1=ALU.mult),
            r=[xbuf(kt), "G_" + tag] + ["rstd_%s_%d" % (tag, c) for c in range(nch)], w=[hbuf(kt)])


def f_p3(nc, io, final):
    xT_d, h2_d, halo_d = io["xT"], io["h2T"], io["halo"]
    wup_d, cw_d, wdn_d, gn_d = io["w_up"], io["conv_w"], io["w_down"], io["gain_next"]
    xo_d = None if final else io["xT_out"]
    ho_d = io["hT_out"]

    P = Prog(nc)
    ones_f, ones_b = emit_consts(P)
    X = P.sb("X", [128, KT, T], F32)
    H = P.sb("H", [128, KT, T], BF16)
    HH = P.sb("HH", [128, KT, 2], BF16)
    xv = xT_d.rearrange("(kt p) t -> p kt t", p=128)
    hv = h2_d.rearrange("(kt p) t -> p kt t", p=128)
    for kt in range(KT):
        P.dma("sp", H[:, kt, :], hv[:, kt, :], w=["H%d" % kt])
    P.dma("sp", HH[:], halo_d.rearrange("p (kt t) -> p kt t", t=2), w=["HH"])
    HM = P.sb("HM", [128, 1], F32)
    P.dma("sp", HM[:], io["hmul"], w=["HM"])
    P.op("dve", lambda e: e.tensor_scalar(out=HH[:], in0=HH[:], scalar1=HM[:, 0:1], scalar2=None, op0=ALU.mult),
         r=["HH", "HM"], w=["HH"])
    for kt in range(KT):
        P.dma("sp", X[:, kt, :], xv[:, kt, :], w=["X%d" % kt])
    CW = P.sb("CW", [128, 3, NFT], F32)
    P.dma("sp", CW[:], cw_d, w=["CW"])

    GW = 2
    NG = NFT // GW
    NBUF = 2
    Wg = [P.sb("Wg%d" % i, [128, KT, 128 * GW], BF16) for i in range(NBUF)]
    Wu = [P.sb("Wu%d" % i, [128, KT, 128 * GW], BF16) for i in range(NBUF)]
    Wd = [P.sb("Wd%d" % i, [128, GW, D], BF16) for i in range(NBUF)]
    M = [P.sb("M%d" % i, [128, GW, T], BF16) for i in range(2)]
    Gs = [P.sb("Gs%d" % i, [128, T + 2], F32) for i in range(2)]
    Tm = [P.sb("Tm%d" % i, [128, T], F32) for i in range(2)]
    Sl = [P.sb("Sl%d" % i, [128, T], F32) for i in range(2)]
    g_ps = P.ps("g_ps", [128, T])
    u_ps = P.ps("u_ps", [128, T])
    h_ps = P.ps("h_ps", [128, 512])
    y_ps = [P.ps("y_ps%d" % i, [128, 512]) for i in range(3)]

    wupv = wup_d.rearrange("(kt p) n -> p kt n", p=128)
    wdnv = wdn_d.rearrange("(ft p) n -> p ft n", p=128)

    def load_up(g):
        b = g % NBUF
        c0 = g * 128 * GW
        P.dma("pool", Wg[b][:], wupv[:, :, c0:c0 + 128 * GW], w=["Wg%d" % b])
        P.dma("pool", Wu[b][:], wupv[:, :, DFF + c0:DFF + c0 + 128 * GW], w=["Wu%d" % b])

    def load_dn(g):
        b = g % NBUF
        P.dma("pool", Wd[b][:], wdnv[:, g * GW:(g + 1) * GW, :], w=["Wd%d" % b])

    def up_group(g):
        b = g % NBUF
        mb = g % 2
        for i in range(GW):
            ft = g * GW + i
            s = ft % 2
            hbufs = ["H%d" % kt for kt in range(KT)]
            fns = []
            for kt in range(KT):
                for c in range(2):
                    fns.append(lambda pe, st, sp_, kt=kt, c=c, b=b, i=i: pe.matmul(
                        g_ps[:, c * 512:(c + 1) * 512], lhsT=Wg[b][:, kt, i * 128:(i + 1) * 128],
                        rhs=H[:, kt, c * 512:(c + 1) * 512], start=(kt == 0), stop=(kt == KT - 1),
                        skip_group_check=True))
                fns.append(lambda pe, st, sp_, kt=kt, b=b, i=i: pe.matmul(
                    h_ps[:, 0:2], lhsT=Wg[b][:, kt, i * 128:(i + 1) * 128], rhs=HH[:, kt, :],
                    start=(kt == 0), stop=(kt == KT - 1), skip_group_check=True))
            P.mm_group(fns, r=hbufs + ["HH", "Wg%d" % b], w=["g_ps0", "g_ps1", "h_ps"])
            for c in range(2):
                cs = slice(c * 512, (c + 1) * 512)
                P.mm_group([lambda pe, st, sp_, kt=kt, cs=cs, b=b, i=i: pe.matmul(
                    u_ps[:, cs], lhsT=Wu[b][:, kt, i * 128:(i + 1) * 128], rhs=H[:, kt, cs], start=st, stop=sp_)
                    for kt in range(KT)], r=hbufs + ["Wu%d" % b], w=["u_ps%d" % c])
            P.op("act", lambda e, s=s: e.activation(out=Gs[s][:, 0:2], in_=h_ps[:, 0:2], func=AF.Copy),
                 r=["h_ps"], w=["Gs%d_h" % s])
            for c in range(2):
                P.op("act", lambda e, s=s, c=c: e.activation(
                    out=Gs[s][:, 2 + c * 512:2 + (c + 1) * 512], in_=g_ps[:, c * 512:(c + 1) * 512], func=AF.Copy),
                    r=["g_ps%d" % c], w=["Gs%d_%d" % (s, c)])
            gsb = ["Gs%d_h" % s, "Gs%d_0" % s, "Gs%d_1" % s]
            P.op("dve", lambda e, s=s, ft=ft: e.tensor_scalar(
                out=Tm[s][:], in0=Gs[s][:, 2:T + 2], scalar1=CW[:, 2, ft:ft + 1], scalar2=None, op0=ALU.mult),
                r=gsb + ["CW"], w=["Tm%d" % s])
            P.op("dve", lambda e, s=s, ft=ft: e.scalar_tensor_tensor(
                out=Tm[s][:], in0=Gs[s][:, 1:T + 1], scalar=CW[:, 1, ft:ft + 1], in1=Tm[s][:],
                op0=ALU.mult, op1=ALU.add), r=gsb + ["CW", "Tm%d" % s], w=["Tm%d" % s])
            P.op("dve", lambda e, s=s, ft=ft: e.scalar_tensor_tensor(
                out=Tm[s][:], in0=Gs[s][:, 0:T], scalar=CW[:, 0, ft:ft + 1], in1=Tm[s][:],
                op0=ALU.mult, op1=ALU.add), r=gsb + ["CW", "Tm%d" % s], w=["Tm%d" % s])
            P.op("act", lambda e, s=s: e.activation(out=Sl[s][:], in_=Tm[s][:], func=AF.Silu),
                 r=["Tm%d" % s], w=["Sl%d" % s])
            for c in range(2):
                cs = slice(c * 512, (c + 1) * 512)
                P.op("dve", lambda e, s=s, cs=cs, mb=mb, i=i: e.tensor_tensor(
                    out=M[mb][:, i, cs], in0=u_ps[:, cs], in1=Sl[s][:, cs], op=ALU.mult),
                    r=["u_ps%d" % c, "Sl%d" % s], w=["M%d_%d_%d" % (mb, i, c)])

    ycount = [0]

    def down_group(g):
        b = g % NBUF
        mb = g % 2
        for nt in range(KT):
            for c in range(2):
                cs = slice(c * 512, (c + 1) * 512)
                yb = ycount[0] % 3
                ycount[0] += 1
                P.mm_group([lambda pe, st, sp_, i=i, cs=cs, b=b, mb=mb, nt=nt, yb=yb: pe.matmul(
                    y_ps[yb][:], lhsT=Wd[b][:, i, nt * 128:(nt + 1) * 128], rhs=M[mb][:, i, cs], start=st, stop=sp_)
                    for i in range(GW)],
                    r=["M%d_%d_%d" % (mb, i, c) for i in range(GW)] + ["Wd%d" % b], w=["y_ps%d" % yb])
                P.op("dve", lambda e, nt=nt, cs=cs, yb=yb: e.tensor_tensor(
                    out=X[:, nt, cs], in0=y_ps[yb][:], in1=X[:, nt, cs], op=ALU.add),
                    r=["y_ps%d" % yb, "X%d" % nt], w=["X%d" % nt])

    load_up(0)
    load_dn(0)
    for g in range(NG):
        if g + 1 < NG:
            load_up(g + 1)
        up_group(g)
        if g >= 1:
            down_group(g - 1)
        if g + 1 < NG:
            load_dn(g + 1)
    down_group(NG - 1)

    gp = lambda c: "g_ps%d" % c
    xb = lambda kt: "X%d" % kt
    hov = ho_d.rearrange("(kt p) t -> p kt t", p=128)
    if final:
        emit_rmsnorm(P, X, xb, gn_d, X, xb, ones_f, "nn", ss_ps=g_ps, ssbuf=gp)
        for kt in range(KT):
            P.dma("sp", hov[:, kt, :], X[:, kt, :], r=["X%d" % kt], w=["ho"])
    else:
        xov = xo_d.rearrange("(kt p) t -> p kt t", p=128)
        for kt in range(KT):
            P.dma("sp", xov[:, kt, :], X[:, kt, :], r=["X%d" % kt], w=["xo"])
        emit_rmsnorm(P, X, xb, gn_d, H, lambda kt: "H%d" % kt, ones_f, "nn", ss_ps=g_ps, ssbuf=gp)
        for kt in range(KT):
            P.dma("sp", hov[:, kt, :], H[:, kt, :], r=["H%d" % kt], w=["ho"])
    P.finish()
    return nc


HD = 128
SCALE = HD ** -0.5
TWO_PI = float(2.0 * np.pi)
CW1 = 6.28125
CW2 = float(2.0 * np.pi - 6.28125)


def rope_consts():
    half = 16
    inv = np.power(np.float32(500000.0), -np.arange(half, dtype=np.float32) * np.float32(2.0 / 32)).astype(np.float32)
    c = np.zeros((32, 3), np.float32)
    c[:, 0] = np.concatenate([inv, inv])
    c[:16, 1] = -1.0
    c[16:, 1] = 1.0
    pm = np.zeros((32, 32), np.float32)
    for e2 in range(32):
        pm[(e2 + 16) % 32, e2] = 1.0
    return c, pm


def emit_rope_tables(P, pos_d, rc_d, tag="rp"):
    nc = P.nc
    posi = P.sb("posi", [32, T], I32)
    P.dma("sp", posi[:], pos_d.partition_broadcast(32) if hasattr(pos_d, "partition_broadcast") else
          bass.AP(pos_d.tensor, pos_d.offset, [[0, 32], [1, T]]), w=["posi"])
    RC = P.sb("RC", [32, 3], F32)
    P.dma("sp", RC[:], rc_d, w=["RC"])
    posf = P.sb("posf", [32, T], F32)
    P.op("dve", lambda e: e.tensor_copy(out=posf[:], in_=posi[:]), r=["posi"], w=["posf"])
    ang = P.sb("ang", [32, T], F32)
    P.op("dve", lambda e: e.tensor_scalar(out=ang[:], in0=posf[:], scalar1=RC[:, 0:1], scalar2=None, op0=ALU.mult),
         r=["posf", "RC"], w=["ang"])
    tabs = {}
    ni = P.sb("rp_ni", [32, T], I32)
    nf = P.sb("rp_nf", [32, T], F32)
    rr = P.sb("rp_r", [32, T], F32)
    mm = P.sb("rp_m", [32, T], F32)
    for name in ("sin", "cos"):
        if name == "sin":
            P.op("dve", lambda e: e.tensor_scalar(out=nf[:], in0=ang[:], scalar1=1.0 / TWO_PI, scalar2=None,
                                                  op0=ALU.mult), r=["ang"], w=["rp_nf"])
            P.op("dve", lambda e: e.tensor_copy(out=ni[:], in_=nf[:]), r=["rp_nf"], w=["rp_ni"])
            P.op("dve", lambda e: e.tensor_copy(out=nf[:], in_=ni[:]), r=["rp_ni"], w=["rp_nf"])
            P.op("dve", lambda e: e.scalar_tensor_tensor(out=rr[:], in0=nf[:], scalar=-CW1, in1=ang[:],
                                                         op0=ALU.mult, op1=ALU.add), r=["rp_nf", "ang"], w=["rp_r"])
            P.op("dve", lambda e: e.scalar_tensor_tensor(out=rr[:], in0=nf[:], scalar=-CW2, in1=rr[:],
                                                         op0=ALU.mult, op1=ALU.add), r=["rp_nf", "rp_r"], w=["rp_r"])
        else:
            P.op("dve", lambda e: e.tensor_scalar(out=rr[:], in0=rr[:], scalar1=float(np.pi / 2), scalar2=None,
                                                  op0=ALU.add), r=["rp_r"], w=["rp_r"])
        P.op("dve", lambda e: e.tensor_scalar(out=mm[:], in0=rr[:], scalar1=float(np.pi), scalar2=-TWO_PI,
                                              op0=ALU.is_gt, op1=ALU.mult), r=["rp_r"], w=["rp_m"])
        P.op("dve", lambda e: e.tensor_tensor(out=rr[:], in0=rr[:], in1=mm[:], op=ALU.add),
             r=["rp_r", "rp_m"], w=["rp_r"])
        P.op("dve", lambda e: e.tensor_scalar(out=mm[:], in0=rr[:], scalar1=float(-np.pi), scalar2=TWO_PI,
                                              op0=ALU.is_lt, op1=ALU.mult), r=["rp_r"], w=["rp_m"])
        P.op("dve", lambda e: e.tensor_tensor(out=rr[:], in0=rr[:], in1=mm[:], op=ALU.add),
             r=["rp_r", "rp_m"], w=["rp_r"])
        P.op("dve", lambda e: e.tensor_scalar(out=rr[:], in0=rr[:], scalar1=3.1415925, scalar2=-3.1415925,
                                              op0=ALU.min, op1=ALU.max), r=["rp_r"], w=["rp_r"])
        tk = P.sb("tab_" + name + "k", [32, T], F32)
        tq = P.sb("tab_" + name + "q", [32, T], F32)
        P.op("act", lambda e, tk=tk: e.activation(out=tk[:], in_=rr[:], func=AF.Sin), r=["rp_r"], w=["tab_" + name + "k"])
        if name == "sin":
            P.op("dve", lambda e, tk=tk: e.tensor_scalar(out=tk[:], in0=tk[:], scalar1=RC[:, 1:2], scalar2=None,
                                                         op0=ALU.mult), r=["tab_sink", "RC"], w=["tab_sink"])
        P.op("dve", lambda e, tk=tk, tq=tq: e.tensor_scalar(out=tq[:], in0=tk[:], scalar1=SCALE, scalar2=None,
                                                            op0=ALU.mult), r=["tab_" + name + "k"], w=["tab_" + name + "q"])
        tabs[name + "k"] = tk
        tabs[name + "q"] = tq
    return tabs


def emit_outproj_norm(P, OT, xT_d, wout_d, gain_d, xo_d, h2_d, y_ps, ybuf, ones_f, X=None, halo=None):
    if X is None:
        X = P.sb("X", [128, KT, T], F32)
    xv = xT_d.rearrange("(kt p) t -> p kt t", p=128)
    for kt in range(KT):
        P.dma("sp", X[:, kt, :], xv[:, kt, :], w=["X%d" % kt])
    wov = wout_d.rearrange("(kt p) n -> p kt n", p=128)
    Wo = [P.sb("Wo%d" % i, [128, KT, 256], BF16) for i in range(2)]
    otb = ["OT%d" % kt for kt in range(KT)]
    for g in range(D // 256):
        b = g % 2
        P.dma("pool", Wo[b][:], wov[:, :, g * 256:(g + 1) * 256], w=["Wo%d" % b])
        for i in range(2):
            nt = g * 2 + i
            for c in range(2):
                cs = slice(c * 512, (c + 1) * 512)
                P.mm_group([lambda pe, s_, e_, kt=kt, cs=cs, b=b, i=i: pe.matmul(
                    y_ps[:, cs], lhsT=Wo[b][:, kt, i * 128:(i + 1) * 128], rhs=OT[:, kt, cs], start=s_, stop=e_)
                    for kt in range(KT)], r=otb + ["Wo%d" % b], w=[ybuf(c)])
                P.op("dve", lambda e, nt=nt, cs=cs: e.tensor_tensor(
                    out=X[:, nt, cs], in0=y_ps[:, cs], in1=X[:, nt, cs], op=ALU.add),
                    r=[ybuf(c), "X%d" % nt], w=["X%d" % nt])
    xov = xo_d.rearrange("(kt p) t -> p kt t", p=128)
    for kt in range(KT):
        P.dma("sp", xov[:, kt, :], X[:, kt, :], r=["X%d" % kt], w=["xo"])
    emit_rmsnorm(P, X, lambda kt: "X%d" % kt, gain_d, OT, lambda kt: "OT%d" % kt, ones_f, "n2",
                 ss_ps=y_ps, ssbuf=ybuf)
    hov = h2_d.rearrange("(kt p) t -> p kt t", p=128)
    for kt in range(KT):
        P.dma("sp", hov[:, kt, :], OT[:, kt, :], r=["OT%d" % kt], w=["ho"])
    if halo is not None:
        hl_d, hlg_d = halo
        P.dma("sp", hl_d.rearrange("p (kt t) -> p kt t", t=2), OT[:, :, T - 2:T],
              r=["OT%d" % kt for kt in range(KT)], w=["hl"])
        P.coll(hl_d, hlg_d, r=["hl"], w=["hlg"])


def emit_attention(P, nheads, qT_d, ksrc, vsrc, OT, ones_b, kbias_d, tri_d, fox=None, mask=None, per_head=None,
                   pre_head=None):
    nc = P.nc
    T2 = 2 * T
    NKT = T2 // 128
    tri = P.sb("tri", [128, 128], BF16)
    P.dma("sp", tri[:], tri_d, w=["tri"])
    if fox is not None:
        ident = P.sb("ident", [128, 128], BF16)
        P.dma("sp", ident[:], fox["ident"], w=["ident"])
    KB = P.sb("KB", [128, NKT], F32)
    P.dma("sp", KB[:], kbias_d, w=["KB"])
    kTh = [P.sb("kTh%d" % i, [128, T2], BF16) for i in range(2)]
    Vh = [P.sb("Vh%d" % i, [128, NKT, 128], BF16) for i in range(2)]
    qh = [P.sb("qh%d" % i, [128, T], BF16) for i in range(2)]
    if fox is not None:
        qaug = [P.sb("qaug%d" % i, [6, T], BF16) for i in range(2)]
        kaug = [P.sb("kaug%d" % i, [6, T2], BF16) for i in range(2)]
        for i in range(2):
            P.op("dve", lambda e, i=i: e.memset(qaug[i][:], 1.0), w=["qaug%d" % i])
            P.op("dve", lambda e, i=i: e.memset(kaug[i][:], 1.0), w=["kaug%d" % i])
    NST, NPT = 3, 4
    PT = [P.sb("PT%d" % i, [128, 512], BF16) for i in range(NPT)]
    rec = [P.sb("rec%d" % i, [128, 512], F32) for i in range(2)]
    st_ps = [P.ps("st_ps%d" % i, [128, 512]) for i in range(NST)]
    o_ps = [P.ps("o_ps%d" % i, [128, 512]) for i in range(2)]
    d_ps = [P.ps("d_ps%d" % i, [128, 512]) for i in range(1)]
    cnt = {"s": 0, "p": 0, "o": 0}
    for h in range(nheads):
        hs = h % 2
        for (csl, src) in ksrc(h):
            P.dma("sp", kTh[hs][:, csl], src, w=["kTh%d" % hs])
        for (ksl, src) in vsrc(h):
            P.dma("sp", Vh[hs][:, ksl, :], src, w=["Vh%d" % hs])
        P.dma("sp", qh[hs][:], qT_d[h * 128:(h + 1) * 128, :], w=["qh%d" % hs])
        if pre_head is not None:
            pre_head(h)
        rd = ["kTh%d" % hs, "qh%d" % hs]
        if fox is not None:
            FS = fox["FS"]
            P.dma("sp", qaug[hs][0:3, :], FS[0:3, h, T:T2], r=["FS"], w=["qaug%d" % hs])
            P.dma("sp", kaug[hs][3:6, :], FS[3:6, h, :], r=["FS"], w=["kaug%d" % hs])
            rd = rd + ["qaug%d" % hs, "kaug%d" % hs, "ident", "tri"]
        for qc in range(2):
            oi = cnt["o"] % 2
            cnt["o"] += 1
            nk = 8 + 4 * qc + 4
            q0 = qc * 512

            def s_mm(kt, hs=hs, qc=qc, q0=q0, rd=rd):
                si = cnt["s"] % NST
                cnt["s"] += 1
                c_lo = max(0, kt - 8 - 4 * qc) * 128
                mms = [lambda pe, s_, e_, kt=kt, c_lo=c_lo, si=si, hs=hs, q0=q0: pe.matmul(
                    st_ps[si][:, c_lo:512], lhsT=kTh[hs][:, kt * 128:(kt + 1) * 128],
                    rhs=qh[hs][:, q0 + c_lo:q0 + 512], start=s_, stop=e_)]
                if fox is not None:
                    mms.append(lambda pe, s_, e_, kt=kt, c_lo=c_lo, si=si, hs=hs, q0=q0: pe.matmul(
                        st_ps[si][:, c_lo:512], lhsT=kaug[hs][:, kt * 128:(kt + 1) * 128],
                        rhs=qaug[hs][:, q0 + c_lo:q0 + 512], start=s_, stop=e_))
                    if kt - 8 - 4 * qc >= 0:
                        mms.append(lambda pe, s_, e_, c_lo=c_lo, si=si: pe.matmul(
                            st_ps[si][:, c_lo:c_lo + 128], lhsT=ident[:], rhs=tri[:], start=s_, stop=e_))
                P.mm_group(mms, r=rd, w=["st_ps%d" % si])
                return si, c_lo

            def rest(kt, si, c_lo, hs=hs, qc=qc, oi=oi, nk=nk):
                pi = cnt["p"] % NPT
                cnt["p"] += 1
                P.op("act", lambda e, si=si, pi=pi, c_lo=c_lo, kt=kt: e.activation(
                    out=PT[pi][:, c_lo:512], in_=st_ps[si][:, c_lo:512], func=AF.Exp, bias=KB[:, kt:kt + 1]),
                    r=["st_ps%d" % si, "KB"], w=["PT%d" % pi])
                j0 = 8 + 4 * qc - kt
                if mask is not None:
                    jlo = j0 + c_lo // 128
                    ncol = 512 - c_lo
                    P.op("dve", lambda e, pi=pi, c_lo=c_lo, jlo=jlo, ncol=ncol: e.tensor_tensor(
                        out=PT[pi][:, c_lo:512], in0=PT[pi][:, c_lo:512],
                        in1=mask[:, jlo * 128:jlo * 128 + ncol], op=ALU.mult),
                        r=["PT%d" % pi, "mask"], w=["PT%d" % pi])
                first, last = (kt == 0), (kt == nk - 1)
                P.mm_group([lambda pe, s_, e_, kt=kt, pi=pi, c_lo=c_lo, oi=oi, hs=hs, first=first, last=last: pe.matmul(
                    o_ps[oi][:, c_lo:512], lhsT=Vh[hs][:, kt, :], rhs=PT[pi][:, c_lo:512],
                    start=first, stop=last, skip_group_check=True)],
                    r=["Vh%d" % hs, "PT%d" % pi], w=["o_ps%d" % oi])
                P.mm_group([lambda pe, s_, e_, kt=kt, pi=pi, c_lo=c_lo, first=first, last=last: pe.matmul(
                    d_ps[0][:, c_lo:512], lhsT=ones_b[:], rhs=PT[pi][:, c_lo:512],
                    start=first, stop=last, skip_group_check=True)],
                    r=["ones_b", "PT%d" % pi], w=["d_ps0"])

            pend = [s_mm(0), s_mm(1)]
            for kt in range(nk):
                if kt + 2 < nk:
                    pend.append(s_mm(kt + 2))
                rest(kt, *pend.pop(0))
            ri = oi
            P.op("dve", lambda e, ri=ri: e.reciprocal(out=rec[ri][:], in_=d_ps[0][:]),
                 r=["d_ps0"], w=["rec%d" % ri])
            P.op("dve", lambda e, ri=ri, oi=oi, h=h, q0=q0: e.tensor_tensor(
                out=OT[:, h, q0:q0 + 512], in0=o_ps[oi][:], in1=rec[ri][:], op=ALU.mult),
                r=["o_ps%d" % oi, "rec%d" % ri], w=["OT%d" % h])
        if per_head is not None:
            per_head(h)
    return dict(st_ps=st_ps, o_ps=o_ps, d_ps=d_ps)


def kv_sources(kT_d, V_d, Kg, Vg, vrows):
    Vv = V_d.rearrange("(kt p) (hh e) -> p kt hh e", p=128, e=128)
    nvt = vrows // 128

    def ksrc(h):
        j, i = (h * 128) // 1024, (h * 128) % 1024
        return [(slice(0, T), Kg[j][i:i + 128, :]), (slice(T, 2 * T), kT_d[h * 128:(h + 1) * 128, :])]

    def vsrc(h):
        out = []
        for j, g in enumerate(Vg):
            gv = g[0:vrows, :].rearrange("(kt p) (hh e) -> p kt hh e", p=128, e=128)
            out.append((slice(j * nvt, (j + 1) * nvt), gv[:, :, h, :]))
        out.append((slice(8, 16), Vv[:, :, h, :]))
        return out
    return ksrc, vsrc


def mask_strip():
    k = np.arange(128)[:, None]
    cols = np.arange(16 * 128)[None, :]
    dl = cols - k
    m = ((dl >= 0) & (dl <= 128)).astype(np.float32)
    m += ((dl >= 0) & (dl % 4 == 0) & (dl <= 512))
    m += ((dl >= 0) & (dl % 16 == 0) & (dl <= 2048))
    return m


def build_fused(upto=None):
    nc = bass.Bass("TRN2", target_bir_lowering=False)
    itn = lambda name, shape, dt: nc.dram_tensor(name, list(shape), dt).ap()
    specs = dict(
        xT=([D, T], F32), pos=([1, T], I32), kbias=([128, 16], F32), hmul=([128, 1], F32),
        rope_c=([32, 3], F32), rope_pm=([32, 32], F32), negtri=([128, 128], BF16), tri=([128, 128], BF16),
        ident=([128, 128], BF16), mask=([128, 2048], BF16),
        norm_mix=([4, 128, KT], F32), norm_ffn=([4, 128, KT], F32), norm_final=([128, KT], F32))
    for e_ in range(2):
        specs.update({"ev_w_in_%d" % e_: ([D, 5120], F32), "ev_conv_w_%d" % e_: ([128, 31, 8], F32),
                      "ev_conv_b_%d" % e_: ([128, 8], F32), "ev_ln_g_%d" % e_: ([128, 8], F32),
                      "ev_ln_b_%d" % e_: ([128, 8], F32), "ev_w_out_%d" % e_: ([D, D], F32),
                      "od_w_in_%d" % e_: ([D, 6160], F32), "od_b_f_%d" % e_: ([16], F32),
                      "od_w_out_%d" % e_: ([D, D], F32)})
    for l_ in range(4):
        specs.update({"ffn_w_up_%d" % l_: ([D, 2 * DFF], F32), "ffn_conv_w_%d" % l_: ([128, 3, NFT], F32),
                      "ffn_w_down_%d" % l_: ([DFF, D], F32)})

    class _Lazy(dict):
        def __missing__(self, name):
            shape, dt = specs[name]
            ap = nc.dram_tensor(name, list(shape), dt, kind="ExternalInput").ap()
            self[name] = ap
            return ap
    E = _Lazy()
    nc._ext_names = E
    out_d = nc.dram_tensor("out", [D, T], F32, kind="ExternalOutput").ap()
    XM, XN = itn("XM", [D, T], F32), itn("XN", [D, T], F32)
    HT, H2 = itn("HT", [D, T], BF16), itn("H2", [D, T], BF16)
    HL, HLG = itn("HL", [128, 2 * KT], BF16), itn("HLG", [256, 2 * KT], BF16)
    QTo, KTo, Vo = itn("QTo", [2048, T], BF16), itn("KTo", [2048, T], BF16), itn("Vo", [T, 2048], BF16)
    QTe, KTe, Ve = itn("QTe", [1024, T], BF16), itn("KTe", [1024, T], BF16), itn("Ve", [T, 1024], BF16)
    KG = [itn("KG%d" % j, [2048, T], BF16) for j in range(2)]
    VGo = [itn("VGo%d" % j, [1024, 2048], BF16) for j in range(2)]
    VGe = itn("VGe", [2048, 1024], BF16)
    LF, LFG = itn("LF", [16, T], F32), itn("LFG", [32, T], F32)
    GLU, GHL, GHLG = itn("GLU", [1024, T], F32), itn("GHL", [1024, 32], F32), itn("GHLG", [2048, 32], F32)
    FS = itn("FS", [6, 16, 2 * T], BF16)

    def dump(src):
        P = Prog(nc)
        P.dma("sp", out_d[0:src.shape[0], :], src, w=["dump"])
        P.finish()
        return nc

    f_p0(nc, dict(xT=E["xT"], gain=E["norm_mix"][0], hT=HT))
    xcur = E["xT"]
    for l in range(4):
        e = l // 2
        if upto == "p1_%d" % l:
            f_p1(nc, dict(hT=HT, w_in=E["ev_w_in_%d" % e], qT=QTe, kT=KTe, V=Ve, pos=E["pos"], rope_c=E["rope_c"],
                          rope_pm=E["rope_pm"], gluT=GLU, glu_hl=GHL, glu_hlg=GHLG,
                          k_xchg=[(KTe, KG[0])], v_xchg=[(Ve, VGe)]), True) if l % 2 == 0 else \
                f_p1(nc, dict(hT=HT, w_in=E["od_w_in_%d" % e], qT=QTo, kT=KTo, V=Vo,
                              b_f=E["od_b_f_%d" % e].rearrange("(h o) -> h o", o=1), lf=LF, lfg=LFG,
                              k_xchg=[(KTo[0:1024, :], KG[0]), (KTo[1024:2048, :], KG[1])],
                              v_xchg=[(Vo[0:512, :], VGo[0]), (Vo[512:1024, :], VGo[1])]), False)
            return dump(GLU if l % 2 == 0 else XM)
        if l % 2 == 0:
            f_p1(nc, dict(hT=HT, w_in=E["ev_w_in_%d" % e], qT=QTe, kT=KTe, V=Ve, pos=E["pos"], rope_c=E["rope_c"],
                          rope_pm=E["rope_pm"], gluT=GLU, glu_hl=GHL, glu_hlg=GHLG,
                          k_xchg=[(KTe, KG[0])], v_xchg=[(Ve, VGe)]), True)
            f_p2_even(nc, dict(xT=xcur, qT=QTe, kT=KTe, V=Ve, Kg=[KG[0]], Vg=[VGe], gluT=GLU, glu_hlg=GHLG,
                               conv_w=E["ev_conv_w_%d" % e], conv_b=E["ev_conv_b_%d" % e], ln_g=E["ev_ln_g_%d" % e],
                               ln_b=E["ev_ln_b_%d" % e], kbias=E["kbias"], hmul=E["hmul"], mask=E["mask"], tri=E["tri"],
                               ident=E["ident"],
                               w_out=E["ev_w_out_%d" % e], gain=E["norm_ffn"][l], xT_out=XM, h2T=H2, hl=HL, hlg=HLG))
        else:
            f_p1(nc, dict(hT=HT, w_in=E["od_w_in_%d" % e], qT=QTo, kT=KTo, V=Vo,
                          b_f=E["od_b_f_%d" % e].rearrange("(h o) -> h o", o=1), lf=LF, lfg=LFG,
                          k_xchg=[(KTo[0:1024, :], KG[0]), (KTo[1024:2048, :], KG[1])],
                          v_xchg=[(Vo[0:512, :], VGo[0]), (Vo[512:1024, :], VGo[1])]), False)
            f_p2_odd(nc, dict(xT=xcur, qT=QTo, kT=KTo, V=Vo, Kg=KG, Vg=VGo, lf=LF, lfg=LFG, FS=FS,
                              kbias=E["kbias"], negtri=E["negtri"], ident=E["ident"],
                              w_out=E["od_w_out_%d" % e], gain=E["norm_ffn"][l], xT_out=XM, h2T=H2, hl=HL, hlg=HLG))
        if upto == "p2_%d" % l:
            return dump(XM)
        final = (l == 3)
        f_p3(nc, dict(xT=XM, h2T=H2, halo=HLG[0:128, :], hmul=E["hmul"], w_up=E["ffn_w_up_%d" % l],
                      conv_w=E["ffn_conv_w_%d" % l], w_down=E["ffn_w_down_%d" % l],
                      gain_next=(E["norm_final"] if final else E["norm_mix"][l + 1]),
                      xT_out=XN, hT_out=(out_d if final else HT)), final)
        xcur = XN
        if upto == "p3_%d" % l:
            return dump(XN)
    return nc


_NC_CACHE = {}


def kernel(x, positions, norm_mix, norm_ffn, norm_final, ev_w_in, ev_conv_w, ev_conv_b, ev_ln_g, ev_ln_b,
           ev_w_out, od_w_in, od_b_f, od_w_out, ffn_w_up, ffn_conv_w, ffn_w_down):
    import ml_dtypes
    bf = ml_dtypes.bfloat16
    f32 = np.float32
    x = np.asarray(x, f32)
    positions = np.asarray(positions, np.int32)
    A = lambda a: np.ascontiguousarray(np.asarray(a, f32))
    if "fused" not in _NC_CACHE:
        _NC_CACHE["fused"] = build_fused()
    nc = _NC_CACHE["fused"]
    rc, pm = rope_consts()
    shared = dict(
        rope_c=rc, rope_pm=pm,
        negtri=np.where(np.arange(128)[None, :] >= np.arange(128)[:, None], 0.0, -30000.0).astype(bf),
        tri=(np.arange(128)[None, :] >= np.arange(128)[:, None]).astype(bf),
        ident=np.eye(128).astype(bf), mask=mask_strip().astype(bf),
        norm_mix=A(np.asarray(norm_mix, f32).reshape(4, KT, 128).transpose(0, 2, 1)),
        norm_ffn=A(np.asarray(norm_ffn, f32).reshape(4, KT, 128).transpose(0, 2, 1)),
        norm_final=A(np.asarray(norm_final, f32).reshape(KT, 128).T))
    pvec = lambda v: A(np.asarray(v, f32).reshape(8, 128).T)
    for e_ in range(2):
        for nm, arr in (("ev_w_in", ev_w_in), ("ev_conv_w", ev_conv_w), ("ev_conv_b", ev_conv_b), ("ev_ln_g", ev_ln_g),
                        ("ev_ln_b", ev_ln_b), ("ev_w_out", ev_w_out), ("od_w_in", od_w_in), ("od_b_f", od_b_f),
                        ("od_w_out", od_w_out)):
            a_ = np.asarray(arr)[e_]
            if nm == "ev_conv_w":
                a_ = np.asarray(a_, f32).reshape(31, 8, 128).transpose(2, 0, 1)
            elif nm in ("ev_conv_b", "ev_ln_g", "ev_ln_b"):
                a_ = pvec(a_)
            shared["%s_%d" % (nm, e_)] = A(a_)
    for l_ in range(4):
        for nm, arr in (("ffn_w_up", ffn_w_up), ("ffn_conv_w", ffn_conv_w), ("ffn_w_down", ffn_w_down)):
            a_ = np.asarray(arr)[l_]
            if nm == "ffn_conv_w":
                a_ = np.asarray(a_, f32).reshape(3, NFT, 128).transpose(2, 0, 1)
            shared["%s_%d" % (nm, l_)] = A(a_)
    kb_a = np.concatenate([np.full((128, 8), -30000.0, f32), np.zeros((128, 8), f32)], 1)
    kb_b = np.zeros((128, 16), f32)
    in_maps = []
    for c in range(NCORES):
        b, h = c // 2, c % 2
        m = dict(shared)
        m["xT"] = np.ascontiguousarray(x[b, h * T:(h + 1) * T, :].T)
        m["pos"] = np.ascontiguousarray(positions[b:b + 1, h * T:(h + 1) * T])
        m["kbias"] = kb_b if h == 1 else kb_a
        m["hmul"] = np.full((128, 1), float(h), f32)
        in_maps.append(m)
    used = set(nc._ext_names.keys())
    in_maps = [{k: v for k, v in m.items() if k in used} for m in in_maps]
    res = run_bass_kernel_spmd(nc, in_maps, core_ids=list(range(NCORES)))
    out = np.empty((4, 2 * T, D), f32)
    for c in range(NCORES):
        b, h = c // 2, c % 2
        out[b, h * T:(h + 1) * T, :] = np.asarray(res.results[c]["out"]).T
    return out


def f_p0(nc, io):
    P = Prog(nc)
    ones_f, ones_b = emit_consts(P)
    X = P.sb("X", [128, KT, T], F32)
    H = P.sb("H", [128, KT, T], BF16)
    xv = io["xT"].rearrange("(kt p) t -> p kt t", p=128)
    for kt in range(KT):
        P.dma("sp", X[:, kt, :], xv[:, kt, :], w=["X%d" % kt])
    emit_rmsnorm(P, X, lambda kt: "X%d" % kt, io["gain"], H, lambda kt: "H%d" % kt, ones_f, "n0")
    hv = io["hT"].rearrange("(kt p) t -> p kt t", p=128)
    for kt in range(KT):
        P.dma("sp", hv[:, kt, :], H[:, kt, :], r=["H%d" % kt], w=["hT_out"])
    P.finish()


def f_p1(nc, io, even):
    NH = 8 if even else 16
    DA = NH * HD
    hT_d, win_d = io["hT"], io["w_in"]
    qT_d, kT_d, V_d = io["qT"], io["kT"], io["V"]
    P = Prog(nc)
    H = P.sb("H", [128, KT, T], BF16)
    hv = hT_d.rearrange("(kt p) t -> p kt t", p=128)
    for kt in range(KT):
        P.dma("sp", H[:, kt, :], hv[:, kt, :], w=["H%d" % kt])
    hbufs = ["H%d" % kt for kt in range(KT)]
    winv = win_d.rearrange("(kt p) n -> p kt n", p=128)
    NWB = 4 if even else 6
    W = [P.sb("W%d" % i, [128, KT, 512], BF16) for i in range(NWB)]
    ps = [P.ps("ps%d" % i, [128, T]) for i in range(3)]
    pcount = [0]
    st = [P.sb("st%d" % i, [128, T], BF16) for i in range(3)]
    scount = [0]
    if even:
        tabs = emit_rope_tables(P, io["pos"], io["rope_c"])
        Pm = P.sb("Pm", [32, 32], F32)
        P.dma("sp", Pm[:], io["rope_pm"], w=["Pm"])
        qs32 = [P.sb("qs32_%d" % i, [32, T], F32) for i in range(2)]
        t1 = [P.sb("rt1_%d" % i, [32, T], F32) for i in range(2)]
        t2 = [P.sb("rt2_%d" % i, [32, T], F32) for i in range(2)]
        sw_ps = P.ps("sw_ps", [32, T])
        rcount = [0]

    groups = []
    for g in range(DA // 512):
        groups.append(("k", DA + g * 512, 512, g))
    for g in range(DA // 512):
        groups.append(("v", 2 * DA + g * 512, 512, g))
    groups.append(("xchg", 0, 0, 0))
    for g in range(DA // 512):
        groups.append(("q", g * 512, 512, g))
    if even:
        for g in range(2):
            groups.append(("glu_a", 3 * DA + g * 512, 512, g))
            groups.append(("glu_g", 3 * DA + 1024 + g * 512, 512, g))
    else:
        groups.append(("f", 3 * DA, 16, 0))
    wgroups = [g for g in groups if g[0] != "xchg"]
    slot_of = {}

    def load(i):
        if i >= len(wgroups):
            return
        kind, c0, ncol, _ = wgroups[i]
        b = i % NWB
        slot_of[i] = b
        P.dma("pool", W[b][:, :, 0:ncol], winv[:, :, c0:c0 + ncol], w=["W%d" % b])

    def feat_tile(b, i):
        pi = pcount[0] % 3
        pcount[0] += 1
        for c in range(2):
            cs = slice(c * 512, (c + 1) * 512)
            P.mm_group([lambda pe, s_, e_, kt=kt, cs=cs, b=b, i=i, pi=pi: pe.matmul(
                ps[pi][:, cs], lhsT=W[b][:, kt, i * 128:(i + 1) * 128], rhs=H[:, kt, cs], start=s_, stop=e_)
                for kt in range(KT)], r=hbufs + ["W%d" % b], w=["ps%d_%d" % (pi, c)])
        return pi

    def psb(pi):
        return ["ps%d_0" % pi, "ps%d_1" % pi]

    def qk_group(which, b, g):
        out_d = qT_d if which == "q" else kT_d
        for i in range(4):
            h = g * 4 + i
            pi = feat_tile(b, i)
            si = scount[0] % 3
            scount[0] += 1
            sc = SCALE if which == "q" else 1.0
            if even:
                ri = rcount[0] % 2
                rcount[0] += 1
                P.op("dve", lambda e, pi=pi, ri=ri: e.tensor_copy(out=qs32[ri][:], in_=ps[pi][0:32, :]),
                     r=psb(pi), w=["qs32_%d" % ri])
                for c in range(2):
                    cs = slice(c * 512, (c + 1) * 512)
                    P.mm_group([lambda pe, s_, e_, cs=cs, ri=ri: pe.matmul(
                        sw_ps[:, cs], lhsT=Pm[:], rhs=qs32[ri][:, cs], start=True, stop=True)],
                        r=["qs32_%d" % ri, "Pm"], w=["sw_ps%d" % c])
                ct = tabs["cos" + which]
                sn = tabs["sin" + which]
                P.op("dve", lambda e, ri=ri, ct=ct: e.tensor_tensor(out=t1[ri][:], in0=qs32[ri][:], in1=ct[:], op=ALU.mult),
                     r=["qs32_%d" % ri, "tab_cos" + which], w=["rt1_%d" % ri])
                P.op("dve", lambda e, ri=ri, sn=sn: e.tensor_tensor(out=t2[ri][:], in0=sw_ps[:], in1=sn[:], op=ALU.mult),
                     r=["sw_ps0", "sw_ps1", "tab_sin" + which], w=["rt2_%d" % ri])
                P.op("act", lambda e, pi=pi, si=si, sc=sc: e.activation(out=st[si][:], in_=ps[pi][:], func=AF.Copy, scale=sc),
                     r=psb(pi), w=["st%d_lo" % si, "st%d_hi" % si])
                P.op("dve", lambda e, ri=ri, si=si: e.tensor_tensor(out=st[si][0:32, :], in0=t1[ri][:], in1=t2[ri][:], op=ALU.add),
                     r=["rt1_%d" % ri, "rt2_%d" % ri], w=["st%d_lo" % si])
            else:
                P.op("act", lambda e, pi=pi, si=si, sc=sc: e.activation(out=st[si][:], in_=ps[pi][:], func=AF.Copy, scale=sc),
                     r=psb(pi), w=["st%d_lo" % si, "st%d_hi" % si])
            P.dma("sp", out_d[h * 128:(h + 1) * 128, :], st[si][:], r=["st%d_lo" % si, "st%d_hi" % si],
                  w=[which + "T_out"])

    vst = [P.sb("vst%d" % i, [128, 512], BF16) for i in range(3)]
    vcount = [0]

    def v_group(b, g):
        for tt in range(T // 128):
            pi = pcount[0] % 3
            pcount[0] += 1
            P.mm_group([lambda pe, s_, e_, kt=kt, b=b, tt=tt, pi=pi: pe.matmul(
                ps[pi][:, 0:512], lhsT=H[:, kt, tt * 128:(tt + 1) * 128], rhs=W[b][:, kt, :], start=s_, stop=e_)
                for kt in range(KT)], r=hbufs + ["W%d" % b], w=["ps%d_0" % pi])
            vi = vcount[0] % 3
            vcount[0] += 1
            P.op("dve", lambda e, pi=pi, vi=vi: e.tensor_copy(out=vst[vi][:], in_=ps[pi][:, 0:512]),
                 r=["ps%d_0" % pi], w=["vst%d" % vi])
            P.dma("sp", V_d[tt * 128:(tt + 1) * 128, g * 512:(g + 1) * 512], vst[vi][:], r=["vst%d" % vi], w=["V_out"])

    if even:
        sg = [P.sb("sg%d" % i, [128, T], F32) for i in range(2)]
        gl = [P.sb("gl%d" % i, [128, T], F32) for i in range(2)]

    def glu_group(ba, bg, g):
        for i in range(4):
            ct_ = g * 4 + i
            pa = feat_tile(ba, i)
            pg = feat_tile(bg, i)
            s_ = ct_ % 2
            P.op("act", lambda e, pg=pg, s_=s_: e.activation(out=sg[s_][:], in_=ps[pg][:], func=AF.Sigmoid),
                 r=psb(pg), w=["sg%d" % s_])
            P.op("dve", lambda e, pa=pa, s_=s_: e.tensor_tensor(out=gl[s_][:], in0=ps[pa][:], in1=sg[s_][:], op=ALU.mult),
                 r=psb(pa) + ["sg%d" % s_], w=["gl%d" % s_])
            P.dma("sp", io["gluT"][ct_ * 128:(ct_ + 1) * 128, :], gl[s_][:], r=["gl%d" % s_], w=["glu_out"])
            P.dma("sp", io["glu_hl"][ct_ * 128:(ct_ + 1) * 128, :], gl[s_][:, T - 32:T], r=["gl%d" % s_], w=["gluh_out"])

    def f_group(b):
        BFt = P.sb("BFt", [16, 1], F32)
        P.dma("sp", BFt[:], io["b_f"], w=["BFt"])
        NB = P.sb("NB", [16, 1], F32)
        P.op("dve", lambda e: e.tensor_scalar(out=NB[:], in0=BFt[:], scalar1=-1.0, scalar2=None, op0=ALU.mult),
             r=["BFt"], w=["NB"])
        pi = pcount[0] % 3
        pcount[0] += 1
        for c in range(2):
            cs = slice(c * 512, (c + 1) * 512)
            P.mm_group([lambda pe, s_, e_, kt=kt, cs=cs, b=b, pi=pi: pe.matmul(
                ps[pi][0:16, cs], lhsT=W[b][:, kt, 0:16], rhs=H[:, kt, cs], start=s_, stop=e_)
                for kt in range(KT)], r=hbufs + ["W%d" % b], w=["ps%d_%d" % (pi, c)])
        e1 = P.sb("e1", [16, T], F32)
        l1 = P.sb("l1", [16, T], F32)
        P.op("act", lambda e, pi=pi: e.activation(out=e1[:], in_=ps[pi][0:16, :], func=AF.Exp, scale=-1.0, bias=NB[:]),
             r=psb(pi) + ["NB"], w=["e1"])
        P.op("act", lambda e: e.activation(out=l1[:], in_=e1[:], func=AF.Ln, bias=1.0), r=["e1"], w=["l1"])
        P.op("dve", lambda e: e.tensor_scalar(out=l1[:], in0=l1[:], scalar1=-1.0, scalar2=None, op0=ALU.mult),
             r=["l1"], w=["l1"])
        P.dma("sp", io["lf"], l1[:], r=["l1"], w=["lf_out"])

    def xchg():
        for j, (src, dst) in enumerate(io["k_xchg"]):
            P.coll(src, dst, r=["kT_out"], w=["Kg%d" % j])
        for j, (src, dst) in enumerate(io["v_xchg"]):
            P.coll(src, dst, r=["V_out"], w=["Vg%d" % j])

    PF = NWB - 1
    for i in range(PF):
        load(i)
    wi = 0
    pend_a = None
    xdone = False
    for grp in groups:
        kind = grp[0]
        if kind == "xchg":
            xchg()
            continue
        load(wi + PF)
        b = slot_of[wi]
        if kind in ("q", "k"):
            qk_group(kind, b, grp[3])
        elif kind == "v":
            v_group(b, grp[3])
        elif kind == "glu_a":
            pend_a = b
        elif kind == "glu_g":
            glu_group(pend_a, b, grp[3])
        elif kind == "f":
            f_group(b)
        wi += 1
    if even:
        P.coll(io["glu_hl"], io["glu_hlg"], r=["gluh_out"], w=["gluhg"])
    else:
        P.coll(io["lf"], io["lfg"], r=["lf_out"], w=["lfg"])
    P.finish()


def f_p2_odd(nc, io):
    P = Prog(nc)
    ones_f, ones_b = emit_consts(P)
    FS = io["FS"]
    A = P.sb("fA", [16, T], F32)
    B = P.sb("fB", [16, T], F32)
    C = P.sb("fC", [16, T], F32)
    carry = P.sb("fcarry", [16, 1], F32)
    P.op("dve", lambda e: e.memset(carry[:], 0.0), w=["fcarry"])
    parts = [P.sb("fp%d" % i, [16, T], BF16) for i in range(6)]
    for half in range(2):
        hsl = slice(half * T, (half + 1) * T)
        src = io["lfg"][0:16, :] if half == 0 else io["lf"]
        P.dma("sp", A[:], src, w=["fA"])
        P.op("dve", lambda e: e.memset(B[:], 1.0), w=["fB"])
        P.op("dve", lambda e: e.tensor_tensor_scan(out=C[:], data0=B[:], data1=A[:], initial=carry[:],
                                                   op0=ALU.mult, op1=ALU.add),
             r=["fA", "fB", "fcarry"], w=["fC"])
        P.op("dve", lambda e: e.tensor_copy(out=carry[:], in_=C[:, T - 1:T]), r=["fC"], w=["fcarry"])
        P.op("dve", lambda e: e.tensor_copy(out=parts[0][:], in_=C[:]), r=["fC"], w=["fp0"])
        P.op("dve", lambda e: e.tensor_copy(out=A[:], in_=parts[0][:]), r=["fp0"], w=["fA"])
        P.op("dve", lambda e: e.tensor_tensor(out=B[:], in0=C[:], in1=A[:], op=ALU.subtract), r=["fC", "fA"], w=["fB"])
        P.op("dve", lambda e: e.tensor_copy(out=parts[1][:], in_=B[:]), r=["fB"], w=["fp1"])
        P.op("dve", lambda e: e.tensor_copy(out=A[:], in_=parts[1][:]), r=["fp1"], w=["fA"])
        P.op("dve", lambda e: e.tensor_tensor(out=C[:], in0=B[:], in1=A[:], op=ALU.subtract), r=["fB", "fA"], w=["fC"])
        P.op("dve", lambda e: e.tensor_copy(out=parts[2][:], in_=C[:]), r=["fC"], w=["fp2"])
        for i in range(3):
            P.op("dve", lambda e, i=i: e.tensor_scalar(out=parts[3 + i][:], in0=parts[i][:], scalar1=-1.0,
                                                       scalar2=None, op0=ALU.mult), r=["fp%d" % i], w=["fp%d" % (3 + i)])
        for i in range(6):
            P.dma("sp", FS[i, :, hsl], parts[i][:], r=["fp%d" % i], w=["FS"])
    OT = P.sb("OT", [128, KT, T], BF16)
    y_ps = P.ps("y_ps", [128, T])
    ksrc, vsrc = kv_sources(io["kT"], io["V"], io["Kg"], io["Vg"], 512)
    emit_attention(P, 16, io["qT"], ksrc, vsrc, OT, ones_b, io["kbias"], io["negtri"],
                   fox=dict(FS=FS, ident=io["ident"]))
    emit_outproj_norm(P, OT, io["xT"], io["w_out"], io["gain"], io["xT_out"], io["h2T"], y_ps,
                      lambda c: "y_ps%d" % c, ones_f, halo=(io["hl"], io["hlg"]))
    P.finish()


def f_p2_even(nc, io):
    P = Prog(nc)
    ones_f, ones_b = emit_consts(P)
    X = P.sb("X", [128, KT, T], F32)
    OT = P.sb("OT", [128, KT, T], BF16)
    y_ps = P.ps("y_ps", [128, T])
    mask = P.sb("mask", [128, 2048], BF16)
    P.dma("sp", mask[:], io["mask"], w=["mask"])
    HM = P.sb("HM", [128, 1], F32)
    P.dma("sp", HM[:], io["hmul"], w=["HM"])
    CWt = P.sb("CWt", [128, 31, 8], F32)
    P.dma("sp", CWt[:], io["conv_w"], w=["CWt"])
    CP = P.sb("CP", [128, 3, 8], F32)
    for j, dd in enumerate((io["conv_b"], io["ln_g"], io["ln_b"])):
        P.dma("sp", CP[:, j, :], dd, w=["CP"])
    Gt = [P.sb("Gt%d" % i, [128, T + 30], BF16) for i in range(2)]
    Dg = [P.sb("Dg%d" % i, [128, 31, 128], BF16) for i in range(2)]
    identc = P.sb("identc", [128, 128], BF16)
    P.dma("sp", identc[:], io["ident"], w=["identc"])
    glu_d, ghg = io["gluT"], io["glu_hlg"]

    def conv_prep(ct):
        gi = ct % 2
        P.dma("pool", Gt[gi][:, 0:30], ghg[ct * 128:(ct + 1) * 128, 2:32], w=["Gt%d_h" % gi])
        P.dma("pool", Gt[gi][:, 30:T + 30], glu_d[ct * 128:(ct + 1) * 128, :], w=["Gt%d" % gi])
        P.op("dve", lambda e, gi=gi: e.tensor_scalar(out=Gt[gi][:, 0:30], in0=Gt[gi][:, 0:30], scalar1=HM[:, 0:1],
                                                     scalar2=None, op0=ALU.mult), r=["Gt%d_h" % gi, "HM"], w=["Gt%d_h" % gi])
        for k in range(31):
            P.op("dve", lambda e, gi=gi, ct=ct, k=k: e.tensor_scalar(
                out=Dg[gi][:, k, :], in0=identc[:], scalar1=CWt[:, k, ct:ct + 1], scalar2=None, op0=ALU.mult),
                r=["identc", "CWt"], w=["Dg%d" % gi])

    def conv_tile(ct):
        gi = ct % 2
        gb = ["Gt%d_h" % gi, "Gt%d" % gi, "Dg%d" % gi]
        ab = "X%d" % (8 + ct)
        for c in range(2):
            P.mm_group([lambda pe, s_, e_, gi=gi, k=k, c=c: pe.matmul(
                y_ps[:, c * 512:(c + 1) * 512], lhsT=Dg[gi][:, k, :], rhs=Gt[gi][:, k + c * 512:k + c * 512 + 512],
                start=s_, stop=e_) for k in range(31)], r=gb, w=["y_ps%d" % c])
            P.op("act", lambda e, ct=ct, c=c: e.activation(
                out=X[:, 8 + ct, c * 512:(c + 1) * 512], in_=y_ps[:, c * 512:(c + 1) * 512], func=AF.Identity,
                bias=CP[:, 0, ct:ct + 1]), r=["y_ps%d" % c, "CP"], w=[ab])

    ksrc, vsrc = kv_sources(io["kT"], io["V"], io["Kg"], io["Vg"], 1024)
    ps = emit_attention(P, 8, io["qT"], ksrc, vsrc, OT, ones_b, io["kbias"], io["tri"], mask=mask, per_head=conv_tile, pre_head=conv_prep)
    st_ps = ps["st_ps"]
    sqb = [P.sb("lsq%d" % i, [128, 512], F32) for i in range(2)]
    i = 0
    for c in range(2):
        cs = slice(c * 512, (c + 1) * 512)
        for ct in range(8):
            s_ = i % 2
            i += 1
            P.op("act", lambda e, ct=ct, s_=s_, cs=cs: e.activation(out=sqb[s_][:], in_=X[:, 8 + ct, cs], func=AF.Square),
                 r=["X%d" % (8 + ct)], w=["lsq%d" % s_])
            P.mm_group([lambda pe, a_, b_, ct=ct, cs=cs: pe.matmul(
                y_ps[:, cs], lhsT=ones_f[:], rhs=X[:, 8 + ct, cs], start=(ct == 0), stop=(ct == 7),
                skip_group_check=True)], r=["X%d" % (8 + ct), "ones_f"], w=["y_ps%d" % c])
            P.mm_group([lambda pe, a_, b_, ct=ct, s_=s_, c=c: pe.matmul(
                st_ps[c][:], lhsT=ones_f[:], rhs=sqb[s_][:], start=(ct == 0), stop=(ct == 7),
                skip_group_check=True)], r=["lsq%d" % s_, "ones_f"], w=["st_ps%d" % c])
    mean = P.sb("ln_mean", [128, T], F32)
    var = P.sb("ln_var", [128, T], F32)
    for c in range(2):
        cs = slice(c * 512, (c + 1) * 512)
        P.op("dve", lambda e, cs=cs: e.tensor_scalar(out=mean[:, cs], in0=y_ps[:, cs], scalar1=1.0 / 1024, scalar2=None,
                                                     op0=ALU.mult), r=["y_ps%d" % c], w=["ln_mean%d" % c])
        P.op("dve", lambda e, cs=cs: e.tensor_tensor(out=var[:, cs], in0=mean[:, cs], in1=mean[:, cs], op=ALU.mult),
             r=["ln_mean%d" % c], w=["ln_var%d" % c])
        P.op("dve", lambda e, cs=cs, c=c: e.scalar_tensor_tensor(
            out=var[:, cs], in0=st_ps[c][:], scalar=1.0 / 1024, in1=var[:, cs], op0=ALU.mult, op1=ALU.subtract),
            r=["st_ps%d" % c, "ln_var%d" % c], w=["ln_var%d" % c])
        P.op("act", lambda e, cs=cs: e.activation(out=var[:, cs], in_=var[:, cs], func=AF.Sqrt, bias=EPS),
             r=["ln_var%d" % c], w=["ln_var%d" % c])
        P.op("dve", lambda e, cs=cs: e.reciprocal(out=var[:, cs], in_=var[:, cs]),
             r=["ln_var%d" % c], w=["ln_var%d" % c])
    lt = [P.sb("lt%d" % i, [128, T], F32) for i in range(2)]
    stat = ["ln_mean0", "ln_mean1", "ln_var0", "ln_var1"]
    for ct in range(8):
        li = ct % 2
        P.op("dve", lambda e, ct=ct, li=li: e.tensor_tensor(out=lt[li][:], in0=X[:, 8 + ct, :], in1=mean[:], op=ALU.subtract),
             r=["X%d" % (8 + ct)] + stat, w=["lt%d" % li])
        P.op("dve", lambda e, li=li: e.tensor_tensor(out=lt[li][:], in0=lt[li][:], in1=var[:], op=ALU.mult),
             r=["lt%d" % li] + stat, w=["lt%d" % li])
        P.op("act", lambda e, ct=ct, li=li: e.activation(out=OT[:, 8 + ct, :], in_=lt[li][:], func=AF.Silu,
                                                         scale=CP[:, 1, ct:ct + 1], bias=CP[:, 2, ct:ct + 1]),
             r=["lt%d" % li, "CP"], w=["OT%d" % (8 + ct)])
    emit_outproj_norm(P, OT, io["xT"], io["w_out"], io["gain"], io["xT_out"], io["h2T"], y_ps,
                      lambda c: "y_ps%d" % c, ones_f, X=X, halo=((io["hl"], io["hlg"]) if "hl" in io else None))
    P.finish()
```
